# Optimizing a Trainium2 kernel written in Bass

```python
import math
import jax, jax.numpy as jnp
from jax import lax
import numpy as np

D_MODEL = 1024
BATCH = 8
SEQ = 4096
DEPTH = 4

CHUNK = 64
Q_BLOCK = 128
EPS = 1e-6
ROPE_THETA = 10000.0

SB_HEAD_DIM = 64
SB_WIDTH = D_MODEL // 2
SB_HEADS = SB_WIDTH // SB_HEAD_DIM
DF_HEAD_DIM = 64
DF_WIDTH = D_MODEL - SB_WIDTH
DF_HEADS = DF_WIDTH // (2 * DF_HEAD_DIM)
MIX_WIDTH = SB_WIDTH + DF_WIDTH
PROJ_WIDTH = 3 * SB_WIDTH + 3 * DF_WIDTH
D_FF = 4 * D_MODEL

kernel_name = "hybrid_stickbreak_diffattn_sqrelu"


def rmsnorm(x, g):
    xf = x.astype(jnp.float32)
    y = xf * lax.rsqrt(jnp.mean(xf * xf, axis=-1, keepdims=True) + EPS) * g.astype(jnp.float32)
    return y.astype(x.dtype)


def rope(x, pos):
    d = x.shape[-1]
    inv = 1.0 / (ROPE_THETA ** (jnp.arange(0, d, 2, dtype=jnp.float32) / d))
    ang = pos.astype(jnp.float32)[:, None] * inv[None, :]
    ang = jnp.concatenate([ang, ang], axis=-1)
    shape = (1, x.shape[1]) + (1,) * (x.ndim - 3) + (d,)
    cos = jnp.cos(ang).reshape(shape)
    sin = jnp.sin(ang).reshape(shape)
    xf = x.astype(jnp.float32)
    x1, x2 = xf[..., : d // 2], xf[..., d // 2:]
    rot = jnp.concatenate([-x2, x1], axis=-1)
    return (xf * cos + rot * sin).astype(x.dtype)


def stick_breaking_attention(q, k, v):
    B, S, H, d = q.shape
    nb = S // Q_BLOCK
    scale = d ** -0.5
    kh = k.transpose(0, 2, 1, 3)
    vh = v.transpose(0, 2, 1, 3)
    qb = q.transpose(0, 2, 1, 3).reshape(B, H, nb, Q_BLOCK, d).transpose(2, 0, 1, 3, 4)
    kpos = jnp.arange(S)
    qpos = kpos.reshape(nb, Q_BLOCK)

    def block(args):
        qblk, qp = args
        z = jnp.einsum('bhqd,bhkd->bhqk', qblk, kh).astype(jnp.float32) * scale
        strict = kpos[None, :] < qp[:, None]
        log_beta = jax.nn.log_sigmoid(z)
        log_keep = jnp.where(strict, jax.nn.log_sigmoid(-z), 0.0)
        between = lax.cumsum(log_keep, axis=log_keep.ndim - 1, reverse=True) - log_keep
        w = jnp.where(strict, jnp.exp(log_beta + between), 0.0)
        return jnp.einsum('bhqk,bhkd->bhqd', w.astype(vh.dtype), vh)

    out = lax.map(block, (qb, qpos))
    return out.transpose(1, 0, 3, 2, 4).reshape(B, S, H * d)


def differential_attention(q, k, v, lam):
    B, S, H, _, d = q.shape
    dv = v.shape[-1]
    nb = S // Q_BLOCK
    scale = d ** -0.5
    kh = k.transpose(0, 2, 3, 1, 4)
    vh = v.transpose(0, 2, 1, 3)
    qb = q.transpose(0, 2, 3, 1, 4).reshape(B, H, 2, nb, Q_BLOCK, d).transpose(3, 0, 1, 2, 4, 5)
    kchunk = jnp.arange(S) // CHUNK
    qpos = jnp.arange(S).reshape(nb, Q_BLOCK)
    neg = jnp.finfo(jnp.float32).min

    def block(args):
        qblk, qp = args
        s = jnp.einsum('bhmqd,bhmkd->bhmqk', qblk, kh).astype(jnp.float32) * scale
        allowed = kchunk[None, :] <= (qp // CHUNK)[:, None]
        p = jax.nn.softmax(jnp.where(allowed, s, neg), axis=-1)
        attn = p[:, :, 0] - lam * p[:, :, 1]
        return jnp.einsum('bhqk,bhkd->bhqd', attn.astype(vh.dtype), vh)

    out = lax.map(block, (qb, qpos))
    return out.transpose(1, 0, 3, 2, 4).reshape(B, S, H, dv)


def setup_inputs(seed: int = 0) -> dict:
    key = jax.random.key(seed)
    ks = jax.random.split(key, 14)
    f32 = jnp.float32
    out_scale = (2.0 * DEPTH) ** -0.5
    x = jax.random.normal(ks[0], (BATCH, SEQ, D_MODEL), f32)
    w_in = jax.random.normal(ks[1], (DEPTH, D_MODEL, PROJ_WIDTH), f32) * D_MODEL ** -0.5
    w_o = jax.random.normal(ks[2], (DEPTH, MIX_WIDTH, D_MODEL), f32) * (MIX_WIDTH ** -0.5) * out_scale
    attn_norm = 1.0 + 0.02 * jax.random.normal(ks[3], (DEPTH, D_MODEL), f32)
    subln_norm = 1.0 + 0.02 * jax.random.normal(ks[4], (DEPTH, 2 * DF_HEAD_DIM), f32)
    lam_q1 = 0.1 * jax.random.normal(ks[5], (DEPTH, DF_HEAD_DIM), f32)
    lam_k1 = 0.1 * jax.random.normal(ks[6], (DEPTH, DF_HEAD_DIM), f32)
    lam_q2 = 0.1 * jax.random.normal(ks[7], (DEPTH, DF_HEAD_DIM), f32)
    lam_k2 = 0.1 * jax.random.normal(ks[8], (DEPTH, DF_HEAD_DIM), f32)
    mlp_norm = 1.0 + 0.02 * jax.random.normal(ks[9], (DEPTH, D_MODEL), f32)
    w_ff1 = jax.random.normal(ks[10], (DEPTH, D_MODEL, D_FF), f32) * D_MODEL ** -0.5
    w_ff2 = jax.random.normal(ks[11], (DEPTH, D_FF, D_MODEL), f32) * (D_FF ** -0.5) * out_scale
    final_norm = 1.0 + 0.02 * jax.random.normal(ks[12], (D_MODEL,), f32)
    return {"x": x, "w_in": w_in, "w_o": w_o, "attn_norm": attn_norm,
            "subln_norm": subln_norm, "lam_q1": lam_q1, "lam_k1": lam_k1,
            "lam_q2": lam_q2, "lam_k2": lam_k2, "mlp_norm": mlp_norm,
            "w_ff1": w_ff1, "w_ff2": w_ff2, "final_norm": final_norm}


def reference(x, w_in, w_o, attn_norm, subln_norm, lam_q1, lam_k1, lam_q2, lam_k2,
              mlp_norm, w_ff1, w_ff2, final_norm):
    B, S, _ = x.shape
    pos = jnp.arange(S)
    splits = np.cumsum([SB_WIDTH, SB_WIDTH, SB_WIDTH, DF_WIDTH, DF_WIDTH]).tolist()
    for l in range(DEPTH):
        h = rmsnorm(x, attn_norm[l])
        proj = jnp.einsum('bsd,de->bse', h, w_in[l])
        sb_q, sb_k, sb_v, df_q, df_k, df_v = jnp.split(proj, splits, axis=-1)

        sb_out = stick_breaking_attention(
            sb_q.reshape(B, S, SB_HEADS, SB_HEAD_DIM),
            sb_k.reshape(B, S, SB_HEADS, SB_HEAD_DIM),
            sb_v.reshape(B, S, SB_HEADS, SB_HEAD_DIM))

        lambda_init = 0.8 - 0.6 * math.exp(-0.3 * l)
        lam = (jnp.exp(jnp.sum(lam_q1[l].astype(jnp.float32) * lam_k1[l].astype(jnp.float32)))
               - jnp.exp(jnp.sum(lam_q2[l].astype(jnp.float32) * lam_k2[l].astype(jnp.float32)))
               + lambda_init)
        dq = rope(df_q.reshape(B, S, DF_HEADS, 2, DF_HEAD_DIM), pos)
        dk = rope(df_k.reshape(B, S, DF_HEADS, 2, DF_HEAD_DIM), pos)
        dvv = df_v.reshape(B, S, DF_HEADS, 2 * DF_HEAD_DIM)
        df_heads = differential_attention(dq, dk, dvv, lam)
        df_out = (rmsnorm(df_heads, subln_norm[l]) * (1.0 - lambda_init)).reshape(B, S, DF_WIDTH)

        mixed = jnp.concatenate([sb_out, df_out.astype(sb_out.dtype)], axis=-1)
        x = x + jnp.einsum('bse,ed->bsd', mixed, w_o[l])

        h = rmsnorm(x, mlp_norm[l])
        u = jnp.square(jax.nn.relu(jnp.einsum('bsd,df->bsf', h, w_ff1[l])))
        x = x + jnp.einsum('bsf,fd->bsd', u, w_ff2[l])
    return rmsnorm(x, final_norm)
```

```python
import math
from contextlib import ExitStack

import numpy as np
import concourse.bass as bass
import concourse.mybir as mybir
from concourse.bass_utils import run_bass_kernel_spmd

F32 = mybir.dt.float32
BF16 = mybir.dt.bfloat16
AF = mybir.ActivationFunctionType
ALU = mybir.AluOpType
AX = mybir.AxisListType

S = 4096
D = 1024
L = 4
NTT = 8
EPS = 1e-6
NEG = -30000.0
ENGS = ("pe", "act", "dve", "pool", "sp")
DMA_INC = 16


class Buf:
    __slots__ = ("name", "writers", "readers", "sem", "cnt", "gen_deps")

    def __init__(self, name=""):
        self.name = name
        self.writers = []
        self.readers = []
        self.sem = None
        self.cnt = 0
        self.gen_deps = []


class Op:
    __slots__ = ("eng", "fn", "deps", "sig", "dma", "count")

    def __init__(self, eng, fn):
        self.eng = eng
        self.fn = fn
        self.deps = []
        self.sig = False
        self.dma = None
        self.count = None


class Prog:
    def __init__(self, nc, n_dma_sems=160):
        self.nc = nc
        self.ops = {e: [] for e in ENGS}
        self.n_dma_sems = n_dma_sems
        self.dma_sem_next = 0
        self.dma_sem_max = 0
        self.dma_last = {}
        self.pending = {e: [] for e in ENGS}

    def op(self, eng, fn, reads=(), writes=()):
        o = Op(eng, fn)
        idx = len(self.ops[eng])
        deps = []
        for b in reads:
            deps.extend(b.writers)
        for b in writes:
            deps.extend(b.writers)
            deps.extend(b.readers)
        deps.extend(self.pending[eng])
        self.pending[eng] = []
        ev = ("e", eng, idx)
        o.deps = [d for d in deps if not (d[0] == "e" and d[1] == "pe" and eng == "pe")]
        self.ops[eng].append(o)
        for b in reads:
            b.readers.append(ev)
        for b in writes:
            b.gen_deps = b.writers + b.readers
            b.writers = [ev]
            b.readers = []
        return ev

    def dma(self, eng, fn, reads=(), writes=(), sem_buf=None, parallel=False):
        o = Op(eng, fn)
        if sem_buf is None:
            sem_buf = writes[0] if writes else reads[0]
        if sem_buf.sem is None:
            sem_buf.sem = self.dma_sem_next
            sem_buf.cnt = self.dma_last.get(sem_buf.sem, 0)
            self.dma_sem_next += 1
            self.dma_sem_max = max(self.dma_sem_max, self.dma_sem_next)
            assert self.dma_sem_next <= self.n_dma_sems, "out of dma sems"
        sem_buf.cnt += DMA_INC
        ev = ("d", sem_buf.sem, sem_buf.cnt)
        self.dma_last[sem_buf.sem] = sem_buf.cnt
        deps = []
        for b in reads:
            deps.extend(b.writers)
        for b in writes:
            if parallel and b.writers and all(w[0] == "d" for w in b.writers) and not b.readers:
                deps.extend(b.gen_deps)
            else:
                deps.extend(b.writers)
                deps.extend(b.readers)
        deps.extend(self.pending[eng])
        self.pending[eng] = []
        o.deps = deps
        o.dma = sem_buf.sem
        self.ops[eng].append(o)
        for b in reads:
            b.readers.append(ev)
        for b in writes:
            if parallel and b.writers and all(w[0] == "d" for w in b.writers) and not b.readers:
                b.writers = b.writers + [ev]
            else:
                b.gen_deps = b.writers + b.readers
                b.writers = [ev]
                b.readers = []
        return ev

    def barrier(self):
        evs = []
        for e in ENGS:
            for i in range(len(self.ops[e]) - 1, -1, -1):
                if self.ops[e][i].dma is None:
                    evs.append(("e", e, i))
                    break
        for s, v in self.dma_last.items():
            evs.append(("d", s, v))
        for e in ENGS:
            self.pending[e].extend(evs)
        self.dma_sem_next = 0

    def emit(self):
        nc = self.nc
        for e in ENGS:
            for o in self.ops[e]:
                for d in o.deps:
                    if d[0] == "e":
                        self.ops[d[1]][d[2]].sig = True
        for e in ENGS:
            c = 0
            for o in self.ops[e]:
                if o.dma is None and o.sig:
                    c += 1
                    o.count = c
        with ExitStack() as st:
            esem = {e: st.enter_context(nc.semaphore("s_" + e)) for e in ENGS}
            dsem = [st.enter_context(nc.semaphore("d%d" % i)) for i in range(self.dma_sem_max)]
            block = st.enter_context(nc.Block())
            handles = {"pe": block.tensor, "act": block.scalar, "dve": block.vector,
                       "pool": block.gpsimd, "sp": block.sync}

            def make(e):
                def body(eng):
                    waited = {}
                    for o in self.ops[e]:
                        need = {}
                        for d in o.deps:
                            if d[0] == "e":
                                key = ("e", d[1])
                                val = self.ops[d[1]][d[2]].count
                            else:
                                key = ("d", d[1])
                                val = d[2]
                            if val > need.get(key, 0):
                                need[key] = val
                        for key, val in need.items():
                            if waited.get(key, 0) >= val:
                                continue
                            waited[key] = val
                            sem = esem[key[1]] if key[0] == "e" else dsem[key[1]]
                            eng.wait_ge(sem, val)
                        ins = o.fn(eng)
                        if o.dma is not None:
                            ins.then_inc(dsem[o.dma], DMA_INC)
                        elif o.sig:
                            ins.then_inc(esem[e], 1)
                    if e == "sp":
                        for s, v in self.dma_last.items():
                            if waited.get(("d", s), 0) < v:
                                eng.wait_ge(dsem[s], v)
                return body

            for e in ENGS:
                handles[e](make(e))


class Tile:
    def __init__(self, t, N, off, n, name=""):
        self.t, self.N, self.off, self.n = t, N, off, n
        self.buf = Buf(name)

    def ap(self, o, dims, p0=0, npart=128):
        return bass.AP(self.t, p0 * self.N + self.off + o, [[self.N, npart]] + [list(d) for d in dims])

    def sl(self, a, b, p0=0, p1=128):
        return self.t[p0:p1, self.off + a:self.off + b]


class Arena:
    def __init__(self, t, N):
        self.t, self.N, self.off = t, N, 0

    def reset(self):
        self.off = 0

    def tile(self, n, name=""):
        o = self.off
        self.off += n
        assert self.off <= self.N, ("arena overflow", name, self.off, self.N)
        return Tile(self.t, self.N, o, n, name)


def lam_init(l):
    return 0.8 - 0.6 * math.exp(-0.3 * l)


def build(n_layers=L, debug=False):
    nc = bass.Bass("TRN2", target_bir_lowering=False)
    dt_ = nc.dram_tensor
    x_d = dt_("x", [S, D], F32, kind="ExternalInput")
    win_d = dt_("w_in", [L, D, 4096], F32, kind="ExternalInput")
    wo_d = dt_("w_o", [L, D, D], F32, kind="ExternalInput")
    w1_d = dt_("w_ff1", [L, D, 4096], F32, kind="ExternalInput")
    w2_d = dt_("w_ff2", [L, 4096, D], F32, kind="ExternalInput")
    NV = 76 + 1024
    vec_d = dt_("vecs", [128, NV], F32, kind="ExternalInput")
    cst_d = dt_("consts", [128, 640], F32, kind="ExternalInput")
    rope_d = dt_("rope", [2, 128, S], F32, kind="ExternalInput")
    out_d = dt_("out", [S, D], F32, kind="ExternalOutput")
    sk = "ExternalOutput" if debug else "Internal"
    wbin = dt_("wbin", [L * 8, 128, 4096], BF16)
    wbo = dt_("wbo", [L * 2, 128, 4096], BF16)
    wb1 = dt_("wb1", [L * 8, 128, 4096], BF16)
    wb2 = dt_("wb2", [L * 8, 128, 4096], BF16)
    xT_d = dt_("xT", [8, 128, S], F32, kind=sk)
    hT_d = dt_("hT", [8, 128, S], BF16, kind=sk)
    qTs_d = dt_("qTs", [4, 128, S], BF16, kind=sk)
    kTs_d = dt_("kTs", [4, 128, S], BF16, kind=sk)
    qTd_d = dt_("qTd", [4, 128, S], BF16, kind=sk)
    kTd_d = dt_("kTd", [4, 128, S], BF16, kind=sk)
    vs_d = dt_("vs", [32, 128, 512], BF16, kind=sk)
    vd_d = dt_("vd", [32, 128, 512], BF16, kind=sk)
    mixT_d = dt_("mixT", [8, 128, S], BF16, kind=sk)

    def dap(h, off, dims):
        return bass.AP(h, off, [list(d) for d in dims])

    NBF = 63488
    NF32 = 15360
    with ExitStack() as st:
        abf_t = st.enter_context(nc.sbuf_tensor("abf", [128, NBF], BF16))
        af_t = st.enter_context(nc.sbuf_tensor("af32", [128, NF32], F32))
        cb_t = st.enter_context(nc.sbuf_tensor("cb", [128, 640], BF16))
        cf_t = st.enter_context(nc.sbuf_tensor("cf", [128, 640], F32))
        vec_t = st.enter_context(nc.sbuf_tensor("vec", [128, NV], F32))
        sm_t = st.enter_context(nc.sbuf_tensor("small", [128, 512], F32))
        psb = [st.enter_context(nc.psum_tensor("ps%d" % i, [128, 512], F32)) for i in range(8)]
        PB = [Buf("ps%d" % i) for i in range(8)]
        P = Prog(nc)
        ABF = Arena(abf_t, NBF)
        AF32 = Arena(af_t, NF32)
        CB = Tile(cb_t, 640, 0, 640, "cb")
        CF = Tile(cf_t, 640, 0, 640, "cf")
        VEC = Tile(vec_t, NV, 0, NV, "vec")
        SM = Arena(sm_t, 512)
        sfold = SM.tile(4, "sfold")
        neglam = SM.tile(4, "neglam")
        lamt = SM.tile(8, "lamt")
        epst = SM.tile(1, "epst")

        IDF = CF.sl(0, 128)
        IDB = CB.sl(0, 128)
        NEGM = CB.sl(128, 256)
        LTRI = CB.sl(256, 384)
        NEG1 = CB.sl(384, 512)
        ONES = CB.sl(512, 640)

        def psl(i, a, b, p0=0, p1=128):
            return psb[i][p0:p1, a:b]

        EPSB = epst.sl(0, 1)
        P.op("pool", lambda e: e.memset(EPSB, EPS), writes=[epst.buf])

        def mm(pi, out, lhsT, rhs, start, stop, reads, skip=False):
            if skip:
                P.op("pe", lambda e: e.matmul(out=out, lhsT=lhsT, rhs=rhs, start=start, stop=stop,
                                              skip_group_check=True), reads=reads, writes=[PB[pi]])
            else:
                P.op("pe", lambda e: e.matmul(out=out, lhsT=lhsT, rhs=rhs, start=start, stop=stop),
                     reads=reads, writes=[PB[pi]])

        def tr(pi, out, in_, reads):
            P.op("pe", lambda e: e.transpose(out=out, in_=in_, identity=IDF), reads=list(reads) + [CF.buf],
                 writes=[PB[pi]])

        def act(out, in_, func, reads, writes, scale=1.0, bias=0.0, accum=None):
            if accum is None:
                P.op("act", lambda e: e.activation(out=out, in_=in_, func=func, bias=bias, scale=scale),
                     reads=reads, writes=writes)
            else:
                P.op("act", lambda e: e.activation(out=out, in_=in_, func=func, bias=bias, scale=scale,
                                                   accum_out=accum), reads=reads, writes=writes)

        def tt(eng, out, in0, in1, op, reads, writes):
            P.op(eng, lambda e: e.tensor_tensor(out=out, in0=in0, in1=in1, op=op), reads=reads, writes=writes)

        def ts(eng, out, in0, s1, s2, op0, op1, reads, writes):
            if s2 is None:
                P.op(eng, lambda e: e.tensor_scalar(out=out, in0=in0, scalar1=s1, scalar2=None, op0=op0),
                     reads=reads, writes=writes)
            else:
                P.op(eng, lambda e: e.tensor_scalar(out=out, in0=in0, scalar1=s1, scalar2=s2, op0=op0, op1=op1),
                     reads=reads, writes=writes)

        def stt(eng, out, in0, scalar, in1, op0, op1, reads, writes):
            P.op(eng, lambda e: e.scalar_tensor_tensor(out=out, in0=in0, scalar=scalar, in1=in1, op0=op0, op1=op1),
                 reads=reads, writes=writes)

        def cp(eng, out, in_, reads, writes):
            P.op(eng, lambda e: e.tensor_copy(out=out, in_=in_), reads=reads, writes=writes)

        def ms(eng, ap_, val, reads, writes):
            P.op(eng, lambda e: e.memset(ap_, val), reads=reads, writes=writes)

        def dma(out, in_, reads=(), writes=(), parallel=False):
            P.dma("sp", lambda e: e.dma_start(out=out, in_=in_), reads=list(reads), writes=list(writes),
                  parallel=parallel)

        MUL, ADD, SUB = ALU.mult, ALU.add, ALU.subtract

        dma(cf_t[:, :], cst_d.ap()[:, :], writes=[CF.buf])
        dma(vec_t[:, :], vec_d.ap()[:, :], writes=[VEC.buf])
        cp("dve", cb_t[:, :], cf_t[:, :], [CF.buf], [CB.buf])
        for l in range(L):
            ts("pool", sfold.sl(l, l + 1), VEC.sl(72 + l, 73 + l), 1.0 - lam_init(l), None, MUL, None,
               [VEC.buf], [sfold.buf])
        lprod = AF32.tile(512, "lprod")
        tt("dve", lprod.sl(0, 256), VEC.sl(76, 76 + 256), VEC.sl(76 + 256, 76 + 512), MUL, [VEC.buf], [lprod.buf])
        tt("dve", lprod.sl(256, 512), VEC.sl(76 + 512, 76 + 768), VEC.sl(76 + 768, 76 + 1024), MUL, [VEC.buf],
           [lprod.buf])
        lp_in = lprod.ap(0, [[64, 8], [1, 64]])
        lp_out = lamt.sl(0, 8)
        P.op("dve", lambda e: e.tensor_reduce(out=lp_out, in_=lp_in, axis=AX.X, op=ADD),
             reads=[lprod.buf], writes=[lamt.buf])
        act(lamt.sl(0, 8), lamt.sl(0, 8), AF.Exp, [lamt.buf], [lamt.buf])
        tt("dve", lamt.sl(0, 4), lamt.sl(0, 4), lamt.sl(4, 8), SUB, [lamt.buf], [lamt.buf])
        for l in range(L):
            ts("dve", neglam.sl(l, l + 1), lamt.sl(l, l + 1), lam_init(l), -1.0, ADD, MUL, [lamt.buf], [neglam.buf])

        P.barrier()
        ABF.reset(); AF32.reset()
        WS = [AF32.tile(4096, "ws%d" % i) for i in range(2)]
        WB = [ABF.tile(4096, "wb%d" % i) for i in range(2)]
        prep = []
        for l in range(n_layers):
            prep += [("in", l, g) for g in range(8)] + [("o", l, h) for h in range(2)]
            prep += [("f1", l, g) for g in range(8)] + [("f2", l, g) for g in range(8)]
        for i, (kind, l, g) in enumerate(prep):
            ws, wb = WS[i % 2], WB[i % 2]
            ce = "dve" if i % 2 == 0 else "pool"
            if kind in ("in", "f1"):
                src_h = win_d if kind == "in" else w1_d
                dma(ws.ap(0, [[512, 8], [1, 512]]),
                    dap(src_h, l * D * 4096 + g * 512, [[4096, 128], [128 * 4096, 8], [1, 512]]), writes=[ws.buf])
                gcol = (0 if kind == "in" else 32) + l * 8
                tt(ce, wb.ap(0, [[512, 8], [1, 512]]), ws.ap(0, [[512, 8], [1, 512]]),
                   VEC.ap(gcol, [[1, 8], [0, 512]]), MUL, [ws.buf, VEC.buf], [wb.buf])
                dst = dap(wbin if kind == "in" else wb1, (l * 8 + g) * 128 * 4096, [[4096, 128], [1, 4096]])
            elif kind == "f2":
                dma(ws.ap(0, [[128, 32], [1, 128]]),
                    dap(w2_d, l * 4096 * D + g * 128, [[D, 128], [128 * D, 32], [1, 128]]), writes=[ws.buf])
                cp(ce, wb.sl(0, 4096), ws.sl(0, 4096), [ws.buf], [wb.buf])
                dst = dap(wb2, (l * 8 + g) * 128 * 4096, [[4096, 128], [1, 4096]])
            else:
                dma(ws.ap(0, [[1024, 4], [1, 1024]]),
                    dap(wo_d, l * D * D + g * 512 * D, [[D, 128], [128 * D, 4], [1, 1024]]), writes=[ws.buf])
                if g == 0:
                    cp(ce, wb.sl(0, 4096), ws.sl(0, 4096), [ws.buf], [wb.buf])
                else:
                    ts(ce, wb.sl(0, 4096), ws.sl(0, 4096), sfold.sl(l, l + 1), None, MUL, None,
                       [ws.buf, sfold.buf], [wb.buf])
                dst = dap(wbo, (l * 2 + g) * 128 * 4096, [[4096, 128], [1, 4096]])
            dma(dst, wb.sl(0, 4096), reads=[wb.buf])

        def norm_stats(XT, SQ, RS, pi):
            act(SQ.sl(0, 4096), XT.sl(0, 4096), AF.Square, [XT.buf], [SQ.buf])
            for c in range(8):
                mm(pi, psl(pi, 0, 512), ONES, SQ.sl(c * 512, (c + 1) * 512), c == 0, c == 7, [SQ.buf, CB.buf])
            act(RS.sl(0, 512), psl(pi, 0, 512), AF.Ln, [PB[pi]], [RS.buf], scale=1.0 / D, bias=EPSB)
            act(RS.sl(0, 512), RS.sl(0, 512), AF.Exp, [RS.buf], [RS.buf], scale=-0.5)

        def norm_apply(XT, RS, HT, eng):
            tt(eng, HT.ap(0, [[512, 8], [1, 512]]), XT.ap(0, [[512, 8], [1, 512]]), RS.ap(0, [[0, 8], [1, 512]]),
               MUL, [XT.buf, RS.buf], [HT.buf])

        def store_tile8(T_, dram_h, tt_):
            dma(dap(dram_h, tt_ * 512, [[S, 128], [128 * S, 8], [1, 512]]), T_.ap(0, [[512, 8], [1, 512]]),
                reads=[T_.buf])

        def load_tile8(T_, dram_h, tt_):
            dma(T_.ap(0, [[512, 8], [1, 512]]), dap(dram_h, tt_ * 512, [[S, 128], [128 * S, 8], [1, 512]]),
                writes=[T_.buf])

        P.barrier()
        ABF.reset(); AF32.reset()
        XIN = [AF32.tile(1024, "xin%d" % i) for i in range(2)]
        XTt = [AF32.tile(4096, "xt%d" % i) for i in range(2)]
        RSt = [AF32.tile(512, "rs%d" % i) for i in range(2)]
        SQt = [ABF.tile(4096, "sq%d" % i) for i in range(2)]
        HTt = [ABF.tile(4096, "ht%d" % i) for i in range(2)]
        for tt_ in range(NTT):
            XT = XTt[tt_ % 2]
            for j in range(4):
                tb = tt_ * 4 + j
                xin = XIN[tb % 2]
                dma(xin.sl(0, 1024), x_d.ap()[tb * 128:(tb + 1) * 128, :], writes=[xin.buf])
                for hb in range(2):
                    pi = (tb * 2 + hb) % 4
                    for c4 in range(4):
                        c = hb * 4 + c4
                        tr(pi, psl(pi, c4 * 128, (c4 + 1) * 128), xin.sl(c * 128, (c + 1) * 128), [xin.buf])
                    cp("dve", XT.ap(hb * 4 * 512 + j * 128, [[512, 4], [1, 128]]),
                       bass.AP(psb[pi], 0, [[512, 128], [128, 4], [1, 128]]), [PB[pi]], [XT.buf])
            store_tile8(XT, xT_d, tt_)
            norm_stats(XT, SQt[tt_ % 2], RSt[tt_ % 2], 4 + tt_ % 2)
            norm_apply(XT, RSt[tt_ % 2], HTt[tt_ % 2], "pool")
            store_tile8(HTt[tt_ % 2], hT_d, tt_)

        for l in range(n_layers):
            P.barrier()
            ABF.reset(); AF32.reset()
            HTR = ABF.tile(32768, "htr")
            WG = [ABF.tile(4096, "wg%d" % i) for i in range(3)]
            STG = [ABF.tile(4096, "stg%d" % i) for i in range(2)]
            VSTG = [ABF.tile(2048, "vstg%d" % i) for i in range(2)]
            COS = AF32.tile(4096, "cos")
            SIN = AF32.tile(4096, "sin")
            T1 = [AF32.tile(512, "t1_%d" % i) for i in range(2)]
            T2 = [AF32.tile(512, "t2_%d" % i) for i in range(2)]
            for c in range(8):
                dma(HTR.sl(c * 4096, (c + 1) * 4096), dap(hT_d, c * 128 * S, [[S, 128], [1, S]]),
                    writes=[HTR.buf], parallel=(c > 0))
            dma(COS.sl(0, 4096), dap(rope_d, 0, [[S, 128], [1, S]]), writes=[COS.buf])
            dma(SIN.sl(0, 4096), dap(rope_d, 128 * S, [[S, 128], [1, S]]), writes=[SIN.buf])
            wg_i = [0]

            def load_wg(g):
                w = WG[wg_i[0] % 3]
                wg_i[0] += 1
                dma(w.sl(0, 4096), dap(wbin, (l * 8 + g) * 128 * 4096, [[4096, 128], [1, 4096]]), writes=[w.buf])
                return w

            psr = [0]
            stg_i = [0]
            for g, dst_h, sc in ((0, qTs_d, 0.125), (1, kTs_d, 1.0)):
                w = load_wg(g)
                for ci in range(4):
                    stg = STG[stg_i[0] % 2]
                    stg_i[0] += 1
                    for tt_ in range(NTT):
                        pi = psr[0] % 4
                        psr[0] += 1
                        for dmc in range(8):
                            mm(pi, psl(pi, 0, 512), w.sl(dmc * 512 + ci * 128, dmc * 512 + (ci + 1) * 128),
                               HTR.sl(dmc * 4096 + tt_ * 512, dmc * 4096 + (tt_ + 1) * 512), dmc == 0, dmc == 7,
                               [w.buf, HTR.buf])
                        act(stg.sl(tt_ * 512, (tt_ + 1) * 512), psl(pi, 0, 512), AF.Copy, [PB[pi]], [stg.buf],
                            scale=sc)
                    dma(dap(dst_h, ci * 128 * S, [[S, 128], [1, S]]), stg.sl(0, 4096), reads=[stg.buf])
            for g, dst_h, sc in ((2, qTd_d, 0.125), (4, kTd_d, 1.0)):
                wa = load_wg(g)
                wp = load_wg(g + 1)
                for ci in range(4):
                    stg = STG[stg_i[0] % 2]
                    stg_i[0] += 1
                    for tt_ in range(NTT):
                        pa = 4 + psr[0] % 2
                        pb_ = 6 + psr[0] % 2
                        t1 = T1[psr[0] % 2]
                        t2 = T2[psr[0] % 2]
                        psr[0] += 1
                        for pi, w in ((pa, wa), (pb_, wp)):
                            for dmc in range(8):
                                mm(pi, psl(pi, 0, 512), w.sl(dmc * 512 + ci * 128, dmc * 512 + (ci + 1) * 128),
                                   HTR.sl(dmc * 4096 + tt_ * 512, dmc * 4096 + (tt_ + 1) * 512), dmc == 0,
                                   dmc == 7, [w.buf, HTR.buf])
                        stt("dve", t1.sl(0, 512), psl(pa, 0, 512), sc, COS.sl(tt_ * 512, (tt_ + 1) * 512), MUL, MUL,
                            [PB[pa], COS.buf], [t1.buf])
                        stt("dve", t2.sl(0, 512), psl(pb_, 0, 512), sc, SIN.sl(tt_ * 512, (tt_ + 1) * 512), MUL, MUL,
                            [PB[pb_], SIN.buf], [t2.buf])
                        tt("pool", stg.sl(tt_ * 512, (tt_ + 1) * 512), t1.sl(0, 512), t2.sl(0, 512), ADD,
                           [t1.buf, t2.buf], [stg.buf])
                    dma(dap(dst_h, ci * 128 * S, [[S, 128], [1, S]]), stg.sl(0, 4096), reads=[stg.buf])
            for g, dst_h in ((6, vs_d), (7, vd_d)):
                w = load_wg(g)
                for tb in range(32):
                    vst = VSTG[(tb // 4) % 2]
                    pi = psr[0] % 4
                    psr[0] += 1
                    for dmc in range(8):
                        mm(pi, psl(pi, 0, 512), HTR.sl(dmc * 4096 + tb * 128, dmc * 4096 + (tb + 1) * 128),
                           w.sl(dmc * 512, (dmc + 1) * 512), dmc == 0, dmc == 7, [w.buf, HTR.buf])
                    act(vst.sl((tb % 4) * 512, (tb % 4 + 1) * 512), psl(pi, 0, 512), AF.Copy, [PB[pi]], [vst.buf])
                    if tb % 4 == 3:
                        dma(dap(dst_h, (tb - 3) * 128 * 512, [[512, 128], [128 * 512, 4], [1, 512]]),
                            vst.ap(0, [[512, 4], [1, 512]]), reads=[vst.buf])
            if debug == "A":
                break

            P.barrier()
            ABF.reset(); AF32.reset()
            QT = [ABF.tile(4096, "qt%d" % i) for i in range(2)]
            KT = [ABF.tile(4096, "kt%d" % i) for i in range(2)]
            VP = [ABF.tile(32 * 132, "vp%d" % i) for i in range(2)]
            SPt = [ABF.tile(512, "sp%d" % i) for i in range(3)]
            Wt = [ABF.tile(512, "w%d" % i) for i in range(3)]
            ACC = [ABF.tile(512, "acc%d" % i) for i in range(2)]
            OST = [ABF.tile(4096, "ost%d" % i) for i in range(2)]
            Et = [AF32.tile(512, "e%d" % i) for i in range(3)]
            DJ = [AF32.tile(512, "dj%d" % i) for i in range(2)]
            DEN = [AF32.tile(16, "den%d" % i) for i in range(2)]
            SSQ = [AF32.tile(8, "ssq%d" % i) for i in range(2)]
            JUNK = AF32.tile(128, "junk")
            for i in range(2):
                ms("pool", VP[i].ap(128, [[132, 32], [1, 1]]), 1.0, [], [VP[i].buf])

            def load_qkv(qh, kh, vh, c, slot):
                dma(QT[slot].sl(0, 4096), dap(qh, c * 128 * S, [[S, 128], [1, S]]), writes=[QT[slot].buf])
                dma(KT[slot].sl(0, 4096), dap(kh, c * 128 * S, [[S, 128], [1, S]]), writes=[KT[slot].buf])
                dma(VP[slot].ap(0, [[132, 32], [1, 128]]),
                    dap(vh, c * 128, [[512, 128], [128 * 512, 32], [1, 128]]), writes=[VP[slot].buf])

            blocks = []
            for pc in range(4):
                for hh in range(2):
                    for tq in range(NTT):
                        for i, kb in enumerate(range(4 * tq + 3, -1, -1)):
                            blocks.append((pc, pc % 2, hh, tq, i, kb))
            nb = len(blocks)
            load_qkv(qTs_d, kTs_d, vs_d, 0, 0)
            ost_i = [0]
            cur_ost = {}

            def sb_stage1(n):
                pc, slot, hh, tq, i, kb = blocks[n]
                if hh == 0 and tq == 0 and i == 0 and pc + 1 < 4:
                    load_qkv(qTs_d, kTs_d, vs_d, pc + 1, (pc + 1) % 2)
                pb = 64 * hh
                c0 = 128 * max(0, kb - 4 * tq)
                diag = kb >= 4 * tq
                q0 = tq * 512
                zi = n % 2
                acc = ACC[tq % 2]
                if i == 0:
                    ms("pool", acc.sl(0, 512), 0.0, [], [acc.buf])
                mm(zi, psl(zi, c0, 512), KT[slot].sl(kb * 128, (kb + 1) * 128, pb, pb + 64),
                   QT[slot].sl(q0 + c0, q0 + 512, pb, pb + 64), True, not diag, [KT[slot].buf, QT[slot].buf])
                if diag:
                    mm(zi, psl(zi, c0, c0 + 128), IDB, NEGM, False, True, [CB.buf])
                E = Et[n % 3]
                SPn = SPt[n % 3]
                act(E.sl(c0, 512), psl(zi, c0, 512), AF.Exp, [PB[zi]], [E.buf])
                act(SPn.sl(c0, 512), E.sl(c0, 512), AF.Ln, [E.buf], [SPn.buf], bias=1.0)

            def sb_stage2(n):
                pc, slot, hh, tq, i, kb = blocks[n]
                pb = 64 * hh
                c0 = 128 * max(0, kb - 4 * tq)
                diag = kb >= 4 * tq
                q0 = tq * 512
                ai = 2 + n % 2
                acc = ACC[tq % 2]
                SPn = SPt[n % 3]
                Wn = Wt[n % 3]
                mm(ai, psl(ai, c0, 512), KT[slot].sl(kb * 128, (kb + 1) * 128, pb, pb + 64),
                   QT[slot].sl(q0 + c0, q0 + 512, pb, pb + 64), True, False, [KT[slot].buf, QT[slot].buf])
                if diag:
                    mm(ai, psl(ai, c0, c0 + 128), IDB, NEGM, False, False, [CB.buf])
                mm(ai, psl(ai, c0, 512), LTRI, SPn.sl(c0, 512), False, i == 0, [CB.buf, SPn.buf])
                if i > 0:
                    mm(ai, psl(ai, c0, 512), NEG1, acc.sl(c0, 512), False, True, [CB.buf, acc.buf])
                act(Wn.sl(c0, 512), psl(ai, c0, 512), AF.Exp, [PB[ai]], [Wn.buf])
                if kb > 0:
                    tt("pool", acc.sl(c0, 512), acc.sl(c0, 512), SPn.sl(c0, 512), ADD, [SPn.buf, acc.buf], [acc.buf])

            def sb_stage3(n):
                pc, slot, hh, tq, i, kb = blocks[n]
                pb = 64 * hh
                c0 = 128 * max(0, kb - 4 * tq)
                oi = 4 + tq % 2
                Wn = Wt[n % 3]
                mm(oi, psl(oi, c0, 512, 0, 64), VP[slot].ap(kb * 132 + pb, [[1, 64]]), Wn.sl(c0, 512),
                   i == 0, kb == 0, [VP[slot].buf, Wn.buf], skip=True)
                if kb == 0:
                    if tq == 0:
                        cur_ost[(pc, hh)] = OST[ost_i[0] % 2]
                        ost_i[0] += 1
                    ost = cur_ost[(pc, hh)]
                    cp("dve", ost.sl(tq * 512, (tq + 1) * 512, 0, 64), psl(oi, 0, 512, 0, 64), [PB[oi]], [ost.buf])
                    if tq == NTT - 1:
                        dma(dap(mixT_d, pc * 128 * S + pb * S, [[S, 64], [1, S]]), ost.sl(0, 4096, 0, 64),
                            reads=[ost.buf])

            for step in range(nb + 2 if debug != "Bdf" else 0):
                if step < nb:
                    sb_stage1(step)
                if 0 <= step - 1 < nb:
                    sb_stage2(step - 1)
                if 0 <= step - 2 < nb:
                    sb_stage3(step - 2)

            load_qkv(qTd_d, kTd_d, vd_d, 0, 0)
            srot = [0]
            grp = [0]
            for h in range(4 if debug != "Bsb" else 0):
                slot = h % 2
                if h + 1 < 4:
                    load_qkv(qTd_d, kTd_d, vd_d, h + 1, (h + 1) % 2)
                dst = OST[ost_i[0] % 2]
                ost_i[0] += 1
                for tq in range(NTT):
                    ob = 2 + 3 * (grp[0] % 2)
                    grp[0] += 1
                    q0 = tq * 512
                    for b3 in range(3):
                        ms("dve", psl(ob + b3, 0, 512), 0.0, [], [PB[ob + b3]])
                    for m in range(2):
                        mb = 64 * m
                        for kb in range(4 * tq + 4):
                            c0 = 128 * max(0, kb - 4 * tq)
                            si = srot[0] % 2
                            Pm = Wt[srot[0] % 3]
                            srot[0] += 1
                            mm(si, psl(si, c0, 512), KT[slot].sl(kb * 128, (kb + 1) * 128, mb, mb + 64),
                               QT[slot].sl(q0 + c0, q0 + 512, mb, mb + 64), True, True,
                               [KT[slot].buf, QT[slot].buf])
                            act(Pm.sl(c0, 512), psl(si, c0, 512), AF.Exp, [PB[si]], [Pm.buf])
                            if kb >= 4 * tq:
                                ms("pool", Pm.sl(c0, c0 + 64, 64, 128), 0.0, [Pm.buf], [Pm.buf])
                            for j in range(c0 // 128, 4):
                                a = m * 4 + j
                                bi = ob + a // 3
                                co = (a % 3) * 132
                                mm(bi, psl(bi, co, co + 129), Pm.sl(j * 128, (j + 1) * 128),
                                   VP[slot].ap(kb * 132, [[1, 129]]), False, False, [Pm.buf, VP[slot].buf], skip=True)
                    den = DEN[tq % 2]
                    ssq = SSQ[tq % 2]
                    dj = DJ[tq % 2]
                    for a in range(8):
                        bi = ob + a // 3
                        co = (a % 3) * 132
                        cp("dve", den.sl(a, a + 1), psl(bi, co + 128, co + 129), [PB[bi]], [den.buf])
                    P.op("dve", (lambda o_, i_: (lambda e: e.reciprocal(out=o_, in_=i_)))(den.sl(0, 8), den.sl(0, 8)),
                         reads=[den.buf], writes=[den.buf])
                    ts("dve", den.sl(4, 8), den.sl(4, 8), neglam.sl(l, l + 1), None, MUL, None,
                       [den.buf, neglam.buf], [den.buf])
                    ms("pool", ssq.sl(0, 4), 0.0, [], [ssq.buf])
                    for j in range(4):
                        a0, a1 = j, 4 + j
                        b0, c0_ = ob + a0 // 3, (a0 % 3) * 132
                        b1, c1_ = ob + a1 // 3, (a1 % 3) * 132
                        djj = dj.sl(j * 128, (j + 1) * 128)
                        ts("dve", djj, psl(b0, c0_, c0_ + 128), den.sl(j, j + 1), None, MUL, None,
                           [PB[b0], den.buf], [dj.buf])
                        stt("dve", djj, psl(b1, c1_, c1_ + 128), den.sl(4 + j, 5 + j), djj, MUL, ADD,
                            [PB[b1], den.buf, dj.buf], [dj.buf])
                        act(JUNK.sl(0, 128), djj, AF.Square, [dj.buf], [JUNK.buf, ssq.buf], accum=ssq.sl(j, j + 1))
                    act(ssq.sl(0, 4), ssq.sl(0, 4), AF.Ln, [ssq.buf], [ssq.buf], scale=1.0 / 128, bias=EPSB)
                    act(ssq.sl(0, 4), ssq.sl(0, 4), AF.Exp, [ssq.buf], [ssq.buf], scale=-0.5)
                    for j in range(4):
                        djj = dj.sl(j * 128, (j + 1) * 128)
                        ts("dve", djj, djj, ssq.sl(j, j + 1), None, MUL, None, [dj.buf, ssq.buf], [dj.buf])
                    si = srot[0] % 2
                    srot[0] += 1
                    for j in range(4):
                        tr(si, psl(si, j * 128, (j + 1) * 128), dj.sl(j * 128, (j + 1) * 128), [dj.buf])
                    cp("dve", dst.sl(tq * 512, (tq + 1) * 512), psl(si, 0, 512), [PB[si]], [dst.buf])
                dma(dap(mixT_d, (4 + h) * 128 * S, [[S, 128], [1, S]]), dst.sl(0, 4096), reads=[dst.buf])
            if debug and debug.startswith("B"):
                break

            P.barrier()
            ABF.reset(); AF32.reset()
            WO = ABF.tile(8192, "wo")
            MT = [ABF.tile(4096, "mt%d" % i) for i in range(2)]
            HT2 = ABF.tile(4096, "ht2")
            W1G = [ABF.tile(4096, "w1g%d" % i) for i in range(2)]
            W2G = [ABF.tile(4096, "w2g%d" % i) for i in range(2)]
            Rt = [ABF.tile(512, "r%d" % i) for i in range(2)]
            UT = ABF.tile(16384, "ut")
            SQ = ABF.tile(4096, "sq")
            HTn = ABF.tile(4096, "htn")
            XTc = [AF32.tile(4096, "xtc%d" % i) for i in range(2)]
            RS = AF32.tile(512, "rs")
            YT = AF32.tile(4096, "yt")
            OUTS = [AF32.tile(1024, "outs%d" % i) for i in range(2)]
            last = (l == L - 1)
            for hf in range(2):
                dma(WO.sl(hf * 4096, (hf + 1) * 4096),
                    dap(wbo, (l * 2 + hf) * 128 * 4096, [[4096, 128], [1, 4096]]), writes=[WO.buf], parallel=(hf > 0))
            load_tile8(MT[0], mixT_d, 0)
            load_tile8(XTc[0], xT_d, 0)
            prot = [0]
            w1i = [0]
            w2i = [0]

            def load_w1(fg):
                w = W1G[w1i[0] % 2]
                w1i[0] += 1
                dma(w.sl(0, 4096), dap(wb1, (l * 8 + fg) * 128 * 4096, [[4096, 128], [1, 4096]]), writes=[w.buf])
                return w

            def load_w2(dc):
                w = W2G[w2i[0] % 2]
                w2i[0] += 1
                dma(w.sl(0, 4096), dap(wb2, (l * 8 + dc) * 128 * 4096, [[4096, 128], [1, 4096]]), writes=[w.buf])
                return w

            for tt_ in range(NTT):
                XT = XTc[tt_ % 2]
                MTt = MT[tt_ % 2]
                if tt_ + 1 < NTT:
                    load_tile8(MT[(tt_ + 1) % 2], mixT_d, tt_ + 1)
                    load_tile8(XTc[(tt_ + 1) % 2], xT_d, tt_ + 1)
                w1n = load_w1(0)
                for dc in range(8):
                    pi = prot[0] % 4
                    prot[0] += 1
                    for ec in range(8):
                        mm(pi, psl(pi, 0, 512), WO.sl(ec * 1024 + dc * 128, ec * 1024 + (dc + 1) * 128),
                           MTt.sl(ec * 512, (ec + 1) * 512), ec == 0, ec == 7, [WO.buf, MTt.buf])
                    tt("dve", XT.sl(dc * 512, (dc + 1) * 512), psl(pi, 0, 512), XT.sl(dc * 512, (dc + 1) * 512), ADD,
                       [PB[pi], XT.buf], [XT.buf])
                norm_stats(XT, SQ, RS, 4)
                norm_apply(XT, RS, HT2, "pool")
                for fg in range(8):
                    w1 = w1n
                    if fg + 1 < 8:
                        w1n = load_w1(fg + 1)
                    else:
                        w2n = load_w2(0)
                    for fi in range(4):
                        fc = fg * 4 + fi
                        pi = prot[0] % 4
                        prot[0] += 1
                        for dmc in range(8):
                            mm(pi, psl(pi, 0, 512), w1.sl(dmc * 512 + fi * 128, dmc * 512 + (fi + 1) * 128),
                               HT2.sl(dmc * 512, (dmc + 1) * 512), dmc == 0, dmc == 7, [w1.buf, HT2.buf])
                        R = Rt[fc % 2]
                        act(R.sl(0, 512), psl(pi, 0, 512), AF.Relu, [PB[pi]], [R.buf])
                        tt("pool", UT.sl(fc * 512, (fc + 1) * 512), R.sl(0, 512), R.sl(0, 512), MUL, [R.buf], [UT.buf])
                for dc in range(8):
                    w2 = w2n
                    if dc + 1 < 8:
                        w2n = load_w2(dc + 1)
                    pi = prot[0] % 4
                    prot[0] += 1
                    for fc in range(32):
                        mm(pi, psl(pi, 0, 512), w2.sl(fc * 128, (fc + 1) * 128), UT.sl(fc * 512, (fc + 1) * 512),
                           fc == 0, fc == 31, [w2.buf, UT.buf])
                    tt("dve", XT.sl(dc * 512, (dc + 1) * 512), psl(pi, 0, 512), XT.sl(dc * 512, (dc + 1) * 512), ADD,
                       [PB[pi], XT.buf], [XT.buf])
                norm_stats(XT, SQ, RS, 4)
                if not last:
                    store_tile8(XT, xT_d, tt_)
                    norm_apply(XT, RS, HTn, "pool")
                    store_tile8(HTn, hT_d, tt_)
                else:
                    if debug:
                        store_tile8(XT, xT_d, tt_)
                    for c in range(8):
                        stt("dve", YT.sl(c * 512, (c + 1) * 512), XT.sl(c * 512, (c + 1) * 512), VEC.sl(64 + c, 65 + c),
                            RS.sl(0, 512), MUL, MUL, [XT.buf, VEC.buf, RS.buf], [YT.buf])
                    for j in range(4):
                        tb = tt_ * 4 + j
                        outs = OUTS[tb % 2]
                        for hb in range(2):
                            pi = 5 + (tb * 2 + hb) % 3
                            for c4 in range(4):
                                c = hb * 4 + c4
                                tr(pi, psl(pi, c4 * 128, (c4 + 1) * 128),
                                   YT.sl(c * 512 + j * 128, c * 512 + (j + 1) * 128), [YT.buf])
                            cp("dve", outs.sl(hb * 512, (hb + 1) * 512), psl(pi, 0, 512), [PB[pi]], [outs.buf])
                        dma(out_d.ap()[tb * 128:(tb + 1) * 128, :], outs.sl(0, 1024), reads=[outs.buf])
            if debug == "C":
                break
        P.emit()
        nc._prog_stats = {e: len(P.ops[e]) for e in ENGS}
    return nc


def host_layout(w_in, attn_norm, subln_norm, lam_q1, lam_k1, lam_q2, lam_k2, mlp_norm, final_norm):
    f32 = np.float32
    perm = np.concatenate([(np.arange(64) + 32) % 64 + 64 * g for g in range(8)])
    sbq, sbk, sbv = w_in[:, :, 0:512], w_in[:, :, 512:1024], w_in[:, :, 1024:1536]
    dfq, dfk, dfv = w_in[:, :, 1536:2048], w_in[:, :, 2048:2560], w_in[:, :, 2560:3072]
    w_ext = np.ascontiguousarray(np.concatenate(
        [sbq, sbk, dfq, dfq[:, :, perm], dfk, dfk[:, :, perm], sbv, dfv], axis=2).astype(f32))
    vecs = np.zeros((128, 76 + 1024), f32)
    vecs[:, 0:32] = attn_norm.reshape(L, 8, 128).transpose(2, 0, 1).reshape(128, 32)
    vecs[:, 32:64] = mlp_norm.reshape(L, 8, 128).transpose(2, 0, 1).reshape(128, 32)
    vecs[:, 64:72] = final_norm.reshape(8, 128).T
    vecs[:, 72:76] = subln_norm.T
    lam = np.stack([lam_q1, lam_k1, lam_q2, lam_k2], 0).reshape(-1)
    vecs[:, 76:] = np.broadcast_to(lam[None, :], (128, 1024))
    consts = np.zeros((128, 640), f32)
    r = np.arange(128)
    consts[:, 0:128] = np.eye(128, dtype=f32)
    consts[:, 128:256] = np.where(r[:, None] >= r[None, :], NEG, 0.0)
    consts[:, 256:384] = np.where(r[:, None] >= r[None, :], -1.0, 0.0)
    consts[:, 384:512] = -1.0
    consts[:, 512:640] = 1.0
    inv = 1.0 / (10000.0 ** (np.arange(0, 64, 2, dtype=np.float32) / 64))
    ang = np.arange(S, dtype=np.float32)[:, None] * inv[None, :]
    ang = np.concatenate([ang, ang], -1)
    cos = np.cos(ang).T.astype(f32)
    sin = np.sin(ang).T.astype(f32)
    sin_s = np.concatenate([-sin[:32], sin[32:]], 0)
    rope = np.stack([np.concatenate([cos, cos], 0), np.concatenate([sin_s, sin_s], 0)], 0)
    return w_ext, vecs, consts, np.ascontiguousarray(rope.astype(f32))


_NC_CACHE = {}


def kernel(x, w_in, w_o, attn_norm, subln_norm, lam_q1, lam_k1, lam_q2, lam_k2, mlp_norm, w_ff1, w_ff2,
           final_norm):
    x = np.asarray(x, np.float32)
    w_ext, vecs, consts, rope = host_layout(np.asarray(w_in, np.float32), np.asarray(attn_norm, np.float32),
                                            np.asarray(subln_norm, np.float32), np.asarray(lam_q1, np.float32),
                                            np.asarray(lam_k1, np.float32), np.asarray(lam_q2, np.float32),
                                            np.asarray(lam_k2, np.float32), np.asarray(mlp_norm, np.float32),
                                            np.asarray(final_norm, np.float32))
    if "nc" not in _NC_CACHE:
        _NC_CACHE["nc"] = build()
    nc = _NC_CACHE["nc"]
    shared = {"w_in": w_ext, "w_o": np.ascontiguousarray(np.asarray(w_o, np.float32)),
              "w_ff1": np.ascontiguousarray(np.asarray(w_ff1, np.float32)),
              "w_ff2": np.ascontiguousarray(np.asarray(w_ff2, np.float32)),
              "vecs": vecs, "consts": consts, "rope": rope}
    in_maps = [dict(shared, x=np.ascontiguousarray(x[c])) for c in range(8)]
    res = run_bass_kernel_spmd(nc, in_maps, core_ids=list(range(8)))
    return np.stack([np.asarray(r["out"], np.float32) for r in res.results], 0)
```

```python
import math
from contextlib import ExitStack

import numpy as np
import concourse.bass as bass
import concourse.mybir as mybir
from concourse.bass_utils import run_bass_kernel_spmd

F32 = mybir.dt.float32
BF16 = mybir.dt.bfloat16
AF = mybir.ActivationFunctionType
ALU = mybir.AluOpType
AX = mybir.AxisListType

S = 4096
D = 1024
L = 4
NTT = 8
EPS = 1e-6
NEG = -30000.0
ENGS = ("pe", "act", "dve", "pool", "sp")
DMA_INC = 16


class Buf:
    __slots__ = ("name", "writers", "readers", "sem", "cnt", "gen_deps")

    def __init__(self, name=""):
        self.name = name
        self.writers = []
        self.readers = []
        self.sem = None
        self.cnt = 0
        self.gen_deps = []


class Op:
    __slots__ = ("eng", "fn", "deps", "sig", "dma", "count")

    def __init__(self, eng, fn):
        self.eng = eng
        self.fn = fn
        self.deps = []
        self.sig = False
        self.dma = None
        self.count = None


class Prog:
    def __init__(self, nc, n_dma_sems=160):
        self.nc = nc
        self.ops = {e: [] for e in ENGS}
        self.n_dma_sems = n_dma_sems
        self.dma_sem_next = 0
        self.dma_sem_max = 0
        self.dma_last = {}
        self.pending = {e: [] for e in ENGS}

    def op(self, eng, fn, reads=(), writes=()):
        o = Op(eng, fn)
        idx = len(self.ops[eng])
        deps = []
        for b in reads:
            deps.extend(b.writers)
        for b in writes:
            deps.extend(b.writers)
            deps.extend(b.readers)
        deps.extend(self.pending[eng])
        self.pending[eng] = []
        ev = ("e", eng, idx)
        o.deps = [d for d in deps if not (d[0] == "e" and d[1] == "pe" and eng == "pe")]
        self.ops[eng].append(o)
        for b in reads:
            b.readers.append(ev)
        for b in writes:
            b.gen_deps = b.writers + b.readers
            b.writers = [ev]
            b.readers = []
        return ev

    def dma(self, eng, fn, reads=(), writes=(), sem_buf=None, parallel=False):
        o = Op(eng, fn)
        if sem_buf is None:
            sem_buf = writes[0] if writes else reads[0]
        if sem_buf.sem is None:
            sem_buf.sem = self.dma_sem_next
            sem_buf.cnt = self.dma_last.get(sem_buf.sem, 0)
            self.dma_sem_next += 1
            self.dma_sem_max = max(self.dma_sem_max, self.dma_sem_next)
            assert self.dma_sem_next <= self.n_dma_sems, "out of dma sems"
        sem_buf.cnt += DMA_INC
        ev = ("d", sem_buf.sem, sem_buf.cnt)
        self.dma_last[sem_buf.sem] = sem_buf.cnt
        deps = []
        for b in reads:
            deps.extend(b.writers)
        for b in writes:
            if parallel and b.writers and all(w[0] == "d" for w in b.writers) and not b.readers:
                deps.extend(b.gen_deps)
            else:
                deps.extend(b.writers)
                deps.extend(b.readers)
        deps.extend(self.pending[eng])
        self.pending[eng] = []
        o.deps = deps
        o.dma = sem_buf.sem
        self.ops[eng].append(o)
        for b in reads:
            b.readers.append(ev)
        for b in writes:
            if parallel and b.writers and all(w[0] == "d" for w in b.writers) and not b.readers:
                b.writers = b.writers + [ev]
            else:
                b.gen_deps = b.writers + b.readers
                b.writers = [ev]
                b.readers = []
        return ev

    def barrier(self):
        evs = []
        for e in ENGS:
            for i in range(len(self.ops[e]) - 1, -1, -1):
                if self.ops[e][i].dma is None:
                    evs.append(("e", e, i))
                    break
        for s, v in self.dma_last.items():
            evs.append(("d", s, v))
        for e in ENGS:
            self.pending[e].extend(evs)
        self.dma_sem_next = 0

    def emit(self):
        nc = self.nc
        for e in ENGS:
            for o in self.ops[e]:
                for d in o.deps:
                    if d[0] == "e":
                        self.ops[d[1]][d[2]].sig = True
        for e in ENGS:
            c = 0
            for o in self.ops[e]:
                if o.dma is None and o.sig:
                    c += 1
                    o.count = c
        with ExitStack() as st:
            esem = {e: st.enter_context(nc.semaphore("s_" + e)) for e in ENGS}
            dsem = [st.enter_context(nc.semaphore("d%d" % i)) for i in range(self.dma_sem_max)]
            block = st.enter_context(nc.Block())
            handles = {"pe": block.tensor, "act": block.scalar, "dve": block.vector,
                       "pool": block.gpsimd, "sp": block.sync}

            def make(e):
                def body(eng):
                    waited = {}
                    for o in self.ops[e]:
                        need = {}
                        for d in o.deps:
                            if d[0] == "e":
                                key = ("e", d[1])
                                val = self.ops[d[1]][d[2]].count
                            else:
                                key = ("d", d[1])
                                val = d[2]
                            if val > need.get(key, 0):
                                need[key] = val
                        for key, val in need.items():
                            if waited.get(key, 0) >= val:
                                continue
                            waited[key] = val
                            sem = esem[key[1]] if key[0] == "e" else dsem[key[1]]
                            eng.wait_ge(sem, val)
                        ins = o.fn(eng)
                        if o.dma is not None:
                            ins.then_inc(dsem[o.dma], DMA_INC)
                        elif o.sig:
                            ins.then_inc(esem[e], 1)
                    if e == "sp":
                        for s, v in self.dma_last.items():
                            if waited.get(("d", s), 0) < v:
                                eng.wait_ge(dsem[s], v)
                return body

            for e in ENGS:
                handles[e](make(e))


class Tile:
    def __init__(self, t, N, off, n, name=""):
        self.t, self.N, self.off, self.n = t, N, off, n
        self.buf = Buf(name)

    def ap(self, o, dims, p0=0, npart=128):
        return bass.AP(self.t, p0 * self.N + self.off + o, [[self.N, npart]] + [list(d) for d in dims])

    def sl(self, a, b, p0=0, p1=128):
        return self.t[p0:p1, self.off + a:self.off + b]


class Arena:
    def __init__(self, t, N):
        self.t, self.N, self.off = t, N, 0

    def reset(self):
        self.off = 0

    def tile(self, n, name=""):
        o = self.off
        self.off += n
        assert self.off <= self.N, ("arena overflow", name, self.off, self.N)
        return Tile(self.t, self.N, o, n, name)


def lam_init(l):
    return 0.8 - 0.6 * math.exp(-0.3 * l)


def build(n_layers=L, debug=False):
    nc = bass.Bass("TRN2", target_bir_lowering=False)
    dt_ = nc.dram_tensor
    x_d = dt_("x", [S, D], F32, kind="ExternalInput")
    win_d = dt_("w_in", [L, D, 4096], F32, kind="ExternalInput")
    wo_d = dt_("w_o", [L, D, D], F32, kind="ExternalInput")
    w1_d = dt_("w_ff1", [L, D, 4096], F32, kind="ExternalInput")
    w2_d = dt_("w_ff2", [L, 4096, D], F32, kind="ExternalInput")
    NV = 76 + 1024
    vec_d = dt_("vecs", [128, NV], F32, kind="ExternalInput")
    cst_d = dt_("consts", [128, 640], F32, kind="ExternalInput")
    rope_d = dt_("rope", [2, 128, S], F32, kind="ExternalInput")
    out_d = dt_("out", [S, D], F32, kind="ExternalOutput")
    sk = "ExternalOutput" if debug else "Internal"
    wbin = dt_("wbin", [L * 8, 128, 4096], BF16)
    wbo = dt_("wbo", [L * 2, 128, 4096], BF16)
    wb1 = dt_("wb1", [L * 8, 128, 4096], BF16)
    wb2 = dt_("wb2", [L * 8, 128, 4096], BF16)
    xT_d = dt_("xT", [8, 128, S], F32, kind=sk)
    hT_d = dt_("hT", [8, 128, S], BF16, kind=sk)
    qTs_d = dt_("qTs", [4, 128, S], BF16, kind=sk)
    kTs_d = dt_("kTs", [4, 128, S], BF16, kind=sk)
    qTd_d = dt_("qTd", [4, 128, S], BF16, kind=sk)
    kTd_d = dt_("kTd", [4, 128, S], BF16, kind=sk)
    vs_d = dt_("vs", [32, 128, 512], BF16, kind=sk)
    vd_d = dt_("vd", [32, 128, 512], BF16, kind=sk)
    mixT_d = dt_("mixT", [8, 128, S], BF16, kind=sk)

    def dap(h, off, dims):
        return bass.AP(h, off, [list(d) for d in dims])

    NBF = 63488
    NF32 = 15360
    with ExitStack() as st:
        abf_t = st.enter_context(nc.sbuf_tensor("abf", [128, NBF], BF16))
        af_t = st.enter_context(nc.sbuf_tensor("af32", [128, NF32], F32))
        cb_t = st.enter_context(nc.sbuf_tensor("cb", [128, 640], BF16))
        cf_t = st.enter_context(nc.sbuf_tensor("cf", [128, 640], F32))
        vec_t = st.enter_context(nc.sbuf_tensor("vec", [128, NV], F32))
        sm_t = st.enter_context(nc.sbuf_tensor("small", [128, 512], F32))
        psb = [st.enter_context(nc.psum_tensor("ps%d" % i, [128, 512], F32)) for i in range(8)]
        PB = [Buf("ps%d" % i) for i in range(8)]
        P = Prog(nc)
        ABF = Arena(abf_t, NBF)
        AF32 = Arena(af_t, NF32)
        CB = Tile(cb_t, 640, 0, 640, "cb")
        CF = Tile(cf_t, 640, 0, 640, "cf")
        VEC = Tile(vec_t, NV, 0, NV, "vec")
        SM = Arena(sm_t, 512)
        sfold = SM.tile(4, "sfold")
        neglam = SM.tile(4, "neglam")
        lamt = SM.tile(8, "lamt")
        epst = SM.tile(1, "epst")

        IDF = CF.sl(0, 128)
        IDB = CB.sl(0, 128)
        NEGM = CB.sl(128, 256)
        LTRI = CB.sl(256, 384)
        NEG1 = CB.sl(384, 512)
        ONES = CB.sl(512, 640)

        def psl(i, a, b, p0=0, p1=128):
            return psb[i][p0:p1, a:b]

        EPSB = epst.sl(0, 1)
        P.op("pool", lambda e: e.memset(EPSB, EPS), writes=[epst.buf])

        def mm(pi, out, lhsT, rhs, start, stop, reads, skip=False):
            if skip:
                P.op("pe", lambda e: e.matmul(out=out, lhsT=lhsT, rhs=rhs, start=start, stop=stop,
                                              skip_group_check=True), reads=reads, writes=[PB[pi]])
            else:
                P.op("pe", lambda e: e.matmul(out=out, lhsT=lhsT, rhs=rhs, start=start, stop=stop),
                     reads=reads, writes=[PB[pi]])

        def tr(pi, out, in_, reads):
            P.op("pe", lambda e: e.transpose(out=out, in_=in_, identity=IDF), reads=list(reads) + [CF.buf],
                 writes=[PB[pi]])

        def act(out, in_, func, reads, writes, scale=1.0, bias=0.0, accum=None):
            if accum is None:
                P.op("act", lambda e: e.activation(out=out, in_=in_, func=func, bias=bias, scale=scale),
                     reads=reads, writes=writes)
            else:
                P.op("act", lambda e: e.activation(out=out, in_=in_, func=func, bias=bias, scale=scale,
                                                   accum_out=accum), reads=reads, writes=writes)

        def tt(eng, out, in0, in1, op, reads, writes):
            P.op(eng, lambda e: e.tensor_tensor(out=out, in0=in0, in1=in1, op=op), reads=reads, writes=writes)

        def ts(eng, out, in0, s1, s2, op0, op1, reads, writes):
            if s2 is None:
                P.op(eng, lambda e: e.tensor_scalar(out=out, in0=in0, scalar1=s1, scalar2=None, op0=op0),
                     reads=reads, writes=writes)
            else:
                P.op(eng, lambda e: e.tensor_scalar(out=out, in0=in0, scalar1=s1, scalar2=s2, op0=op0, op1=op1),
                     reads=reads, writes=writes)

        def stt(eng, out, in0, scalar, in1, op0, op1, reads, writes):
            P.op(eng, lambda e: e.scalar_tensor_tensor(out=out, in0=in0, scalar=scalar, in1=in1, op0=op0, op1=op1),
                 reads=reads, writes=writes)

        def cp(eng, out, in_, reads, writes):
            P.op(eng, lambda e: e.tensor_copy(out=out, in_=in_), reads=reads, writes=writes)

        def ms(eng, ap_, val, reads, writes):
            P.op(eng, lambda e: e.memset(ap_, val), reads=reads, writes=writes)

        def dma(out, in_, reads=(), writes=(), parallel=False):
            P.dma("sp", lambda e: e.dma_start(out=out, in_=in_), reads=list(reads), writes=list(writes),
                  parallel=parallel)

        MUL, ADD, SUB = ALU.mult, ALU.add, ALU.subtract

        dma(cf_t[:, :], cst_d.ap()[:, :], writes=[CF.buf])
        dma(vec_t[:, :], vec_d.ap()[:, :], writes=[VEC.buf])
        cp("dve", cb_t[:, :], cf_t[:, :], [CF.buf], [CB.buf])
        for l in range(L):
            ts("pool", sfold.sl(l, l + 1), VEC.sl(72 + l, 73 + l), 1.0 - lam_init(l), None, MUL, None,
               [VEC.buf], [sfold.buf])
        lprod = AF32.tile(512, "lprod")
        tt("dve", lprod.sl(0, 256), VEC.sl(76, 76 + 256), VEC.sl(76 + 256, 76 + 512), MUL, [VEC.buf], [lprod.buf])
        tt("dve", lprod.sl(256, 512), VEC.sl(76 + 512, 76 + 768), VEC.sl(76 + 768, 76 + 1024), MUL, [VEC.buf],
           [lprod.buf])
        lp_in = lprod.ap(0, [[64, 8], [1, 64]])
        lp_out = lamt.sl(0, 8)
        P.op("dve", lambda e: e.tensor_reduce(out=lp_out, in_=lp_in, axis=AX.X, op=ADD),
             reads=[lprod.buf], writes=[lamt.buf])
        act(lamt.sl(0, 8), lamt.sl(0, 8), AF.Exp, [lamt.buf], [lamt.buf])
        tt("dve", lamt.sl(0, 4), lamt.sl(0, 4), lamt.sl(4, 8), SUB, [lamt.buf], [lamt.buf])
        for l in range(L):
            ts("dve", neglam.sl(l, l + 1), lamt.sl(l, l + 1), lam_init(l), -1.0, ADD, MUL, [lamt.buf], [neglam.buf])

        P.barrier()
        ABF.reset(); AF32.reset()
        WS = [AF32.tile(4096, "ws%d" % i) for i in range(2)]
        WB = [ABF.tile(4096, "wb%d" % i) for i in range(2)]
        prep = []
        for l in range(n_layers):
            prep += [("in", l, g) for g in range(8)] + [("o", l, h) for h in range(2)]
            prep += [("f1", l, g) for g in range(8)] + [("f2", l, g) for g in range(8)]
        for i, (kind, l, g) in enumerate(prep):
            ws, wb = WS[i % 2], WB[i % 2]
            ce = "dve" if i % 2 == 0 else "pool"
            if kind in ("in", "f1"):
                src_h = win_d if kind == "in" else w1_d
                dma(ws.ap(0, [[512, 8], [1, 512]]),
                    dap(src_h, l * D * 4096 + g * 512, [[4096, 128], [128 * 4096, 8], [1, 512]]), writes=[ws.buf])
                gcol = (0 if kind == "in" else 32) + l * 8
                tt(ce, wb.ap(0, [[512, 8], [1, 512]]), ws.ap(0, [[512, 8], [1, 512]]),
                   VEC.ap(gcol, [[1, 8], [0, 512]]), MUL, [ws.buf, VEC.buf], [wb.buf])
                dst = dap(wbin if kind == "in" else wb1, (l * 8 + g) * 128 * 4096, [[4096, 128], [1, 4096]])
            elif kind == "f2":
                for q4 in range(4):
                    dma(ws.ap(q4 * 8 * 128, [[128, 8], [1, 128]]),
                        dap(w2_d, l * 4096 * D + q4 * 8 * 128 * D + g * 128, [[D, 128], [128 * D, 8], [1, 128]]),
                        writes=[ws.buf], parallel=(q4 > 0))
                cp(ce, wb.sl(0, 4096), ws.sl(0, 4096), [ws.buf], [wb.buf])
                dst = dap(wb2, (l * 8 + g) * 128 * 4096, [[4096, 128], [1, 4096]])
            else:
                dma(ws.ap(0, [[1024, 4], [1, 1024]]),
                    dap(wo_d, l * D * D + g * 512 * D, [[D, 128], [128 * D, 4], [1, 1024]]), writes=[ws.buf])
                if g == 0:
                    cp(ce, wb.sl(0, 4096), ws.sl(0, 4096), [ws.buf], [wb.buf])
                else:
                    ts(ce, wb.sl(0, 4096), ws.sl(0, 4096), sfold.sl(l, l + 1), None, MUL, None,
                       [ws.buf, sfold.buf], [wb.buf])
                dst = dap(wbo, (l * 2 + g) * 128 * 4096, [[4096, 128], [1, 4096]])
            dma(dst, wb.sl(0, 4096), reads=[wb.buf])

        def norm_stats(XT, SQ, RS, pi):
            act(SQ.sl(0, 4096), XT.sl(0, 4096), AF.Square, [XT.buf], [SQ.buf])
            for c in range(8):
                mm(pi, psl(pi, 0, 512), ONES, SQ.sl(c * 512, (c + 1) * 512), c == 0, c == 7, [SQ.buf, CB.buf])
            act(RS.sl(0, 512), psl(pi, 0, 512), AF.Ln, [PB[pi], epst.buf], [RS.buf], scale=1.0 / D, bias=EPSB)
            act(RS.sl(0, 512), RS.sl(0, 512), AF.Exp, [RS.buf], [RS.buf], scale=-0.5)

        def norm_apply(XT, RS, HT, eng):
            tt(eng, HT.ap(0, [[512, 8], [1, 512]]), XT.ap(0, [[512, 8], [1, 512]]), RS.ap(0, [[0, 8], [1, 512]]),
               MUL, [XT.buf, RS.buf], [HT.buf])

        def store_tile8(T_, dram_h, tt_):
            dma(dap(dram_h, tt_ * 512, [[S, 128], [128 * S, 8], [1, 512]]), T_.ap(0, [[512, 8], [1, 512]]),
                reads=[T_.buf])

        def load_tile8(T_, dram_h, tt_):
            dma(T_.ap(0, [[512, 8], [1, 512]]), dap(dram_h, tt_ * 512, [[S, 128], [128 * S, 8], [1, 512]]),
                writes=[T_.buf])

        P.barrier()
        ABF.reset(); AF32.reset()
        XIN = [AF32.tile(1024, "xin%d" % i) for i in range(2)]
        XTt = [AF32.tile(4096, "xt%d" % i) for i in range(2)]
        RSt = [AF32.tile(512, "rs%d" % i) for i in range(2)]
        SQt = [ABF.tile(4096, "sq%d" % i) for i in range(2)]
        HTt = [ABF.tile(4096, "ht%d" % i) for i in range(2)]
        for tt_ in range(NTT):
            XT = XTt[tt_ % 2]
            for j in range(4):
                tb = tt_ * 4 + j
                xin = XIN[tb % 2]
                dma(xin.sl(0, 1024), x_d.ap()[tb * 128:(tb + 1) * 128, :], writes=[xin.buf])
                for hb in range(2):
                    pi = (tb * 2 + hb) % 4
                    for c4 in range(4):
                        c = hb * 4 + c4
                        tr(pi, psl(pi, c4 * 128, (c4 + 1) * 128), xin.sl(c * 128, (c + 1) * 128), [xin.buf])
                    cp("dve", XT.ap(hb * 4 * 512 + j * 128, [[512, 4], [1, 128]]),
                       bass.AP(psb[pi], 0, [[512, 128], [128, 4], [1, 128]]), [PB[pi]], [XT.buf])
            store_tile8(XT, xT_d, tt_)
            norm_stats(XT, SQt[tt_ % 2], RSt[tt_ % 2], 4 + tt_ % 2)
            norm_apply(XT, RSt[tt_ % 2], HTt[tt_ % 2], "pool")
            store_tile8(HTt[tt_ % 2], hT_d, tt_)

        for l in range(n_layers):
            P.barrier()
            ABF.reset(); AF32.reset()
            HTR = ABF.tile(32768, "htr")
            WG = [ABF.tile(4096, "wg%d" % i) for i in range(3)]
            STG = [ABF.tile(4096, "stg%d" % i) for i in range(2)]
            VSTG = [ABF.tile(2048, "vstg%d" % i) for i in range(2)]
            COS = AF32.tile(4096, "cos")
            SIN = AF32.tile(4096, "sin")
            T1 = [AF32.tile(512, "t1_%d" % i) for i in range(2)]
            T2 = [AF32.tile(512, "t2_%d" % i) for i in range(2)]
            for c in range(8):
                dma(HTR.sl(c * 4096, (c + 1) * 4096), dap(hT_d, c * 128 * S, [[S, 128], [1, S]]),
                    writes=[HTR.buf], parallel=(c > 0))
            dma(COS.sl(0, 4096), dap(rope_d, 0, [[S, 128], [1, S]]), writes=[COS.buf])
            dma(SIN.sl(0, 4096), dap(rope_d, 128 * S, [[S, 128], [1, S]]), writes=[SIN.buf])
            wg_i = [0]

            def load_wg(g):
                w = WG[wg_i[0] % 3]
                wg_i[0] += 1
                dma(w.sl(0, 4096), dap(wbin, (l * 8 + g) * 128 * 4096, [[4096, 128], [1, 4096]]), writes=[w.buf])
                return w

            psr = [0]
            stg_i = [0]
            for g, dst_h, sc in ((0, qTs_d, 0.125), (1, kTs_d, 1.0)):
                w = load_wg(g)
                for ci in range(4):
                    stg = STG[stg_i[0] % 2]
                    stg_i[0] += 1
                    for tt_ in range(NTT):
                        pi = psr[0] % 4
                        psr[0] += 1
                        for dmc in range(8):
                            mm(pi, psl(pi, 0, 512), w.sl(dmc * 512 + ci * 128, dmc * 512 + (ci + 1) * 128),
                               HTR.sl(dmc * 4096 + tt_ * 512, dmc * 4096 + (tt_ + 1) * 512), dmc == 0, dmc == 7,
                               [w.buf, HTR.buf])
                        act(stg.sl(tt_ * 512, (tt_ + 1) * 512), psl(pi, 0, 512), AF.Copy, [PB[pi]], [stg.buf],
                            scale=sc)
                    dma(dap(dst_h, ci * 128 * S, [[S, 128], [1, S]]), stg.sl(0, 4096), reads=[stg.buf])
            for g, dst_h, sc in ((2, qTd_d, 0.125), (4, kTd_d, 1.0)):
                wa = load_wg(g)
                wp = load_wg(g + 1)
                for ci in range(4):
                    stg = STG[stg_i[0] % 2]
                    stg_i[0] += 1
                    for tt_ in range(NTT):
                        pa = 4 + psr[0] % 2
                        pb_ = 6 + psr[0] % 2
                        t1 = T1[psr[0] % 2]
                        t2 = T2[psr[0] % 2]
                        psr[0] += 1
                        for pi, w in ((pa, wa), (pb_, wp)):
                            for dmc in range(8):
                                mm(pi, psl(pi, 0, 512), w.sl(dmc * 512 + ci * 128, dmc * 512 + (ci + 1) * 128),
                                   HTR.sl(dmc * 4096 + tt_ * 512, dmc * 4096 + (tt_ + 1) * 512), dmc == 0,
                                   dmc == 7, [w.buf, HTR.buf])
                        stt("dve", t1.sl(0, 512), psl(pa, 0, 512), sc, COS.sl(tt_ * 512, (tt_ + 1) * 512), MUL, MUL,
                            [PB[pa], COS.buf], [t1.buf])
                        stt("dve", t2.sl(0, 512), psl(pb_, 0, 512), sc, SIN.sl(tt_ * 512, (tt_ + 1) * 512), MUL, MUL,
                            [PB[pb_], SIN.buf], [t2.buf])
                        tt("pool", stg.sl(tt_ * 512, (tt_ + 1) * 512), t1.sl(0, 512), t2.sl(0, 512), ADD,
                           [t1.buf, t2.buf], [stg.buf])
                    dma(dap(dst_h, ci * 128 * S, [[S, 128], [1, S]]), stg.sl(0, 4096), reads=[stg.buf])
            for g, dst_h in ((6, vs_d), (7, vd_d)):
                w = load_wg(g)
                for tb in range(32):
                    vst = VSTG[(tb // 4) % 2]
                    pi = psr[0] % 4
                    psr[0] += 1
                    for dmc in range(8):
                        mm(pi, psl(pi, 0, 512), HTR.sl(dmc * 4096 + tb * 128, dmc * 4096 + (tb + 1) * 128),
                           w.sl(dmc * 512, (dmc + 1) * 512), dmc == 0, dmc == 7, [w.buf, HTR.buf])
                    act(vst.sl((tb % 4) * 512, (tb % 4 + 1) * 512), psl(pi, 0, 512), AF.Copy, [PB[pi]], [vst.buf])
                    if tb % 4 == 3:
                        dma(dap(dst_h, (tb - 3) * 128 * 512, [[512, 128], [128 * 512, 4], [1, 512]]),
                            vst.ap(0, [[512, 4], [1, 512]]), reads=[vst.buf])
            if debug == "A":
                break

            P.barrier()
            ABF.reset(); AF32.reset()
            QT = [ABF.tile(4096, "qt%d" % i) for i in range(2)]
            KT = [ABF.tile(4096, "kt%d" % i) for i in range(2)]
            VP = [ABF.tile(32 * 132, "vp%d" % i) for i in range(2)]
            SPt = [ABF.tile(512, "sp%d" % i) for i in range(3)]
            Wt = [ABF.tile(512, "w%d" % i) for i in range(3)]
            ACC = [ABF.tile(512, "acc%d" % i) for i in range(2)]
            OST = [ABF.tile(4096, "ost%d" % i) for i in range(2)]
            Et = [AF32.tile(512, "e%d" % i) for i in range(3)]
            DJ = [AF32.tile(512, "dj%d" % i) for i in range(2)]
            DEN = [AF32.tile(16, "den%d" % i) for i in range(2)]
            SSQ = [AF32.tile(8, "ssq%d" % i) for i in range(2)]
            JUNK = AF32.tile(128, "junk")
            for i in range(2):
                ms("pool", VP[i].ap(128, [[132, 32], [1, 1]]), 1.0, [], [VP[i].buf])

            def load_qkv(qh, kh, vh, c, slot):
                dma(QT[slot].sl(0, 4096), dap(qh, c * 128 * S, [[S, 128], [1, S]]), writes=[QT[slot].buf])
                dma(KT[slot].sl(0, 4096), dap(kh, c * 128 * S, [[S, 128], [1, S]]), writes=[KT[slot].buf])
                for q8 in range(8):
                    dma(VP[slot].ap(q8 * 4 * 132, [[132, 4], [1, 128]]),
                        dap(vh, q8 * 4 * 128 * 512 + c * 128, [[512, 128], [128 * 512, 4], [1, 128]]),
                        writes=[VP[slot].buf], parallel=(q8 > 0))

            blocks = []
            for pc in range(4):
                for hh in range(2):
                    for tq in range(NTT):
                        for i, kb in enumerate(range(4 * tq + 3, -1, -1)):
                            blocks.append((pc, pc % 2, hh, tq, i, kb))
            nb = len(blocks)
            load_qkv(qTs_d, kTs_d, vs_d, 0, 0)
            ost_i = [0]
            cur_ost = {}

            def sb_stage1(n):
                pc, slot, hh, tq, i, kb = blocks[n]
                if hh == 0 and tq == 0 and i == 0 and pc + 1 < 4:
                    load_qkv(qTs_d, kTs_d, vs_d, pc + 1, (pc + 1) % 2)
                pb = 64 * hh
                c0 = 128 * max(0, kb - 4 * tq)
                diag = kb >= 4 * tq
                q0 = tq * 512
                zi = n % 2
                acc = ACC[tq % 2]
                if i == 0:
                    ms("pool", acc.sl(0, 512), 0.0, [], [acc.buf])
                mm(zi, psl(zi, c0, 512), KT[slot].sl(kb * 128, (kb + 1) * 128, pb, pb + 64),
                   QT[slot].sl(q0 + c0, q0 + 512, pb, pb + 64), True, not diag, [KT[slot].buf, QT[slot].buf])
                if diag:
                    mm(zi, psl(zi, c0, c0 + 128), IDB, NEGM, False, True, [CB.buf])
                E = Et[n % 3]
                SPn = SPt[n % 3]
                act(E.sl(c0, 512), psl(zi, c0, 512), AF.Exp, [PB[zi]], [E.buf])
                act(SPn.sl(c0, 512), E.sl(c0, 512), AF.Ln, [E.buf], [SPn.buf], bias=1.0)

            def sb_stage2(n):
                pc, slot, hh, tq, i, kb = blocks[n]
                pb = 64 * hh
                c0 = 128 * max(0, kb - 4 * tq)
                diag = kb >= 4 * tq
                q0 = tq * 512
                ai = 2 + n % 2
                acc = ACC[tq % 2]
                SPn = SPt[n % 3]
                Wn = Wt[n % 3]
                mm(ai, psl(ai, c0, 512), KT[slot].sl(kb * 128, (kb + 1) * 128, pb, pb + 64),
                   QT[slot].sl(q0 + c0, q0 + 512, pb, pb + 64), True, False, [KT[slot].buf, QT[slot].buf])
                if diag:
                    mm(ai, psl(ai, c0, c0 + 128), IDB, NEGM, False, False, [CB.buf])
                mm(ai, psl(ai, c0, 512), LTRI, SPn.sl(c0, 512), False, i == 0, [CB.buf, SPn.buf])
                if i > 0:
                    mm(ai, psl(ai, c0, 512), NEG1, acc.sl(c0, 512), False, True, [CB.buf, acc.buf])
                act(Wn.sl(c0, 512), psl(ai, c0, 512), AF.Exp, [PB[ai]], [Wn.buf])
                if kb > 0:
                    tt("pool", acc.sl(c0, 512), acc.sl(c0, 512), SPn.sl(c0, 512), ADD, [SPn.buf, acc.buf], [acc.buf])

            def sb_stage3(n):
                pc, slot, hh, tq, i, kb = blocks[n]
                pb = 64 * hh
                c0 = 128 * max(0, kb - 4 * tq)
                oi = 4 + tq % 2
                Wn = Wt[n % 3]
                mm(oi, psl(oi, c0, 512, 0, 64), VP[slot].ap(kb * 132 + pb, [[1, 64]]), Wn.sl(c0, 512),
                   i == 0, kb == 0, [VP[slot].buf, Wn.buf], skip=True)
                if kb == 0:
                    if tq == 0:
                        cur_ost[(pc, hh)] = OST[ost_i[0] % 2]
                        ost_i[0] += 1
                    ost = cur_ost[(pc, hh)]
                    cp("dve", ost.sl(tq * 512, (tq + 1) * 512, 0, 64), psl(oi, 0, 512, 0, 64), [PB[oi]], [ost.buf])
                    if tq == NTT - 1:
                        dma(dap(mixT_d, pc * 128 * S + pb * S, [[S, 64], [1, S]]), ost.sl(0, 4096, 0, 64),
                            reads=[ost.buf])

            for step in range(nb + 2 if debug != "Bdf" else 0):
                if step < nb:
                    sb_stage1(step)
                if 0 <= step - 1 < nb:
                    sb_stage2(step - 1)
                if 0 <= step - 2 < nb:
                    sb_stage3(step - 2)

            load_qkv(qTd_d, kTd_d, vd_d, 0, 0)
            srot = [0]
            grp = [0]
            for h in range(4 if debug != "Bsb" else 0):
                slot = h % 2
                if h + 1 < 4:
                    load_qkv(qTd_d, kTd_d, vd_d, h + 1, (h + 1) % 2)
                dst = OST[ost_i[0] % 2]
                ost_i[0] += 1
                for tq in range(NTT):
                    ob = 2 + 3 * (grp[0] % 2)
                    grp[0] += 1
                    q0 = tq * 512
                    for b3 in range(3):
                        ms("dve", psl(ob + b3, 0, 512), 0.0, [], [PB[ob + b3]])
                    for m in range(2):
                        mb = 64 * m
                        for kb in range(4 * tq + 4):
                            c0 = 128 * max(0, kb - 4 * tq)
                            si = srot[0] % 2
                            Pm = Wt[srot[0] % 3]
                            srot[0] += 1
                            mm(si, psl(si, c0, 512), KT[slot].sl(kb * 128, (kb + 1) * 128, mb, mb + 64),
                               QT[slot].sl(q0 + c0, q0 + 512, mb, mb + 64), True, True,
                               [KT[slot].buf, QT[slot].buf])
                            act(Pm.sl(c0, 512), psl(si, c0, 512), AF.Exp, [PB[si]], [Pm.buf])
                            if kb >= 4 * tq:
                                ms("pool", Pm.sl(c0, c0 + 64, 64, 128), 0.0, [Pm.buf], [Pm.buf])
                            for j in range(c0 // 128, 4):
                                a = m * 4 + j
                                bi = ob + a // 3
                                co = (a % 3) * 132
                                mm(bi, psl(bi, co, co + 129), Pm.sl(j * 128, (j + 1) * 128),
                                   VP[slot].ap(kb * 132, [[1, 129]]), False, False, [Pm.buf, VP[slot].buf], skip=True)
                    den = DEN[tq % 2]
                    ssq = SSQ[tq % 2]
                    dj = DJ[tq % 2]
                    for a in range(8):
                        bi = ob + a // 3
                        co = (a % 3) * 132
                        cp("dve", den.sl(a, a + 1), psl(bi, co + 128, co + 129), [PB[bi]], [den.buf])
                    P.op("dve", (lambda o_, i_: (lambda e: e.reciprocal(out=o_, in_=i_)))(den.sl(0, 8), den.sl(0, 8)),
                         reads=[den.buf], writes=[den.buf])
                    ts("dve", den.sl(4, 8), den.sl(4, 8), neglam.sl(l, l + 1), None, MUL, None,
                       [den.buf, neglam.buf], [den.buf])
                    ms("pool", ssq.sl(0, 4), 0.0, [], [ssq.buf])
                    for j in range(4):
                        a0, a1 = j, 4 + j
                        b0, c0_ = ob + a0 // 3, (a0 % 3) * 132
                        b1, c1_ = ob + a1 // 3, (a1 % 3) * 132
                        djj = dj.sl(j * 128, (j + 1) * 128)
                        ts("dve", djj, psl(b0, c0_, c0_ + 128), den.sl(j, j + 1), None, MUL, None,
                           [PB[b0], den.buf], [dj.buf])
                        stt("dve", djj, psl(b1, c1_, c1_ + 128), den.sl(4 + j, 5 + j), djj, MUL, ADD,
                            [PB[b1], den.buf, dj.buf], [dj.buf])
                        act(JUNK.sl(0, 128), djj, AF.Square, [dj.buf], [JUNK.buf, ssq.buf], accum=ssq.sl(j, j + 1))
                    act(ssq.sl(0, 4), ssq.sl(0, 4), AF.Ln, [ssq.buf, epst.buf], [ssq.buf], scale=1.0 / 128, bias=EPSB)
                    act(ssq.sl(0, 4), ssq.sl(0, 4), AF.Exp, [ssq.buf], [ssq.buf], scale=-0.5)
                    for j in range(4):
                        djj = dj.sl(j * 128, (j + 1) * 128)
                        ts("dve", djj, djj, ssq.sl(j, j + 1), None, MUL, None, [dj.buf, ssq.buf], [dj.buf])
                    si = srot[0] % 2
                    srot[0] += 1
                    for j in range(4):
                        tr(si, psl(si, j * 128, (j + 1) * 128), dj.sl(j * 128, (j + 1) * 128), [dj.buf])
                    cp("dve", dst.sl(tq * 512, (tq + 1) * 512), psl(si, 0, 512), [PB[si]], [dst.buf])
                dma(dap(mixT_d, (4 + h) * 128 * S, [[S, 128], [1, S]]), dst.sl(0, 4096), reads=[dst.buf])
            if debug and debug.startswith("B"):
                break

            P.barrier()
            ABF.reset(); AF32.reset()
            WO = ABF.tile(8192, "wo")
            MT = [ABF.tile(4096, "mt%d" % i) for i in range(2)]
            HT2 = ABF.tile(4096, "ht2")
            W1G = [ABF.tile(4096, "w1g%d" % i) for i in range(2)]
            W2G = [ABF.tile(4096, "w2g%d" % i) for i in range(2)]
            Rt = [ABF.tile(512, "r%d" % i) for i in range(2)]
            UT = ABF.tile(16384, "ut")
            SQ = ABF.tile(4096, "sq")
            HTn = ABF.tile(4096, "htn")
            XTc = [AF32.tile(4096, "xtc%d" % i) for i in range(2)]
            RS = AF32.tile(512, "rs")
            YT = AF32.tile(4096, "yt")
            OUTS = [AF32.tile(1024, "outs%d" % i) for i in range(2)]
            last = (l == L - 1)
            for hf in range(2):
                dma(WO.sl(hf * 4096, (hf + 1) * 4096),
                    dap(wbo, (l * 2 + hf) * 128 * 4096, [[4096, 128], [1, 4096]]), writes=[WO.buf], parallel=(hf > 0))
            load_tile8(MT[0], mixT_d, 0)
            load_tile8(XTc[0], xT_d, 0)
            prot = [0]
            w1i = [0]
            w2i = [0]

            def load_w1(fg):
                w = W1G[w1i[0] % 2]
                w1i[0] += 1
                dma(w.sl(0, 4096), dap(wb1, (l * 8 + fg) * 128 * 4096, [[4096, 128], [1, 4096]]), writes=[w.buf])
                return w

            def load_w2(dc):
                w = W2G[w2i[0] % 2]
                w2i[0] += 1
                dma(w.sl(0, 4096), dap(wb2, (l * 8 + dc) * 128 * 4096, [[4096, 128], [1, 4096]]), writes=[w.buf])
                return w

            for tt_ in range(NTT):
                XT = XTc[tt_ % 2]
                MTt = MT[tt_ % 2]
                if tt_ + 1 < NTT:
                    load_tile8(MT[(tt_ + 1) % 2], mixT_d, tt_ + 1)
                    load_tile8(XTc[(tt_ + 1) % 2], xT_d, tt_ + 1)
                w1n = load_w1(0)
                for dc in range(8):
                    pi = prot[0] % 4
                    prot[0] += 1
                    for ec in range(8):
                        mm(pi, psl(pi, 0, 512), WO.sl(ec * 1024 + dc * 128, ec * 1024 + (dc + 1) * 128),
                           MTt.sl(ec * 512, (ec + 1) * 512), ec == 0, ec == 7, [WO.buf, MTt.buf])
                    tt("dve", XT.sl(dc * 512, (dc + 1) * 512), psl(pi, 0, 512), XT.sl(dc * 512, (dc + 1) * 512), ADD,
                       [PB[pi], XT.buf], [XT.buf])
                norm_stats(XT, SQ, RS, 4)
                norm_apply(XT, RS, HT2, "pool")
                for fg in range(8):
                    w1 = w1n
                    if fg + 1 < 8:
                        w1n = load_w1(fg + 1)
                    else:
                        w2n = load_w2(0)
                    for fi in range(4):
                        fc = fg * 4 + fi
                        pi = prot[0] % 4
                        prot[0] += 1
                        for dmc in range(8):
                            mm(pi, psl(pi, 0, 512), w1.sl(dmc * 512 + fi * 128, dmc * 512 + (fi + 1) * 128),
                               HT2.sl(dmc * 512, (dmc + 1) * 512), dmc == 0, dmc == 7, [w1.buf, HT2.buf])
                        R = Rt[fc % 2]
                        act(R.sl(0, 512), psl(pi, 0, 512), AF.Relu, [PB[pi]], [R.buf])
                        tt("pool", UT.sl(fc * 512, (fc + 1) * 512), R.sl(0, 512), R.sl(0, 512), MUL, [R.buf], [UT.buf])
                for dc in range(8):
                    w2 = w2n
                    if dc + 1 < 8:
                        w2n = load_w2(dc + 1)
                    pi = prot[0] % 4
                    prot[0] += 1
                    for fc in range(32):
                        mm(pi, psl(pi, 0, 512), w2.sl(fc * 128, (fc + 1) * 128), UT.sl(fc * 512, (fc + 1) * 512),
                           fc == 0, fc == 31, [w2.buf, UT.buf])
                    tt("dve", XT.sl(dc * 512, (dc + 1) * 512), psl(pi, 0, 512), XT.sl(dc * 512, (dc + 1) * 512), ADD,
                       [PB[pi], XT.buf], [XT.buf])
                norm_stats(XT, SQ, RS, 4)
                if not last:
                    store_tile8(XT, xT_d, tt_)
                    norm_apply(XT, RS, HTn, "pool")
                    store_tile8(HTn, hT_d, tt_)
                else:
                    if debug:
                        store_tile8(XT, xT_d, tt_)
                    for c in range(8):
                        stt("dve", YT.sl(c * 512, (c + 1) * 512), XT.sl(c * 512, (c + 1) * 512), VEC.sl(64 + c, 65 + c),
                            RS.sl(0, 512), MUL, MUL, [XT.buf, VEC.buf, RS.buf], [YT.buf])
                    for j in range(4):
                        tb = tt_ * 4 + j
                        outs = OUTS[tb % 2]
                        for hb in range(2):
                            pi = 5 + (tb * 2 + hb) % 3
                            for c4 in range(4):
                                c = hb * 4 + c4
                                tr(pi, psl(pi, c4 * 128, (c4 + 1) * 128),
                                   YT.sl(c * 512 + j * 128, c * 512 + (j + 1) * 128), [YT.buf])
                            cp("dve", outs.sl(hb * 512, (hb + 1) * 512), psl(pi, 0, 512), [PB[pi]], [outs.buf])
                        dma(out_d.ap()[tb * 128:(tb + 1) * 128, :], outs.sl(0, 1024), reads=[outs.buf])
            if debug == "C":
                break
        P.emit()
        nc._prog_stats = {e: len(P.ops[e]) for e in ENGS}
    return nc


def host_layout(w_in, attn_norm, subln_norm, lam_q1, lam_k1, lam_q2, lam_k2, mlp_norm, final_norm):
    f32 = np.float32
    perm = np.concatenate([(np.arange(64) + 32) % 64 + 64 * g for g in range(8)])
    sbq, sbk, sbv = w_in[:, :, 0:512], w_in[:, :, 512:1024], w_in[:, :, 1024:1536]
    dfq, dfk, dfv = w_in[:, :, 1536:2048], w_in[:, :, 2048:2560], w_in[:, :, 2560:3072]
    w_ext = np.ascontiguousarray(np.concatenate(
        [sbq, sbk, dfq, dfq[:, :, perm], dfk, dfk[:, :, perm], sbv, dfv], axis=2).astype(f32))
    vecs = np.zeros((128, 76 + 1024), f32)
    vecs[:, 0:32] = attn_norm.reshape(L, 8, 128).transpose(2, 0, 1).reshape(128, 32)
    vecs[:, 32:64] = mlp_norm.reshape(L, 8, 128).transpose(2, 0, 1).reshape(128, 32)
    vecs[:, 64:72] = final_norm.reshape(8, 128).T
    vecs[:, 72:76] = subln_norm.T
    lam = np.stack([lam_q1, lam_k1, lam_q2, lam_k2], 0).reshape(-1)
    vecs[:, 76:] = np.broadcast_to(lam[None, :], (128, 1024))
    consts = np.zeros((128, 640), f32)
    r = np.arange(128)
    consts[:, 0:128] = np.eye(128, dtype=f32)
    consts[:, 128:256] = np.where(r[:, None] >= r[None, :], NEG, 0.0)
    consts[:, 256:384] = np.where(r[:, None] >= r[None, :], -1.0, 0.0)
    consts[:, 384:512] = -1.0
    consts[:, 512:640] = 1.0
    inv = 1.0 / (10000.0 ** (np.arange(0, 64, 2, dtype=np.float32) / 64))
    ang = np.arange(S, dtype=np.float32)[:, None] * inv[None, :]
    ang = np.concatenate([ang, ang], -1)
    cos = np.cos(ang).T.astype(f32)
    sin = np.sin(ang).T.astype(f32)
    sin_s = np.concatenate([-sin[:32], sin[32:]], 0)
    rope = np.stack([np.concatenate([cos, cos], 0), np.concatenate([sin_s, sin_s], 0)], 0)
    return w_ext, vecs, consts, np.ascontiguousarray(rope.astype(f32))


_NC_CACHE = {}


def kernel(x, w_in, w_o, attn_norm, subln_norm, lam_q1, lam_k1, lam_q2, lam_k2, mlp_norm, w_ff1, w_ff2,
           final_norm):
    x = np.asarray(x, np.float32)
    w_ext, vecs, consts, rope = host_layout(np.asarray(w_in, np.float32), np.asarray(attn_norm, np.float32),
                                            np.asarray(subln_norm, np.float32), np.asarray(lam_q1, np.float32),
                                            np.asarray(lam_k1, np.float32), np.asarray(lam_q2, np.float32),
                                            np.asarray(lam_k2, np.float32), np.asarray(mlp_norm, np.float32),
                                            np.asarray(final_norm, np.float32))
    if "nc" not in _NC_CACHE:
        _NC_CACHE["nc"] = build()
    nc = _NC_CACHE["nc"]
    shared = {"w_in": w_ext, "w_o": np.ascontiguousarray(np.asarray(w_o, np.float32)),
              "w_ff1": np.ascontiguousarray(np.asarray(w_ff1, np.float32)),
              "w_ff2": np.ascontiguousarray(np.asarray(w_ff2, np.float32)),
              "vecs": vecs, "consts": consts, "rope": rope}
    in_maps = [dict(shared, x=np.ascontiguousarray(x[c])) for c in range(8)]
    res = run_bass_kernel_spmd(nc, in_maps, core_ids=list(range(8)))
    return np.stack([np.asarray(r["out"], np.float32) for r in res.results], 0)
```

```python
import math
from contextlib import ExitStack

import numpy as np
import concourse.bass as bass
import concourse.mybir as mybir
from concourse.bass_utils import run_bass_kernel_spmd

F32 = mybir.dt.float32
BF16 = mybir.dt.bfloat16
AF = mybir.ActivationFunctionType
ALU = mybir.AluOpType
AX = mybir.AxisListType

S = 4096
D = 1024
L = 4
NTT = 8
EPS = 1e-6
NEG = -30000.0
ENGS = ("pe", "act", "dve", "pool", "sp")
DMA_INC = 16


class Buf:
    __slots__ = ("name", "writers", "readers", "sem", "cnt", "gen_deps")

    def __init__(self, name=""):
        self.name = name
        self.writers = []
        self.readers = []
        self.sem = None
        self.cnt = 0
        self.gen_deps = []


class Op:
    __slots__ = ("eng", "fn", "deps", "sig", "dma", "count")

    def __init__(self, eng, fn):
        self.eng = eng
        self.fn = fn
        self.deps = []
        self.sig = False
        self.dma = None
        self.count = None


class Prog:
    def __init__(self, nc, n_dma_sems=160):
        self.nc = nc
        self.ops = {e: [] for e in ENGS}
        self.n_dma_sems = n_dma_sems
        self.dma_sem_next = 0
        self.dma_sem_max = 0
        self.dma_last = {}
        self.pending = {e: [] for e in ENGS}

    def op(self, eng, fn, reads=(), writes=()):
        o = Op(eng, fn)
        idx = len(self.ops[eng])
        deps = []
        for b in reads:
            deps.extend(b.writers)
        for b in writes:
            deps.extend(b.writers)
            deps.extend(b.readers)
        deps.extend(self.pending[eng])
        self.pending[eng] = []
        ev = ("e", eng, idx)
        o.deps = [d for d in deps if not (d[0] == "e" and d[1] == "pe" and eng == "pe")]
        self.ops[eng].append(o)
        for b in reads:
            b.readers.append(ev)
        for b in writes:
            b.gen_deps = b.writers + b.readers
            b.writers = [ev]
            b.readers = []
        return ev

    def dma(self, eng, fn, reads=(), writes=(), sem_buf=None, parallel=False):
        o = Op(eng, fn)
        if sem_buf is None:
            sem_buf = writes[0] if writes else reads[0]
        if sem_buf.sem is None:
            sem_buf.sem = self.dma_sem_next
            sem_buf.cnt = self.dma_last.get(sem_buf.sem, 0)
            self.dma_sem_next += 1
            self.dma_sem_max = max(self.dma_sem_max, self.dma_sem_next)
            assert self.dma_sem_next <= self.n_dma_sems, "out of dma sems"
        sem_buf.cnt += DMA_INC
        ev = ("d", sem_buf.sem, sem_buf.cnt)
        self.dma_last[sem_buf.sem] = sem_buf.cnt
        deps = []
        for b in reads:
            deps.extend(b.writers)
        for b in writes:
            if parallel and b.writers and all(w[0] == "d" for w in b.writers) and not b.readers:
                deps.extend(b.gen_deps)
            else:
                deps.extend(b.writers)
                deps.extend(b.readers)
        deps.extend(self.pending[eng])
        self.pending[eng] = []
        o.deps = deps
        o.dma = sem_buf.sem
        self.ops[eng].append(o)
        for b in reads:
            b.readers.append(ev)
        for b in writes:
            if parallel and b.writers and all(w[0] == "d" for w in b.writers) and not b.readers:
                b.writers = b.writers + [ev]
            else:
                b.gen_deps = b.writers + b.readers
                b.writers = [ev]
                b.readers = []
        return ev

    def barrier(self):
        evs = []
        for e in ENGS:
            for i in range(len(self.ops[e]) - 1, -1, -1):
                if self.ops[e][i].dma is None:
                    evs.append(("e", e, i))
                    break
        for s, v in self.dma_last.items():
            evs.append(("d", s, v))
        for e in ENGS:
            self.pending[e].extend(evs)
        self.dma_sem_next = 0

    def emit(self):
        nc = self.nc
        for e in ENGS:
            for o in self.ops[e]:
                for d in o.deps:
                    if d[0] == "e":
                        self.ops[d[1]][d[2]].sig = True
        for e in ENGS:
            c = 0
            for o in self.ops[e]:
                if o.dma is None and o.sig:
                    c += 1
                    o.count = c
        with ExitStack() as st:
            esem = {e: st.enter_context(nc.semaphore("s_" + e)) for e in ENGS}
            dsem = [st.enter_context(nc.semaphore("d%d" % i)) for i in range(self.dma_sem_max)]
            block = st.enter_context(nc.Block())
            handles = {"pe": block.tensor, "act": block.scalar, "dve": block.vector,
                       "pool": block.gpsimd, "sp": block.sync}

            def make(e):
                def body(eng):
                    waited = {}
                    for o in self.ops[e]:
                        need = {}
                        for d in o.deps:
                            if d[0] == "e":
                                key = ("e", d[1])
                                val = self.ops[d[1]][d[2]].count
                            else:
                                key = ("d", d[1])
                                val = d[2]
                            if val > need.get(key, 0):
                                need[key] = val
                        for key, val in need.items():
                            if waited.get(key, 0) >= val:
                                continue
                            waited[key] = val
                            sem = esem[key[1]] if key[0] == "e" else dsem[key[1]]
                            eng.wait_ge(sem, val)
                        ins = o.fn(eng)
                        if o.dma is not None:
                            ins.then_inc(dsem[o.dma], DMA_INC)
                        elif o.sig:
                            ins.then_inc(esem[e], 1)
                    if e == "sp":
                        for s, v in self.dma_last.items():
                            if waited.get(("d", s), 0) < v:
                                eng.wait_ge(dsem[s], v)
                return body

            for e in ENGS:
                handles[e](make(e))


class Tile:
    def __init__(self, t, N, off, n, name=""):
        self.t, self.N, self.off, self.n = t, N, off, n
        self.buf = Buf(name)

    def ap(self, o, dims, p0=0, npart=128):
        return bass.AP(self.t, p0 * self.N + self.off + o, [[self.N, npart]] + [list(d) for d in dims])

    def sl(self, a, b, p0=0, p1=128):
        return self.t[p0:p1, self.off + a:self.off + b]


class Arena:
    def __init__(self, t, N):
        self.t, self.N, self.off = t, N, 0

    def reset(self):
        self.off = 0

    def tile(self, n, name=""):
        o = self.off
        self.off += n
        assert self.off <= self.N, ("arena overflow", name, self.off, self.N)
        return Tile(self.t, self.N, o, n, name)


def lam_init(l):
    return 0.8 - 0.6 * math.exp(-0.3 * l)


def build(n_layers=L, debug=False):
    nc = bass.Bass("TRN2", target_bir_lowering=False)
    dt_ = nc.dram_tensor
    x_d = dt_("x", [S, D], F32, kind="ExternalInput")
    win_d = dt_("w_in", [L, D, 4096], F32, kind="ExternalInput")
    wo_d = dt_("w_o", [L, D, D], F32, kind="ExternalInput")
    w1_d = dt_("w_ff1", [L, D, 4096], F32, kind="ExternalInput")
    w2_d = dt_("w_ff2", [L, 4096, D], F32, kind="ExternalInput")
    NV = 76 + 1024
    vec_d = dt_("vecs", [128, NV], F32, kind="ExternalInput")
    cst_d = dt_("consts", [128, 640], F32, kind="ExternalInput")
    rope_d = dt_("rope", [2, 128, S], F32, kind="ExternalInput")
    out_d = dt_("out", [S, D], F32, kind="ExternalOutput")
    sk = "ExternalOutput" if debug else "Internal"
    wbin = dt_("wbin", [L * 8, 128, 4096], BF16)
    wbo = dt_("wbo", [L * 2, 128, 4096], BF16)
    wb1 = dt_("wb1", [L * 8, 128, 4096], BF16)
    wb2 = dt_("wb2", [L * 8, 128, 4096], BF16)
    xT_d = dt_("xT", [8, 128, S], F32, kind=sk)
    hT_d = dt_("hT", [8, 128, S], BF16, kind=sk)
    qTs_d = dt_("qTs", [4, 128, S], BF16, kind=sk)
    kTs_d = dt_("kTs", [4, 128, S], BF16, kind=sk)
    qTd_d = dt_("qTd", [4, 128, S], BF16, kind=sk)
    kTd_d = dt_("kTd", [4, 128, S], BF16, kind=sk)
    vs_d = dt_("vs", [32, 128, 512], BF16, kind=sk)
    vd_d = dt_("vd", [32, 128, 512], BF16, kind=sk)
    mixT_d = dt_("mixT", [8, 128, S], BF16, kind=sk)

    def dap(h, off, dims):
        return bass.AP(h, off, [list(d) for d in dims])

    NBF = 63488
    NF32 = 15360
    with ExitStack() as st:
        abf_t = st.enter_context(nc.sbuf_tensor("abf", [128, NBF], BF16))
        af_t = st.enter_context(nc.sbuf_tensor("af32", [128, NF32], F32))
        cb_t = st.enter_context(nc.sbuf_tensor("cb", [128, 640], BF16))
        cf_t = st.enter_context(nc.sbuf_tensor("cf", [128, 640], F32))
        vec_t = st.enter_context(nc.sbuf_tensor("vec", [128, NV], F32))
        sm_t = st.enter_context(nc.sbuf_tensor("small", [128, 512], F32))
        psb = [st.enter_context(nc.psum_tensor("ps%d" % i, [128, 512], F32)) for i in range(8)]
        PB = [Buf("ps%d" % i) for i in range(8)]
        P = Prog(nc)
        ABF = Arena(abf_t, NBF)
        AF32 = Arena(af_t, NF32)
        CB = Tile(cb_t, 640, 0, 640, "cb")
        CF = Tile(cf_t, 640, 0, 640, "cf")
        VEC = Tile(vec_t, NV, 0, NV, "vec")
        SM = Arena(sm_t, 512)
        sfold = SM.tile(4, "sfold")
        neglam = SM.tile(4, "neglam")
        lamt = SM.tile(8, "lamt")
        epst = SM.tile(1, "epst")

        IDF = CF.sl(0, 128)
        IDB = CB.sl(0, 128)
        NEGM = CB.sl(128, 256)
        LTRI = CB.sl(256, 384)
        NEG1 = CB.sl(384, 512)
        ONES = CB.sl(512, 640)

        def psl(i, a, b, p0=0, p1=128):
            return psb[i][p0:p1, a:b]

        EPSB = epst.sl(0, 1)
        P.op("pool", lambda e: e.memset(EPSB, EPS), writes=[epst.buf])

        def mm(pi, out, lhsT, rhs, start, stop, reads, skip=False):
            if skip:
                P.op("pe", lambda e: e.matmul(out=out, lhsT=lhsT, rhs=rhs, start=start, stop=stop,
                                              skip_group_check=True), reads=reads, writes=[PB[pi]])
            else:
                P.op("pe", lambda e: e.matmul(out=out, lhsT=lhsT, rhs=rhs, start=start, stop=stop),
                     reads=reads, writes=[PB[pi]])

        def tr(pi, out, in_, reads):
            P.op("pe", lambda e: e.transpose(out=out, in_=in_, identity=IDF), reads=list(reads) + [CF.buf],
                 writes=[PB[pi]])

        def act(out, in_, func, reads, writes, scale=1.0, bias=0.0, accum=None):
            if accum is None:
                P.op("act", lambda e: e.activation(out=out, in_=in_, func=func, bias=bias, scale=scale),
                     reads=reads, writes=writes)
            else:
                P.op("act", lambda e: e.activation(out=out, in_=in_, func=func, bias=bias, scale=scale,
                                                   accum_out=accum), reads=reads, writes=writes)

        def tt(eng, out, in0, in1, op, reads, writes):
            P.op(eng, lambda e: e.tensor_tensor(out=out, in0=in0, in1=in1, op=op), reads=reads, writes=writes)

        def ts(eng, out, in0, s1, s2, op0, op1, reads, writes):
            if s2 is None:
                P.op(eng, lambda e: e.tensor_scalar(out=out, in0=in0, scalar1=s1, scalar2=None, op0=op0),
                     reads=reads, writes=writes)
            else:
                P.op(eng, lambda e: e.tensor_scalar(out=out, in0=in0, scalar1=s1, scalar2=s2, op0=op0, op1=op1),
                     reads=reads, writes=writes)

        def stt(eng, out, in0, scalar, in1, op0, op1, reads, writes):
            P.op(eng, lambda e: e.scalar_tensor_tensor(out=out, in0=in0, scalar=scalar, in1=in1, op0=op0, op1=op1),
                 reads=reads, writes=writes)

        def cp(eng, out, in_, reads, writes):
            P.op(eng, lambda e: e.tensor_copy(out=out, in_=in_), reads=reads, writes=writes)

        def ms(eng, ap_, val, reads, writes):
            P.op(eng, lambda e: e.memset(ap_, val), reads=reads, writes=writes)

        def dma(out, in_, reads=(), writes=(), parallel=False):
            P.dma("sp", lambda e: e.dma_start(out=out, in_=in_), reads=list(reads), writes=list(writes),
                  parallel=parallel)

        MUL, ADD, SUB = ALU.mult, ALU.add, ALU.subtract

        dma(cf_t[:, :], cst_d.ap()[:, :], writes=[CF.buf])
        dma(vec_t[:, :], vec_d.ap()[:, :], writes=[VEC.buf])
        cp("dve", cb_t[:, :], cf_t[:, :], [CF.buf], [CB.buf])
        for l in range(L):
            ts("pool", sfold.sl(l, l + 1), VEC.sl(72 + l, 73 + l), 1.0 - lam_init(l), None, MUL, None,
               [VEC.buf], [sfold.buf])
        lprod = AF32.tile(512, "lprod")
        tt("dve", lprod.sl(0, 256), VEC.sl(76, 76 + 256), VEC.sl(76 + 256, 76 + 512), MUL, [VEC.buf], [lprod.buf])
        tt("dve", lprod.sl(256, 512), VEC.sl(76 + 512, 76 + 768), VEC.sl(76 + 768, 76 + 1024), MUL, [VEC.buf],
           [lprod.buf])
        lp_in = lprod.ap(0, [[64, 8], [1, 64]])
        lp_out = lamt.sl(0, 8)
        P.op("dve", lambda e: e.tensor_reduce(out=lp_out, in_=lp_in, axis=AX.X, op=ADD),
             reads=[lprod.buf], writes=[lamt.buf])
        act(lamt.sl(0, 8), lamt.sl(0, 8), AF.Exp, [lamt.buf], [lamt.buf])
        tt("dve", lamt.sl(0, 4), lamt.sl(0, 4), lamt.sl(4, 8), SUB, [lamt.buf], [lamt.buf])
        for l in range(L):
            ts("dve", neglam.sl(l, l + 1), lamt.sl(l, l + 1), lam_init(l), -1.0, ADD, MUL, [lamt.buf], [neglam.buf])

        P.barrier()
        ABF.reset(); AF32.reset()
        WS = [AF32.tile(4096, "ws%d" % i) for i in range(2)]
        WB = [ABF.tile(4096, "wb%d" % i) for i in range(2)]
        prep = []
        for l in range(n_layers):
            prep += [("in", l, g) for g in range(8)] + [("o", l, h) for h in range(2)]
            prep += [("f1", l, g) for g in range(8)] + [("f2", l, g) for g in range(8)]
        for i, (kind, l, g) in enumerate(prep):
            ws, wb = WS[i % 2], WB[i % 2]
            ce = "dve" if i % 2 == 0 else "pool"
            if kind in ("in", "f1"):
                src_h = win_d if kind == "in" else w1_d
                dma(ws.ap(0, [[512, 8], [1, 512]]),
                    dap(src_h, l * D * 4096 + g * 512, [[4096, 128], [128 * 4096, 8], [1, 512]]), writes=[ws.buf])
                gcol = (0 if kind == "in" else 32) + l * 8
                tt(ce, wb.ap(0, [[512, 8], [1, 512]]), ws.ap(0, [[512, 8], [1, 512]]),
                   VEC.ap(gcol, [[1, 8], [0, 512]]), MUL, [ws.buf, VEC.buf], [wb.buf])
                dst = dap(wbin if kind == "in" else wb1, (l * 8 + g) * 128 * 4096, [[4096, 128], [1, 4096]])
            elif kind == "f2":
                for q4 in range(4):
                    dma(ws.ap(q4 * 8 * 128, [[128, 8], [1, 128]]),
                        dap(w2_d, l * 4096 * D + q4 * 8 * 128 * D + g * 128, [[D, 128], [128 * D, 8], [1, 128]]),
                        writes=[ws.buf], parallel=(q4 > 0))
                cp(ce, wb.sl(0, 4096), ws.sl(0, 4096), [ws.buf], [wb.buf])
                dst = dap(wb2, (l * 8 + g) * 128 * 4096, [[4096, 128], [1, 4096]])
            else:
                dma(ws.ap(0, [[1024, 4], [1, 1024]]),
                    dap(wo_d, l * D * D + g * 512 * D, [[D, 128], [128 * D, 4], [1, 1024]]), writes=[ws.buf])
                if g == 0:
                    cp(ce, wb.sl(0, 4096), ws.sl(0, 4096), [ws.buf], [wb.buf])
                else:
                    ts(ce, wb.sl(0, 4096), ws.sl(0, 4096), sfold.sl(l, l + 1), None, MUL, None,
                       [ws.buf, sfold.buf], [wb.buf])
                dst = dap(wbo, (l * 2 + g) * 128 * 4096, [[4096, 128], [1, 4096]])
            dma(dst, wb.sl(0, 4096), reads=[wb.buf])

        def norm_stats(XT, SQ, RS, pi):
            act(SQ.sl(0, 4096), XT.sl(0, 4096), AF.Square, [XT.buf], [SQ.buf])
            for c in range(8):
                mm(pi, psl(pi, 0, 512), ONES, SQ.sl(c * 512, (c + 1) * 512), c == 0, c == 7, [SQ.buf, CB.buf])
            act(RS.sl(0, 512), psl(pi, 0, 512), AF.Ln, [PB[pi], epst.buf], [RS.buf], scale=1.0 / D, bias=EPSB)
            act(RS.sl(0, 512), RS.sl(0, 512), AF.Exp, [RS.buf], [RS.buf], scale=-0.5)

        def norm_apply(XT, RS, HT, eng):
            tt(eng, HT.ap(0, [[512, 8], [1, 512]]), XT.ap(0, [[512, 8], [1, 512]]), RS.ap(0, [[0, 8], [1, 512]]),
               MUL, [XT.buf, RS.buf], [HT.buf])

        def store_tile8(T_, dram_h, tt_):
            dma(dap(dram_h, tt_ * 512, [[S, 128], [128 * S, 8], [1, 512]]), T_.ap(0, [[512, 8], [1, 512]]),
                reads=[T_.buf])

        def load_tile8(T_, dram_h, tt_):
            dma(T_.ap(0, [[512, 8], [1, 512]]), dap(dram_h, tt_ * 512, [[S, 128], [128 * S, 8], [1, 512]]),
                writes=[T_.buf])

        P.barrier()
        ABF.reset(); AF32.reset()
        XIN = [AF32.tile(1024, "xin%d" % i) for i in range(2)]
        XTt = [AF32.tile(4096, "xt%d" % i) for i in range(2)]
        RSt = [AF32.tile(512, "rs%d" % i) for i in range(2)]
        SQt = [ABF.tile(4096, "sq%d" % i) for i in range(2)]
        HTt = [ABF.tile(4096, "ht%d" % i) for i in range(2)]
        for tt_ in range(NTT):
            XT = XTt[tt_ % 2]
            for j in range(4):
                tb = tt_ * 4 + j
                xin = XIN[tb % 2]
                dma(xin.sl(0, 1024), x_d.ap()[tb * 128:(tb + 1) * 128, :], writes=[xin.buf])
                for hb in range(2):
                    pi = (tb * 2 + hb) % 4
                    for c4 in range(4):
                        c = hb * 4 + c4
                        tr(pi, psl(pi, c4 * 128, (c4 + 1) * 128), xin.sl(c * 128, (c + 1) * 128), [xin.buf])
                    cp("dve", XT.ap(hb * 4 * 512 + j * 128, [[512, 4], [1, 128]]),
                       bass.AP(psb[pi], 0, [[512, 128], [128, 4], [1, 128]]), [PB[pi]], [XT.buf])
            store_tile8(XT, xT_d, tt_)
            norm_stats(XT, SQt[tt_ % 2], RSt[tt_ % 2], 4 + tt_ % 2)
            norm_apply(XT, RSt[tt_ % 2], HTt[tt_ % 2], "pool")
            store_tile8(HTt[tt_ % 2], hT_d, tt_)

        for l in range(n_layers):
            P.barrier()
            ABF.reset(); AF32.reset()
            HTR = ABF.tile(32768, "htr")
            WG = [ABF.tile(4096, "wg%d" % i) for i in range(3)]
            STG = [ABF.tile(4096, "stg%d" % i) for i in range(2)]
            VSTG = [ABF.tile(2048, "vstg%d" % i) for i in range(2)]
            COS = AF32.tile(4096, "cos")
            SIN = AF32.tile(4096, "sin")
            T1 = [AF32.tile(512, "t1_%d" % i) for i in range(2)]
            T2 = [AF32.tile(512, "t2_%d" % i) for i in range(2)]
            for c in range(8):
                dma(HTR.sl(c * 4096, (c + 1) * 4096), dap(hT_d, c * 128 * S, [[S, 128], [1, S]]),
                    writes=[HTR.buf], parallel=(c > 0))
            dma(COS.sl(0, 4096), dap(rope_d, 0, [[S, 128], [1, S]]), writes=[COS.buf])
            dma(SIN.sl(0, 4096), dap(rope_d, 128 * S, [[S, 128], [1, S]]), writes=[SIN.buf])
            wg_i = [0]

            def load_wg(g):
                w = WG[wg_i[0] % 3]
                wg_i[0] += 1
                dma(w.sl(0, 4096), dap(wbin, (l * 8 + g) * 128 * 4096, [[4096, 128], [1, 4096]]), writes=[w.buf])
                return w

            psr = [0]
            stg_i = [0]
            for g, dst_h, sc in ((0, qTs_d, 0.125), (1, kTs_d, 1.0)):
                w = load_wg(g)
                for ci in range(4):
                    stg = STG[stg_i[0] % 2]
                    stg_i[0] += 1
                    for tt_ in range(NTT):
                        pi = psr[0] % 4
                        psr[0] += 1
                        for dmc in range(8):
                            mm(pi, psl(pi, 0, 512), w.sl(dmc * 512 + ci * 128, dmc * 512 + (ci + 1) * 128),
                               HTR.sl(dmc * 4096 + tt_ * 512, dmc * 4096 + (tt_ + 1) * 512), dmc == 0, dmc == 7,
                               [w.buf, HTR.buf])
                        act(stg.sl(tt_ * 512, (tt_ + 1) * 512), psl(pi, 0, 512), AF.Copy, [PB[pi]], [stg.buf],
                            scale=sc)
                    dma(dap(dst_h, ci * 128 * S, [[S, 128], [1, S]]), stg.sl(0, 4096), reads=[stg.buf])
            for g, dst_h, sc in ((2, qTd_d, 0.125), (4, kTd_d, 1.0)):
                wa = load_wg(g)
                wp = load_wg(g + 1)
                for ci in range(4):
                    stg = STG[stg_i[0] % 2]
                    stg_i[0] += 1
                    for tt_ in range(NTT):
                        pa = 4 + psr[0] % 2
                        pb_ = 6 + psr[0] % 2
                        t1 = T1[psr[0] % 2]
                        t2 = T2[psr[0] % 2]
                        psr[0] += 1
                        for pi, w in ((pa, wa), (pb_, wp)):
                            for dmc in range(8):
                                mm(pi, psl(pi, 0, 512), w.sl(dmc * 512 + ci * 128, dmc * 512 + (ci + 1) * 128),
                                   HTR.sl(dmc * 4096 + tt_ * 512, dmc * 4096 + (tt_ + 1) * 512), dmc == 0,
                                   dmc == 7, [w.buf, HTR.buf])
                        stt("dve", t1.sl(0, 512), psl(pa, 0, 512), sc, COS.sl(tt_ * 512, (tt_ + 1) * 512), MUL, MUL,
                            [PB[pa], COS.buf], [t1.buf])
                        stt("dve", t2.sl(0, 512), psl(pb_, 0, 512), sc, SIN.sl(tt_ * 512, (tt_ + 1) * 512), MUL, MUL,
                            [PB[pb_], SIN.buf], [t2.buf])
                        tt("pool", stg.sl(tt_ * 512, (tt_ + 1) * 512), t1.sl(0, 512), t2.sl(0, 512), ADD,
                           [t1.buf, t2.buf], [stg.buf])
                    dma(dap(dst_h, ci * 128 * S, [[S, 128], [1, S]]), stg.sl(0, 4096), reads=[stg.buf])
            for g, dst_h in ((6, vs_d), (7, vd_d)):
                w = load_wg(g)
                for tb in range(32):
                    vst = VSTG[(tb // 4) % 2]
                    pi = psr[0] % 4
                    psr[0] += 1
                    for dmc in range(8):
                        mm(pi, psl(pi, 0, 512), HTR.sl(dmc * 4096 + tb * 128, dmc * 4096 + (tb + 1) * 128),
                           w.sl(dmc * 512, (dmc + 1) * 512), dmc == 0, dmc == 7, [w.buf, HTR.buf])
                    act(vst.sl((tb % 4) * 512, (tb % 4 + 1) * 512), psl(pi, 0, 512), AF.Copy, [PB[pi]], [vst.buf])
                    if tb % 4 == 3:
                        dma(dap(dst_h, (tb - 3) * 128 * 512, [[512, 128], [128 * 512, 4], [1, 512]]),
                            vst.ap(0, [[512, 4], [1, 512]]), reads=[vst.buf])
            if debug == "A":
                break

            P.barrier()
            ABF.reset(); AF32.reset()
            QTp = [[ABF.tile(4096, "qt%d_%d" % (i, j)) for j in range(2)] for i in range(2)]
            KT = [ABF.tile(4096, "kt%d" % i) for i in range(2)]
            VP = [ABF.tile(32 * 132, "vp%d" % i) for i in range(2)]
            SPt = [ABF.tile(512, "sp%d" % i) for i in range(3)]
            Wt = [ABF.tile(512, "w%d" % i) for i in range(3)]
            ACC = [ABF.tile(512, "acc%d" % i) for i in range(2)]
            OST = [ABF.tile(4096, "ost%d" % i) for i in range(2)]
            Et = [AF32.tile(512, "e%d" % i) for i in range(3)]
            DJ = [AF32.tile(512, "dj%d" % i) for i in range(2)]
            DEN = [AF32.tile(16, "den%d" % i) for i in range(2)]
            SSQ = [AF32.tile(8, "ssq%d" % i) for i in range(2)]
            JUNK = AF32.tile(128, "junk")
            for i in range(2):
                ms("pool", VP[i].ap(128, [[132, 32], [1, 1]]), 1.0, [], [VP[i].buf])
                ms("pool", QTp[i][0].sl(0, 4096, 64, 128), 0.0, [], [QTp[i][0].buf])
                ms("pool", QTp[i][1].sl(0, 4096, 0, 64), 0.0, [], [QTp[i][1].buf])

            def load_qkv(qh, kh, vh, c, slot):
                dma(QTp[slot][0].sl(0, 4096, 0, 64), dap(qh, c * 128 * S, [[S, 64], [1, S]]),
                    writes=[QTp[slot][0].buf])
                dma(QTp[slot][1].sl(0, 4096, 64, 128), dap(qh, c * 128 * S + 64 * S, [[S, 64], [1, S]]),
                    writes=[QTp[slot][1].buf])
                dma(KT[slot].sl(0, 4096), dap(kh, c * 128 * S, [[S, 128], [1, S]]), writes=[KT[slot].buf])
                for q8 in range(8):
                    dma(VP[slot].ap(q8 * 4 * 132, [[132, 4], [1, 128]]),
                        dap(vh, q8 * 4 * 128 * 512 + c * 128, [[512, 128], [128 * 512, 4], [1, 128]]),
                        writes=[VP[slot].buf], parallel=(q8 > 0))

            blocks = []
            for pc in range(4):
                for hh in range(2):
                    for tq in range(NTT):
                        for i, kb in enumerate(range(4 * tq + 3, -1, -1)):
                            blocks.append((pc, pc % 2, hh, tq, i, kb))
            nb = len(blocks) if debug != "Bdf" else 0
            load_qkv(qTs_d, kTs_d, vs_d, 0, 0)
            ost_i = [0]
            cur_ost = {}

            def sb_stage1(n):
                pc, slot, hh, tq, i, kb = blocks[n]
                if hh == 0 and tq == 0 and i == 3 and pc + 1 < 4:
                    load_qkv(qTs_d, kTs_d, vs_d, pc + 1, (pc + 1) % 2)
                c0 = 128 * max(0, kb - 4 * tq)
                diag = kb >= 4 * tq
                q0 = tq * 512
                ai = n % 3
                acc = ACC[tq % 2]
                Q = QTp[slot][hh]
                if i == 0:
                    ms("pool", acc.sl(0, 512), 0.0, [], [acc.buf])
                mm(ai, psl(ai, c0, 512), KT[slot].sl(kb * 128, (kb + 1) * 128), Q.sl(q0 + c0, q0 + 512),
                   True, False, [KT[slot].buf, Q.buf], skip=True)
                if diag:
                    mm(ai, psl(ai, c0, c0 + 128), IDB, NEGM, False, False, [CB.buf], skip=True)
                E = Et[n % 3]
                SPn = SPt[n % 3]
                act(E.sl(c0, 512), psl(ai, c0, 512), AF.Exp, [PB[ai]], [E.buf])
                act(SPn.sl(c0, 512), E.sl(c0, 512), AF.Ln, [E.buf], [SPn.buf], bias=1.0)

            def sb_stage2(n):
                pc, slot, hh, tq, i, kb = blocks[n]
                c0 = 128 * max(0, kb - 4 * tq)
                ai = n % 3
                acc = ACC[tq % 2]
                SPn = SPt[n % 3]
                Wn = Wt[n % 3]
                mm(ai, psl(ai, c0, 512), LTRI, SPn.sl(c0, 512), False, i == 0, [CB.buf, SPn.buf], skip=True)
                if i > 0:
                    mm(ai, psl(ai, c0, 512), NEG1, acc.sl(c0, 512), False, True, [CB.buf, acc.buf], skip=True)
                act(Wn.sl(c0, 512), psl(ai, c0, 512), AF.Exp, [PB[ai]], [Wn.buf])
                if kb > 0:
                    tt("pool", acc.sl(c0, 512), acc.sl(c0, 512), SPn.sl(c0, 512), ADD, [SPn.buf, acc.buf], [acc.buf])

            def sb_stage3(n):
                pc, slot, hh, tq, i, kb = blocks[n]
                pb = 64 * hh
                c0 = 128 * max(0, kb - 4 * tq)
                oi = 6 + tq % 2
                Wn = Wt[n % 3]
                mm(oi, psl(oi, c0, 512), VP[slot].ap(kb * 132, [[1, 128]]), Wn.sl(c0, 512),
                   i == 0, kb == 0, [VP[slot].buf, Wn.buf], skip=True)
                if kb == 0:
                    if tq == 0 and hh == 0:
                        cur_ost[pc] = OST[ost_i[0] % 2]
                        ost_i[0] += 1
                    ost = cur_ost[pc]
                    cp("dve", ost.sl(tq * 512, (tq + 1) * 512, pb, pb + 64), psl(oi, 0, 512, pb, pb + 64),
                       [PB[oi]], [ost.buf])
                    if tq == NTT - 1 and hh == 1:
                        dma(dap(mixT_d, pc * 128 * S, [[S, 128], [1, S]]), ost.sl(0, 4096), reads=[ost.buf])

            for step in range(nb + 2 if nb else 0):
                if step < nb:
                    sb_stage1(step)
                if 0 <= step - 1 < nb:
                    sb_stage2(step - 1)
                if 0 <= step - 2 < nb:
                    sb_stage3(step - 2)

            dblocks = []
            for h in range(4):
                for tq in range(NTT):
                    for m in range(2):
                        for kb in range(4 * tq + 4):
                            dblocks.append((h, h % 2, tq, m, kb))
            ndb = len(dblocks) if debug != "Bsb" else 0
            if ndb:
                load_qkv(qTd_d, kTd_d, vd_d, 0, 0)
            srot = [0]
            grp = [0]
            dinfo = {}
            dstate = {}

            def df_stage1(n):
                h, slot, tq, m, kb = dblocks[n]
                if tq == 0 and m == 0 and kb == 3 and h + 1 < 4:
                    load_qkv(qTd_d, kTd_d, vd_d, h + 1, (h + 1) % 2)
                c0 = 128 * max(0, kb - 4 * tq)
                q0 = tq * 512
                si = srot[0] % 2
                Pm = Wt[srot[0] % 3]
                srot[0] += 1
                dinfo[n] = Pm
                Q = QTp[slot][m]
                mm(si, psl(si, c0, 512), KT[slot].sl(kb * 128, (kb + 1) * 128), Q.sl(q0 + c0, q0 + 512), True, True,
                   [KT[slot].buf, Q.buf])
                act(Pm.sl(c0, 512), psl(si, c0, 512), AF.Exp, [PB[si]], [Pm.buf])
                if kb >= 4 * tq:
                    ms("pool", Pm.sl(c0, c0 + 64, 64, 128), 0.0, [Pm.buf], [Pm.buf])

            def df_stage2(n):
                h, slot, tq, m, kb = dblocks[n]
                c0 = 128 * max(0, kb - 4 * tq)
                Pm = dinfo.pop(n)
                if m == 0 and kb == 0:
                    ob = 2 + 3 * (grp[0] % 2)
                    grp[0] += 1
                    dstate["ob"] = ob
                    if tq == 0:
                        dstate["dst"] = OST[ost_i[0] % 2]
                        ost_i[0] += 1
                    for b3 in range(3):
                        ms("dve", psl(ob + b3, 0, 512), 0.0, [], [PB[ob + b3]])
                ob = dstate["ob"]
                dst = dstate["dst"]
                for j in range(c0 // 128, 4):
                    a = m * 4 + j
                    bi = ob + a // 3
                    co = (a % 3) * 132
                    mm(bi, psl(bi, co, co + 129), Pm.sl(j * 128, (j + 1) * 128),
                       VP[slot].ap(kb * 132, [[1, 129]]), False, False, [Pm.buf, VP[slot].buf], skip=True)
                if not (m == 1 and kb == 4 * tq + 3):
                    return
                den = DEN[tq % 2]
                ssq = SSQ[tq % 2]
                dj = DJ[tq % 2]
                for a in range(8):
                    bi = ob + a // 3
                    co = (a % 3) * 132
                    cp("dve", den.sl(a, a + 1), psl(bi, co + 128, co + 129), [PB[bi]], [den.buf])
                d8 = den.sl(0, 8)
                P.op("dve", lambda e: e.reciprocal(out=d8, in_=d8), reads=[den.buf], writes=[den.buf])
                ts("dve", den.sl(4, 8), den.sl(4, 8), neglam.sl(l, l + 1), None, MUL, None,
                   [den.buf, neglam.buf], [den.buf])
                ms("pool", ssq.sl(0, 4), 0.0, [], [ssq.buf])
                for j in range(4):
                    a0, a1 = j, 4 + j
                    b0, c0_ = ob + a0 // 3, (a0 % 3) * 132
                    b1, c1_ = ob + a1 // 3, (a1 % 3) * 132
                    djj = dj.sl(j * 128, (j + 1) * 128)
                    ts("dve", djj, psl(b0, c0_, c0_ + 128), den.sl(j, j + 1), None, MUL, None,
                       [PB[b0], den.buf], [dj.buf])
                    stt("dve", djj, psl(b1, c1_, c1_ + 128), den.sl(4 + j, 5 + j), djj, MUL, ADD,
                        [PB[b1], den.buf, dj.buf], [dj.buf])
                    act(JUNK.sl(0, 128), djj, AF.Square, [dj.buf], [JUNK.buf, ssq.buf], accum=ssq.sl(j, j + 1))
                act(ssq.sl(0, 4), ssq.sl(0, 4), AF.Ln, [ssq.buf, epst.buf], [ssq.buf], scale=1.0 / 128, bias=EPSB)
                act(ssq.sl(0, 4), ssq.sl(0, 4), AF.Exp, [ssq.buf], [ssq.buf], scale=-0.5)
                for j in range(4):
                    djj = dj.sl(j * 128, (j + 1) * 128)
                    ts("dve", djj, djj, ssq.sl(j, j + 1), None, MUL, None, [dj.buf, ssq.buf], [dj.buf])
                si = srot[0] % 2
                srot[0] += 1
                for j in range(4):
                    tr(si, psl(si, j * 128, (j + 1) * 128), dj.sl(j * 128, (j + 1) * 128), [dj.buf])
                cp("dve", dst.sl(tq * 512, (tq + 1) * 512), psl(si, 0, 512), [PB[si]], [dst.buf])
                if tq == NTT - 1:
                    dma(dap(mixT_d, (4 + h) * 128 * S, [[S, 128], [1, S]]), dst.sl(0, 4096), reads=[dst.buf])

            for step in range(ndb + 1 if ndb else 0):
                if step < ndb:
                    df_stage1(step)
                if 0 <= step - 1 < ndb:
                    df_stage2(step - 1)
            if debug and debug.startswith("B"):
                break

            P.barrier()
            ABF.reset(); AF32.reset()
            WO = ABF.tile(8192, "wo")
            MT = [ABF.tile(4096, "mt%d" % i) for i in range(2)]
            HT2 = ABF.tile(4096, "ht2")
            W1G = [ABF.tile(4096, "w1g%d" % i) for i in range(2)]
            W2G = [ABF.tile(4096, "w2g%d" % i) for i in range(2)]
            Rt = [ABF.tile(512, "r%d" % i) for i in range(2)]
            UT = ABF.tile(16384, "ut")
            SQ = ABF.tile(4096, "sq")
            HTn = ABF.tile(4096, "htn")
            XTc = [AF32.tile(4096, "xtc%d" % i) for i in range(2)]
            RS = AF32.tile(512, "rs")
            YT = AF32.tile(4096, "yt")
            OUTS = [AF32.tile(1024, "outs%d" % i) for i in range(2)]
            last = (l == L - 1)
            for hf in range(2):
                dma(WO.sl(hf * 4096, (hf + 1) * 4096),
                    dap(wbo, (l * 2 + hf) * 128 * 4096, [[4096, 128], [1, 4096]]), writes=[WO.buf], parallel=(hf > 0))
            load_tile8(MT[0], mixT_d, 0)
            load_tile8(XTc[0], xT_d, 0)
            prot = [0]
            w1i = [0]
            w2i = [0]

            def load_w1(fg):
                w = W1G[w1i[0] % 2]
                w1i[0] += 1
                dma(w.sl(0, 4096), dap(wb1, (l * 8 + fg) * 128 * 4096, [[4096, 128], [1, 4096]]), writes=[w.buf])
                return w

            def load_w2(dc):
                w = W2G[w2i[0] % 2]
                w2i[0] += 1
                dma(w.sl(0, 4096), dap(wb2, (l * 8 + dc) * 128 * 4096, [[4096, 128], [1, 4096]]), writes=[w.buf])
                return w

            for tt_ in range(NTT):
                XT = XTc[tt_ % 2]
                MTt = MT[tt_ % 2]
                if tt_ + 1 < NTT:
                    load_tile8(MT[(tt_ + 1) % 2], mixT_d, tt_ + 1)
                    load_tile8(XTc[(tt_ + 1) % 2], xT_d, tt_ + 1)
                w1n = load_w1(0)
                for dc in range(8):
                    pi = prot[0] % 4
                    prot[0] += 1
                    for ec in range(8):
                        mm(pi, psl(pi, 0, 512), WO.sl(ec * 1024 + dc * 128, ec * 1024 + (dc + 1) * 128),
                           MTt.sl(ec * 512, (ec + 1) * 512), ec == 0, ec == 7, [WO.buf, MTt.buf])
                    tt("dve", XT.sl(dc * 512, (dc + 1) * 512), psl(pi, 0, 512), XT.sl(dc * 512, (dc + 1) * 512), ADD,
                       [PB[pi], XT.buf], [XT.buf])
                norm_stats(XT, SQ, RS, 4)
                norm_apply(XT, RS, HT2, "pool")
                for fg in range(8):
                    w1 = w1n
                    if fg + 1 < 8:
                        w1n = load_w1(fg + 1)
                    else:
                        w2n = load_w2(0)
                    for fi in range(4):
                        fc = fg * 4 + fi
                        pi = prot[0] % 4
                        prot[0] += 1
                        for dmc in range(8):
                            mm(pi, psl(pi, 0, 512), w1.sl(dmc * 512 + fi * 128, dmc * 512 + (fi + 1) * 128),
                               HT2.sl(dmc * 512, (dmc + 1) * 512), dmc == 0, dmc == 7, [w1.buf, HT2.buf])
                        R = Rt[fc % 2]
                        act(R.sl(0, 512), psl(pi, 0, 512), AF.Relu, [PB[pi]], [R.buf])
                        tt("pool", UT.sl(fc * 512, (fc + 1) * 512), R.sl(0, 512), R.sl(0, 512), MUL, [R.buf], [UT.buf])
                for dc in range(8):
                    w2 = w2n
                    if dc + 1 < 8:
                        w2n = load_w2(dc + 1)
                    pi = prot[0] % 4
                    prot[0] += 1
                    for fc in range(32):
                        mm(pi, psl(pi, 0, 512), w2.sl(fc * 128, (fc + 1) * 128), UT.sl(fc * 512, (fc + 1) * 512),
                           fc == 0, fc == 31, [w2.buf, UT.buf])
                    tt("dve", XT.sl(dc * 512, (dc + 1) * 512), psl(pi, 0, 512), XT.sl(dc * 512, (dc + 1) * 512), ADD,
                       [PB[pi], XT.buf], [XT.buf])
                norm_stats(XT, SQ, RS, 4)
                if not last:
                    store_tile8(XT, xT_d, tt_)
                    norm_apply(XT, RS, HTn, "pool")
                    store_tile8(HTn, hT_d, tt_)
                else:
                    if debug:
                        store_tile8(XT, xT_d, tt_)
                    for c in range(8):
                        stt("dve", YT.sl(c * 512, (c + 1) * 512), XT.sl(c * 512, (c + 1) * 512), VEC.sl(64 + c, 65 + c),
                            RS.sl(0, 512), MUL, MUL, [XT.buf, VEC.buf, RS.buf], [YT.buf])
                    for j in range(4):
                        tb = tt_ * 4 + j
                        outs = OUTS[tb % 2]
                        for hb in range(2):
                            pi = 5 + (tb * 2 + hb) % 3
                            for c4 in range(4):
                                c = hb * 4 + c4
                                tr(pi, psl(pi, c4 * 128, (c4 + 1) * 128),
                                   YT.sl(c * 512 + j * 128, c * 512 + (j + 1) * 128), [YT.buf])
                            cp("dve", outs.sl(hb * 512, (hb + 1) * 512), psl(pi, 0, 512), [PB[pi]], [outs.buf])
                        dma(out_d.ap()[tb * 128:(tb + 1) * 128, :], outs.sl(0, 1024), reads=[outs.buf])
            if debug == "C":
                break
        P.emit()
        nc._prog_stats = {e: len(P.ops[e]) for e in ENGS}
    return nc


def host_layout(w_in, attn_norm, subln_norm, lam_q1, lam_k1, lam_q2, lam_k2, mlp_norm, final_norm):
    f32 = np.float32
    perm = np.concatenate([(np.arange(64) + 32) % 64 + 64 * g for g in range(8)])
    sbq, sbk, sbv = w_in[:, :, 0:512], w_in[:, :, 512:1024], w_in[:, :, 1024:1536]
    dfq, dfk, dfv = w_in[:, :, 1536:2048], w_in[:, :, 2048:2560], w_in[:, :, 2560:3072]
    w_ext = np.ascontiguousarray(np.concatenate(
        [sbq, sbk, dfq, dfq[:, :, perm], dfk, dfk[:, :, perm], sbv, dfv], axis=2).astype(f32))
    vecs = np.zeros((128, 76 + 1024), f32)
    vecs[:, 0:32] = attn_norm.reshape(L, 8, 128).transpose(2, 0, 1).reshape(128, 32)
    vecs[:, 32:64] = mlp_norm.reshape(L, 8, 128).transpose(2, 0, 1).reshape(128, 32)
    vecs[:, 64:72] = final_norm.reshape(8, 128).T
    vecs[:, 72:76] = subln_norm.T
    lam = np.stack([lam_q1, lam_k1, lam_q2, lam_k2], 0).reshape(-1)
    vecs[:, 76:] = np.broadcast_to(lam[None, :], (128, 1024))
    consts = np.zeros((128, 640), f32)
    r = np.arange(128)
    consts[:, 0:128] = np.eye(128, dtype=f32)
    consts[:, 128:256] = np.where(r[:, None] >= r[None, :], NEG, 0.0)
    consts[:, 256:384] = np.where(r[:, None] >= r[None, :], -1.0, 0.0)
    consts[:, 384:512] = -1.0
    consts[:, 512:640] = 1.0
    inv = 1.0 / (10000.0 ** (np.arange(0, 64, 2, dtype=np.float32) / 64))
    ang = np.arange(S, dtype=np.float32)[:, None] * inv[None, :]
    ang = np.concatenate([ang, ang], -1)
    cos = np.cos(ang).T.astype(f32)
    sin = np.sin(ang).T.astype(f32)
    sin_s = np.concatenate([-sin[:32], sin[32:]], 0)
    rope = np.stack([np.concatenate([cos, cos], 0), np.concatenate([sin_s, sin_s], 0)], 0)
    return w_ext, vecs, consts, np.ascontiguousarray(rope.astype(f32))


_NC_CACHE = {}


def kernel(x, w_in, w_o, attn_norm, subln_norm, lam_q1, lam_k1, lam_q2, lam_k2, mlp_norm, w_ff1, w_ff2,
           final_norm):
    x = np.asarray(x, np.float32)
    w_ext, vecs, consts, rope = host_layout(np.asarray(w_in, np.float32), np.asarray(attn_norm, np.float32),
                                            np.asarray(subln_norm, np.float32), np.asarray(lam_q1, np.float32),
                                            np.asarray(lam_k1, np.float32), np.asarray(lam_q2, np.float32),
                                            np.asarray(lam_k2, np.float32), np.asarray(mlp_norm, np.float32),
                                            np.asarray(final_norm, np.float32))
    if "nc" not in _NC_CACHE:
        _NC_CACHE["nc"] = build()
    nc = _NC_CACHE["nc"]
    shared = {"w_in": w_ext, "w_o": np.ascontiguousarray(np.asarray(w_o, np.float32)),
              "w_ff1": np.ascontiguousarray(np.asarray(w_ff1, np.float32)),
              "w_ff2": np.ascontiguousarray(np.asarray(w_ff2, np.float32)),
              "vecs": vecs, "consts": consts, "rope": rope}
    in_maps = [dict(shared, x=np.ascontiguousarray(x[c])) for c in range(8)]
    res = run_bass_kernel_spmd(nc, in_maps, core_ids=list(range(8)))
    return np.stack([np.asarray(r["out"], np.float32) for r in res.results], 0)
```

```python
import math
from contextlib import ExitStack

import numpy as np
import concourse.bass as bass
import concourse.mybir as mybir
from concourse.bass_utils import run_bass_kernel_spmd

F32 = mybir.dt.float32
BF16 = mybir.dt.bfloat16
AF = mybir.ActivationFunctionType
ALU = mybir.AluOpType
AX = mybir.AxisListType

S = 4096
D = 1024
L = 4
NTT = 8
EPS = 1e-6
NEG = -30000.0
ENGS = ("pe", "act", "dve", "pool", "sp")
DMA_INC = 16


class Buf:
    __slots__ = ("name", "writers", "readers", "sem", "cnt", "gen_deps")

    def __init__(self, name=""):
        self.name = name
        self.writers = []
        self.readers = []
        self.sem = None
        self.cnt = 0
        self.gen_deps = []


class Op:
    __slots__ = ("eng", "fn", "deps", "sig", "dma", "count")

    def __init__(self, eng, fn):
        self.eng = eng
        self.fn = fn
        self.deps = []
        self.sig = False
        self.dma = None
        self.count = None


class Prog:
    def __init__(self, nc, n_dma_sems=160):
        self.nc = nc
        self.ops = {e: [] for e in ENGS}
        self.n_dma_sems = n_dma_sems
        self.dma_sem_next = 0
        self.dma_sem_max = 0
        self.dma_last = {}
        self.pending = {e: [] for e in ENGS}

    def op(self, eng, fn, reads=(), writes=()):
        o = Op(eng, fn)
        idx = len(self.ops[eng])
        deps = []
        for b in reads:
            deps.extend(b.writers)
        for b in writes:
            deps.extend(b.writers)
            deps.extend(b.readers)
        deps.extend(self.pending[eng])
        self.pending[eng] = []
        ev = ("e", eng, idx)
        o.deps = [d for d in deps if not (d[0] == "e" and d[1] == "pe" and eng == "pe")]
        self.ops[eng].append(o)
        for b in reads:
            b.readers.append(ev)
        for b in writes:
            b.gen_deps = b.writers + b.readers
            b.writers = [ev]
            b.readers = []
        return ev

    def dma(self, eng, fn, reads=(), writes=(), sem_buf=None, parallel=False):
        o = Op(eng, fn)
        if sem_buf is None:
            sem_buf = writes[0] if writes else reads[0]
        if sem_buf.sem is None:
            sem_buf.sem = self.dma_sem_next
            sem_buf.cnt = self.dma_last.get(sem_buf.sem, 0)
            self.dma_sem_next += 1
            self.dma_sem_max = max(self.dma_sem_max, self.dma_sem_next)
            assert self.dma_sem_next <= self.n_dma_sems, "out of dma sems"
        sem_buf.cnt += DMA_INC
        ev = ("d", sem_buf.sem, sem_buf.cnt)
        self.dma_last[sem_buf.sem] = sem_buf.cnt
        deps = []
        for b in reads:
            deps.extend(b.writers)
        for b in writes:
            if parallel and b.writers and all(w[0] == "d" for w in b.writers) and not b.readers:
                deps.extend(b.gen_deps)
            else:
                deps.extend(b.writers)
                deps.extend(b.readers)
        deps.extend(self.pending[eng])
        self.pending[eng] = []
        o.deps = deps
        o.dma = sem_buf.sem
        self.ops[eng].append(o)
        for b in reads:
            b.readers.append(ev)
        for b in writes:
            if parallel and b.writers and all(w[0] == "d" for w in b.writers) and not b.readers:
                b.writers = b.writers + [ev]
            else:
                b.gen_deps = b.writers + b.readers
                b.writers = [ev]
                b.readers = []
        return ev

    def barrier(self):
        evs = []
        for e in ENGS:
            for i in range(len(self.ops[e]) - 1, -1, -1):
                if self.ops[e][i].dma is None:
                    evs.append(("e", e, i))
                    break
        for s, v in self.dma_last.items():
            evs.append(("d", s, v))
        for e in ENGS:
            self.pending[e].extend(evs)
        self.dma_sem_next = 0

    def emit(self):
        nc = self.nc
        for e in ENGS:
            for o in self.ops[e]:
                for d in o.deps:
                    if d[0] == "e":
                        self.ops[d[1]][d[2]].sig = True
        for e in ENGS:
            c = 0
            for o in self.ops[e]:
                if o.dma is None and o.sig:
                    c += 1
                    o.count = c
        with ExitStack() as st:
            esem = {e: st.enter_context(nc.semaphore("s_" + e)) for e in ENGS}
            dsem = [st.enter_context(nc.semaphore("d%d" % i)) for i in range(self.dma_sem_max)]
            block = st.enter_context(nc.Block())
            handles = {"pe": block.tensor, "act": block.scalar, "dve": block.vector,
                       "pool": block.gpsimd, "sp": block.sync}

            def make(e):
                def body(eng):
                    waited = {}
                    for o in self.ops[e]:
                        need = {}
                        for d in o.deps:
                            if d[0] == "e":
                                key = ("e", d[1])
                                val = self.ops[d[1]][d[2]].count
                            else:
                                key = ("d", d[1])
                                val = d[2]
                            if val > need.get(key, 0):
                                need[key] = val
                        for key, val in need.items():
                            if waited.get(key, 0) >= val:
                                continue
                            waited[key] = val
                            sem = esem[key[1]] if key[0] == "e" else dsem[key[1]]
                            eng.wait_ge(sem, val)
                        ins = o.fn(eng)
                        if o.dma is not None:
                            ins.then_inc(dsem[o.dma], DMA_INC)
                        elif o.sig:
                            ins.then_inc(esem[e], 1)
                    if e == "sp":
                        for s, v in self.dma_last.items():
                            if waited.get(("d", s), 0) < v:
                                eng.wait_ge(dsem[s], v)
                return body

            for e in ENGS:
                handles[e](make(e))


class Tile:
    def __init__(self, t, N, off, n, name=""):
        self.t, self.N, self.off, self.n = t, N, off, n
        self.buf = Buf(name)

    def ap(self, o, dims, p0=0, npart=128):
        return bass.AP(self.t, p0 * self.N + self.off + o, [[self.N, npart]] + [list(d) for d in dims])

    def sl(self, a, b, p0=0, p1=128):
        return self.t[p0:p1, self.off + a:self.off + b]


class Arena:
    def __init__(self, t, N):
        self.t, self.N, self.off = t, N, 0

    def reset(self):
        self.off = 0

    def tile(self, n, name=""):
        o = self.off
        self.off += n
        assert self.off <= self.N, ("arena overflow", name, self.off, self.N)
        return Tile(self.t, self.N, o, n, name)


def lam_init(l):
    return 0.8 - 0.6 * math.exp(-0.3 * l)


def build(n_layers=L, debug=False, poison=False):
    nc = bass.Bass("TRN2", target_bir_lowering=False)
    dt_ = nc.dram_tensor
    x_d = dt_("x", [S, D], F32, kind="ExternalInput")
    win_d = dt_("w_in", [L, D, 4096], F32, kind="ExternalInput")
    wo_d = dt_("w_o", [L, D, D], F32, kind="ExternalInput")
    w1_d = dt_("w_ff1", [L, D, 4096], F32, kind="ExternalInput")
    w2_d = dt_("w_ff2", [L, 4096, D], F32, kind="ExternalInput")
    NV = 76 + 1024
    vec_d = dt_("vecs", [128, NV], F32, kind="ExternalInput")
    cst_d = dt_("consts", [128, 640], F32, kind="ExternalInput")
    rope_d = dt_("rope", [2, 128, S], F32, kind="ExternalInput")
    out_d = dt_("out", [S, D], F32, kind="ExternalOutput")
    sk = "ExternalOutput" if debug else "Internal"
    wbin = dt_("wbin", [L * 8, 128, 4096], BF16)
    wbo = dt_("wbo", [L * 2, 128, 4096], BF16)
    wb1 = dt_("wb1", [L * 8, 128, 4096], BF16)
    wb2 = dt_("wb2", [L * 8, 128, 4096], BF16)
    xT_d = dt_("xT", [8, 128, S], F32, kind=sk)
    hT_d = dt_("hT", [8, 128, S], BF16, kind=sk)
    qTs_d = dt_("qTs", [4, 128, S], BF16, kind=sk)
    kTs_d = dt_("kTs", [4, 128, S], BF16, kind=sk)
    qTd_d = dt_("qTd", [4, 128, S], BF16, kind=sk)
    kTd_d = dt_("kTd", [4, 128, S], BF16, kind=sk)
    vs_d = dt_("vs", [32, 128, 512], BF16, kind=sk)
    vd_d = dt_("vd", [32, 128, 512], BF16, kind=sk)
    mixT_d = dt_("mixT", [8, 128, S], BF16, kind=sk)

    def dap(h, off, dims):
        return bass.AP(h, off, [list(d) for d in dims])

    NBF = 63488
    NF32 = 15360
    with ExitStack() as st:
        abf_t = st.enter_context(nc.sbuf_tensor("abf", [128, NBF], BF16))
        af_t = st.enter_context(nc.sbuf_tensor("af32", [128, NF32], F32))
        cb_t = st.enter_context(nc.sbuf_tensor("cb", [128, 640], BF16))
        cf_t = st.enter_context(nc.sbuf_tensor("cf", [128, 640], F32))
        vec_t = st.enter_context(nc.sbuf_tensor("vec", [128, NV], F32))
        sm_t = st.enter_context(nc.sbuf_tensor("small", [128, 512], F32))
        psb = [st.enter_context(nc.psum_tensor("ps%d" % i, [128, 512], F32)) for i in range(8)]
        PB = [Buf("ps%d" % i) for i in range(8)]
        P = Prog(nc)
        ABF = Arena(abf_t, NBF)
        AF32 = Arena(af_t, NF32)
        CB = Tile(cb_t, 640, 0, 640, "cb")
        CF = Tile(cf_t, 640, 0, 640, "cf")
        VEC = Tile(vec_t, NV, 0, NV, "vec")
        SM = Arena(sm_t, 512)
        sfold = SM.tile(4, "sfold")
        neglam = SM.tile(4, "neglam")
        lamt = SM.tile(8, "lamt")
        epst = SM.tile(1, "epst")

        IDF = CF.sl(0, 128)
        IDB = CB.sl(0, 128)
        NEGM = CB.sl(128, 256)
        LTRI = CB.sl(256, 384)
        NEG1 = CB.sl(384, 512)
        ONES = CB.sl(512, 640)

        def psl(i, a, b, p0=0, p1=128):
            return psb[i][p0:p1, a:b]

        EPSB = epst.sl(0, 1)
        P.op("pool", lambda e: e.memset(EPSB, EPS), writes=[epst.buf])

        def mm(pi, out, lhsT, rhs, start, stop, reads, skip=False):
            if skip:
                P.op("pe", lambda e: e.matmul(out=out, lhsT=lhsT, rhs=rhs, start=start, stop=stop,
                                              skip_group_check=True), reads=reads, writes=[PB[pi]])
            else:
                P.op("pe", lambda e: e.matmul(out=out, lhsT=lhsT, rhs=rhs, start=start, stop=stop),
                     reads=reads, writes=[PB[pi]])

        def tr(pi, out, in_, reads):
            P.op("pe", lambda e: e.transpose(out=out, in_=in_, identity=IDF), reads=list(reads) + [CF.buf],
                 writes=[PB[pi]])

        def act(out, in_, func, reads, writes, scale=1.0, bias=0.0, accum=None):
            if accum is None:
                P.op("act", lambda e: e.activation(out=out, in_=in_, func=func, bias=bias, scale=scale),
                     reads=reads, writes=writes)
            else:
                P.op("act", lambda e: e.activation(out=out, in_=in_, func=func, bias=bias, scale=scale,
                                                   accum_out=accum), reads=reads, writes=writes)

        def tt(eng, out, in0, in1, op, reads, writes):
            P.op(eng, lambda e: e.tensor_tensor(out=out, in0=in0, in1=in1, op=op), reads=reads, writes=writes)

        def ts(eng, out, in0, s1, s2, op0, op1, reads, writes):
            if s2 is None:
                P.op(eng, lambda e: e.tensor_scalar(out=out, in0=in0, scalar1=s1, scalar2=None, op0=op0),
                     reads=reads, writes=writes)
            else:
                P.op(eng, lambda e: e.tensor_scalar(out=out, in0=in0, scalar1=s1, scalar2=s2, op0=op0, op1=op1),
                     reads=reads, writes=writes)

        def stt(eng, out, in0, scalar, in1, op0, op1, reads, writes):
            P.op(eng, lambda e: e.scalar_tensor_tensor(out=out, in0=in0, scalar=scalar, in1=in1, op0=op0, op1=op1),
                 reads=reads, writes=writes)

        def cp(eng, out, in_, reads, writes):
            P.op(eng, lambda e: e.tensor_copy(out=out, in_=in_), reads=reads, writes=writes)

        def ms(eng, ap_, val, reads, writes):
            P.op(eng, lambda e: e.memset(ap_, val), reads=reads, writes=writes)

        def dma(out, in_, reads=(), writes=(), parallel=False):
            P.dma("sp", lambda e: e.dma_start(out=out, in_=in_), reads=list(reads), writes=list(writes),
                  parallel=parallel)

        MUL, ADD, SUB = ALU.mult, ALU.add, ALU.subtract

        dma(cf_t[:, :], cst_d.ap()[:, :], writes=[CF.buf])
        dma(vec_t[:, :], vec_d.ap()[:, :], writes=[VEC.buf])
        cp("dve", cb_t[:, :], cf_t[:, :], [CF.buf], [CB.buf])
        for l in range(L):
            ts("pool", sfold.sl(l, l + 1), VEC.sl(72 + l, 73 + l), 1.0 - lam_init(l), None, MUL, None,
               [VEC.buf], [sfold.buf])
        lprod = AF32.tile(512, "lprod")
        tt("dve", lprod.sl(0, 256), VEC.sl(76, 76 + 256), VEC.sl(76 + 256, 76 + 512), MUL, [VEC.buf], [lprod.buf])
        tt("dve", lprod.sl(256, 512), VEC.sl(76 + 512, 76 + 768), VEC.sl(76 + 768, 76 + 1024), MUL, [VEC.buf],
           [lprod.buf])
        lp_in = lprod.ap(0, [[64, 8], [1, 64]])
        lp_out = lamt.sl(0, 8)
        P.op("dve", lambda e: e.tensor_reduce(out=lp_out, in_=lp_in, axis=AX.X, op=ADD),
             reads=[lprod.buf], writes=[lamt.buf])
        act(lamt.sl(0, 8), lamt.sl(0, 8), AF.Exp, [lamt.buf], [lamt.buf])
        tt("dve", lamt.sl(0, 4), lamt.sl(0, 4), lamt.sl(4, 8), SUB, [lamt.buf], [lamt.buf])
        for l in range(L):
            ts("dve", neglam.sl(l, l + 1), lamt.sl(l, l + 1), lam_init(l), -1.0, ADD, MUL, [lamt.buf], [neglam.buf])

        if poison:
            P.barrier()
            ABF.reset(); AF32.reset()
            pzb = ABF.tile(4096, "pzb")
            pzf = AF32.tile(4096, "pzf")
            ms("pool", pzb.sl(0, 4096), float("nan"), [], [pzb.buf])
            ms("pool", pzf.sl(0, 4096), float("nan"), [], [pzf.buf])
            for hnd, n128 in ((wbin, L * 8), (wbo, L * 2), (wb1, L * 8), (wb2, L * 8), (hT_d, 8), (qTs_d, 4),
                              (kTs_d, 4), (qTd_d, 4), (kTd_d, 4), (mixT_d, 8)):
                for i_ in range(n128):
                    dma(dap(hnd, i_ * 128 * 4096, [[4096, 128], [1, 4096]]), pzb.sl(0, 4096), reads=[pzb.buf])
            for hnd in (vs_d, vd_d):
                for i_ in range(4):
                    dma(dap(hnd, i_ * 128 * 4096, [[4096, 128], [1, 4096]]), pzb.sl(0, 4096), reads=[pzb.buf])
            for i_ in range(8):
                dma(dap(xT_d, i_ * 128 * 4096, [[4096, 128], [1, 4096]]), pzf.sl(0, 4096), reads=[pzf.buf])
        P.barrier()
        ABF.reset(); AF32.reset()
        WS = [AF32.tile(4096, "ws%d" % i) for i in range(2)]
        WB = [ABF.tile(4096, "wb%d" % i) for i in range(2)]
        prep = []
        for l in range(n_layers):
            prep += [("in", l, g) for g in range(8)] + [("o", l, h) for h in range(2)]
            prep += [("f1", l, g) for g in range(8)] + [("f2", l, g) for g in range(8)]
        for i, (kind, l, g) in enumerate(prep):
            ws, wb = WS[i % 2], WB[i % 2]
            ce = "dve" if i % 2 == 0 else "pool"
            if kind in ("in", "f1"):
                src_h = win_d if kind == "in" else w1_d
                dma(ws.ap(0, [[512, 8], [1, 512]]),
                    dap(src_h, l * D * 4096 + g * 512, [[4096, 128], [128 * 4096, 8], [1, 512]]), writes=[ws.buf])
                gcol = (0 if kind == "in" else 32) + l * 8
                tt(ce, wb.ap(0, [[512, 8], [1, 512]]), ws.ap(0, [[512, 8], [1, 512]]),
                   VEC.ap(gcol, [[1, 8], [0, 512]]), MUL, [ws.buf, VEC.buf], [wb.buf])
                dst = dap(wbin if kind == "in" else wb1, (l * 8 + g) * 128 * 4096, [[4096, 128], [1, 4096]])
            elif kind == "f2":
                for q4 in range(4):
                    dma(ws.ap(q4 * 8 * 128, [[128, 8], [1, 128]]),
                        dap(w2_d, l * 4096 * D + q4 * 8 * 128 * D + g * 128, [[D, 128], [128 * D, 8], [1, 128]]),
                        writes=[ws.buf], parallel=(q4 > 0))
                cp(ce, wb.sl(0, 4096), ws.sl(0, 4096), [ws.buf], [wb.buf])
                dst = dap(wb2, (l * 8 + g) * 128 * 4096, [[4096, 128], [1, 4096]])
            else:
                dma(ws.ap(0, [[1024, 4], [1, 1024]]),
                    dap(wo_d, l * D * D + g * 512 * D, [[D, 128], [128 * D, 4], [1, 1024]]), writes=[ws.buf])
                if g == 0:
                    cp(ce, wb.sl(0, 4096), ws.sl(0, 4096), [ws.buf], [wb.buf])
                else:
                    ts(ce, wb.sl(0, 4096), ws.sl(0, 4096), sfold.sl(l, l + 1), None, MUL, None,
                       [ws.buf, sfold.buf], [wb.buf])
                dst = dap(wbo, (l * 2 + g) * 128 * 4096, [[4096, 128], [1, 4096]])
            dma(dst, wb.sl(0, 4096), reads=[wb.buf])

        def norm_stats(XT, SQ, RS, pi):
            act(SQ.sl(0, 4096), XT.sl(0, 4096), AF.Square, [XT.buf], [SQ.buf])
            for c in range(8):
                mm(pi, psl(pi, 0, 512), ONES, SQ.sl(c * 512, (c + 1) * 512), c == 0, c == 7, [SQ.buf, CB.buf])
            act(RS.sl(0, 512), psl(pi, 0, 512), AF.Ln, [PB[pi], epst.buf], [RS.buf], scale=1.0 / D, bias=EPSB)
            act(RS.sl(0, 512), RS.sl(0, 512), AF.Exp, [RS.buf], [RS.buf], scale=-0.5)

        def norm_apply(XT, RS, HT, eng):
            tt(eng, HT.ap(0, [[512, 8], [1, 512]]), XT.ap(0, [[512, 8], [1, 512]]), RS.ap(0, [[0, 8], [1, 512]]),
               MUL, [XT.buf, RS.buf], [HT.buf])

        def store_tile8(T_, dram_h, tt_):
            dma(dap(dram_h, tt_ * 512, [[S, 128], [128 * S, 8], [1, 512]]), T_.ap(0, [[512, 8], [1, 512]]),
                reads=[T_.buf])

        def load_tile8(T_, dram_h, tt_):
            dma(T_.ap(0, [[512, 8], [1, 512]]), dap(dram_h, tt_ * 512, [[S, 128], [128 * S, 8], [1, 512]]),
                writes=[T_.buf])

        P.barrier()
        ABF.reset(); AF32.reset()
        XIN = [AF32.tile(1024, "xin%d" % i) for i in range(2)]
        XTt = [AF32.tile(4096, "xt%d" % i) for i in range(2)]
        RSt = [AF32.tile(512, "rs%d" % i) for i in range(2)]
        SQt = [ABF.tile(4096, "sq%d" % i) for i in range(2)]
        HTt = [ABF.tile(4096, "ht%d" % i) for i in range(2)]
        for tt_ in range(NTT):
            XT = XTt[tt_ % 2]
            for j in range(4):
                tb = tt_ * 4 + j
                xin = XIN[tb % 2]
                dma(xin.sl(0, 1024), x_d.ap()[tb * 128:(tb + 1) * 128, :], writes=[xin.buf])
                for hb in range(2):
                    pi = (tb * 2 + hb) % 4
                    for c4 in range(4):
                        c = hb * 4 + c4
                        tr(pi, psl(pi, c4 * 128, (c4 + 1) * 128), xin.sl(c * 128, (c + 1) * 128), [xin.buf])
                    cp("dve", XT.ap(hb * 4 * 512 + j * 128, [[512, 4], [1, 128]]),
                       bass.AP(psb[pi], 0, [[512, 128], [128, 4], [1, 128]]), [PB[pi]], [XT.buf])
            store_tile8(XT, xT_d, tt_)
            norm_stats(XT, SQt[tt_ % 2], RSt[tt_ % 2], 4 + tt_ % 2)
            norm_apply(XT, RSt[tt_ % 2], HTt[tt_ % 2], "pool")
            store_tile8(HTt[tt_ % 2], hT_d, tt_)

        for l in range(n_layers):
            P.barrier()
            ABF.reset(); AF32.reset()
            HTR = ABF.tile(32768, "htr")
            WG = [ABF.tile(4096, "wg%d" % i) for i in range(3)]
            STG = [ABF.tile(4096, "stg%d" % i) for i in range(2)]
            VSTG = [ABF.tile(2048, "vstg%d" % i) for i in range(2)]
            COS = AF32.tile(4096, "cos")
            SIN = AF32.tile(4096, "sin")
            T1 = [AF32.tile(512, "t1_%d" % i) for i in range(2)]
            T2 = [AF32.tile(512, "t2_%d" % i) for i in range(2)]
            for c in range(8):
                dma(HTR.sl(c * 4096, (c + 1) * 4096), dap(hT_d, c * 128 * S, [[S, 128], [1, S]]),
                    writes=[HTR.buf], parallel=(c > 0))
            dma(COS.sl(0, 4096), dap(rope_d, 0, [[S, 128], [1, S]]), writes=[COS.buf])
            dma(SIN.sl(0, 4096), dap(rope_d, 128 * S, [[S, 128], [1, S]]), writes=[SIN.buf])
            wg_i = [0]

            def load_wg(g):
                w = WG[wg_i[0] % 3]
                wg_i[0] += 1
                dma(w.sl(0, 4096), dap(wbin, (l * 8 + g) * 128 * 4096, [[4096, 128], [1, 4096]]), writes=[w.buf])
                return w

            psr = [0]
            stg_i = [0]
            for g, dst_h, sc in ((0, qTs_d, 0.125), (1, kTs_d, 1.0)):
                w = load_wg(g)
                for ci in range(4):
                    stg = STG[stg_i[0] % 2]
                    stg_i[0] += 1
                    for tt_ in range(NTT):
                        pi = psr[0] % 4
                        psr[0] += 1
                        for dmc in range(8):
                            mm(pi, psl(pi, 0, 512), w.sl(dmc * 512 + ci * 128, dmc * 512 + (ci + 1) * 128),
                               HTR.sl(dmc * 4096 + tt_ * 512, dmc * 4096 + (tt_ + 1) * 512), dmc == 0, dmc == 7,
                               [w.buf, HTR.buf])
                        act(stg.sl(tt_ * 512, (tt_ + 1) * 512), psl(pi, 0, 512), AF.Copy, [PB[pi]], [stg.buf],
                            scale=sc)
                    dma(dap(dst_h, ci * 128 * S, [[S, 128], [1, S]]), stg.sl(0, 4096), reads=[stg.buf])
            for g, dst_h, sc in ((2, qTd_d, 0.125), (4, kTd_d, 1.0)):
                wa = load_wg(g)
                wp = load_wg(g + 1)
                for ci in range(4):
                    stg = STG[stg_i[0] % 2]
                    stg_i[0] += 1
                    for tt_ in range(NTT):
                        pa = 4 + psr[0] % 2
                        pb_ = 6 + psr[0] % 2
                        t1 = T1[psr[0] % 2]
                        t2 = T2[psr[0] % 2]
                        psr[0] += 1
                        for pi, w in ((pa, wa), (pb_, wp)):
                            for dmc in range(8):
                                mm(pi, psl(pi, 0, 512), w.sl(dmc * 512 + ci * 128, dmc * 512 + (ci + 1) * 128),
                                   HTR.sl(dmc * 4096 + tt_ * 512, dmc * 4096 + (tt_ + 1) * 512), dmc == 0,
                                   dmc == 7, [w.buf, HTR.buf])
                        stt("dve", t1.sl(0, 512), psl(pa, 0, 512), sc, COS.sl(tt_ * 512, (tt_ + 1) * 512), MUL, MUL,
                            [PB[pa], COS.buf], [t1.buf])
                        stt("dve", t2.sl(0, 512), psl(pb_, 0, 512), sc, SIN.sl(tt_ * 512, (tt_ + 1) * 512), MUL, MUL,
                            [PB[pb_], SIN.buf], [t2.buf])
                        tt("pool", stg.sl(tt_ * 512, (tt_ + 1) * 512), t1.sl(0, 512), t2.sl(0, 512), ADD,
                           [t1.buf, t2.buf], [stg.buf])
                    dma(dap(dst_h, ci * 128 * S, [[S, 128], [1, S]]), stg.sl(0, 4096), reads=[stg.buf])
            for g, dst_h in ((6, vs_d), (7, vd_d)):
                w = load_wg(g)
                for tb in range(32):
                    vst = VSTG[(tb // 4) % 2]
                    pi = psr[0] % 4
                    psr[0] += 1
                    for dmc in range(8):
                        mm(pi, psl(pi, 0, 512), HTR.sl(dmc * 4096 + tb * 128, dmc * 4096 + (tb + 1) * 128),
                           w.sl(dmc * 512, (dmc + 1) * 512), dmc == 0, dmc == 7, [w.buf, HTR.buf])
                    act(vst.sl((tb % 4) * 512, (tb % 4 + 1) * 512), psl(pi, 0, 512), AF.Copy, [PB[pi]], [vst.buf])
                    if tb % 4 == 3:
                        dma(dap(dst_h, (tb - 3) * 128 * 512, [[512, 128], [128 * 512, 4], [1, 512]]),
                            vst.ap(0, [[512, 4], [1, 512]]), reads=[vst.buf])
            if debug == "A":
                break

            P.barrier()
            ABF.reset(); AF32.reset()
            QTp = [[ABF.tile(4096, "qt%d_%d" % (i, j)) for j in range(2)] for i in range(2)]
            KT = [ABF.tile(4096, "kt%d" % i) for i in range(2)]
            VP = [ABF.tile(32 * 132, "vp%d" % i) for i in range(2)]
            SPt = [ABF.tile(512, "sp%d" % i) for i in range(3)]
            Wt = [ABF.tile(512, "w%d" % i) for i in range(3)]
            ACC = [ABF.tile(512, "acc%d" % i) for i in range(2)]
            OST = [ABF.tile(4096, "ost%d" % i) for i in range(2)]
            Et = [AF32.tile(512, "e%d" % i) for i in range(3)]
            DJ = [AF32.tile(512, "dj%d" % i) for i in range(2)]
            DEN = [AF32.tile(16, "den%d" % i) for i in range(2)]
            SSQ = [AF32.tile(8, "ssq%d" % i) for i in range(2)]
            JUNK = AF32.tile(128, "junk")
            for i in range(2):
                ms("pool", VP[i].ap(128, [[132, 32], [1, 1]]), 1.0, [], [VP[i].buf])
                ms("pool", QTp[i][0].sl(0, 4096, 64, 128), 0.0, [], [QTp[i][0].buf])
                ms("pool", QTp[i][1].sl(0, 4096, 0, 64), 0.0, [], [QTp[i][1].buf])

            def load_qkv(qh, kh, vh, c, slot):
                dma(QTp[slot][0].sl(0, 4096, 0, 64), dap(qh, c * 128 * S, [[S, 64], [1, S]]),
                    writes=[QTp[slot][0].buf])
                dma(QTp[slot][1].sl(0, 4096, 64, 128), dap(qh, c * 128 * S + 64 * S, [[S, 64], [1, S]]),
                    writes=[QTp[slot][1].buf])
                dma(KT[slot].sl(0, 4096), dap(kh, c * 128 * S, [[S, 128], [1, S]]), writes=[KT[slot].buf])
                for q8 in range(8):
                    dma(VP[slot].ap(q8 * 4 * 132, [[132, 4], [1, 128]]),
                        dap(vh, q8 * 4 * 128 * 512 + c * 128, [[512, 128], [128 * 512, 4], [1, 128]]),
                        writes=[VP[slot].buf], parallel=(q8 > 0))

            blocks = []
            for pc in range(4):
                for hh in range(2):
                    for tq in range(NTT):
                        for i, kb in enumerate(range(4 * tq + 3, -1, -1)):
                            blocks.append((pc, pc % 2, hh, tq, i, kb))
            nb = len(blocks) if debug != "Bdf" else 0
            load_qkv(qTs_d, kTs_d, vs_d, 0, 0)
            ost_i = [0]
            cur_ost = {}

            def sb_stage1(n):
                pc, slot, hh, tq, i, kb = blocks[n]
                if hh == 0 and tq == 0 and i == 3 and pc + 1 < 4:
                    load_qkv(qTs_d, kTs_d, vs_d, pc + 1, (pc + 1) % 2)
                c0 = 128 * max(0, kb - 4 * tq)
                diag = kb >= 4 * tq
                q0 = tq * 512
                ai = n % 3
                acc = ACC[tq % 2]
                Q = QTp[slot][hh]
                if i == 0:
                    ms("pool", acc.sl(0, 512), 0.0, [], [acc.buf])
                mm(ai, psl(ai, c0, 512), KT[slot].sl(kb * 128, (kb + 1) * 128), Q.sl(q0 + c0, q0 + 512),
                   True, False, [KT[slot].buf, Q.buf], skip=True)
                if diag:
                    mm(ai, psl(ai, c0, c0 + 128), IDB, NEGM, False, False, [CB.buf], skip=True)
                E = Et[n % 3]
                SPn = SPt[n % 3]
                act(E.sl(c0, 512), psl(ai, c0, 512), AF.Exp, [PB[ai]], [E.buf])
                act(SPn.sl(c0, 512), E.sl(c0, 512), AF.Ln, [E.buf], [SPn.buf], bias=1.0)

            def sb_stage2(n):
                pc, slot, hh, tq, i, kb = blocks[n]
                c0 = 128 * max(0, kb - 4 * tq)
                ai = n % 3
                acc = ACC[tq % 2]
                SPn = SPt[n % 3]
                Wn = Wt[n % 3]
                mm(ai, psl(ai, c0, 512), LTRI, SPn.sl(c0, 512), False, i == 0, [CB.buf, SPn.buf], skip=True)
                if i > 0:
                    mm(ai, psl(ai, c0, 512), NEG1, acc.sl(c0, 512), False, True, [CB.buf, acc.buf], skip=True)
                act(Wn.sl(c0, 512), psl(ai, c0, 512), AF.Exp, [PB[ai]], [Wn.buf])
                if kb > 0:
                    tt("pool", acc.sl(c0, 512), acc.sl(c0, 512), SPn.sl(c0, 512), ADD, [SPn.buf, acc.buf], [acc.buf])

            def sb_stage3(n):
                pc, slot, hh, tq, i, kb = blocks[n]
                pb = 64 * hh
                c0 = 128 * max(0, kb - 4 * tq)
                oi = 6 + tq % 2
                Wn = Wt[n % 3]
                mm(oi, psl(oi, c0, 512), VP[slot].ap(kb * 132, [[1, 128]]), Wn.sl(c0, 512),
                   i == 0, kb == 0, [VP[slot].buf, Wn.buf], skip=True)
                if kb == 0:
                    if tq == 0 and hh == 0:
                        cur_ost[pc] = OST[ost_i[0] % 2]
                        ost_i[0] += 1
                    ost = cur_ost[pc]
                    cp("dve", ost.sl(tq * 512, (tq + 1) * 512, pb, pb + 64), psl(oi, 0, 512, pb, pb + 64),
                       [PB[oi]], [ost.buf])
                    if tq == NTT - 1 and hh == 1:
                        dma(dap(mixT_d, pc * 128 * S, [[S, 128], [1, S]]), ost.sl(0, 4096), reads=[ost.buf])

            for step in range(nb + 2 if nb else 0):
                if step < nb:
                    sb_stage1(step)
                if 0 <= step - 1 < nb:
                    sb_stage2(step - 1)
                if 0 <= step - 2 < nb:
                    sb_stage3(step - 2)

            dblocks = []
            for h in range(4):
                for tq in range(NTT):
                    for m in range(2):
                        for kb in range(4 * tq + 4):
                            dblocks.append((h, h % 2, tq, m, kb))
            ndb = len(dblocks) if debug != "Bsb" else 0
            if ndb:
                load_qkv(qTd_d, kTd_d, vd_d, 0, 0)
            srot = [0]
            grp = [0]
            dinfo = {}
            dstate = {}

            def df_stage1(n):
                h, slot, tq, m, kb = dblocks[n]
                if tq == 0 and m == 0 and kb == 3 and h + 1 < 4:
                    load_qkv(qTd_d, kTd_d, vd_d, h + 1, (h + 1) % 2)
                c0 = 128 * max(0, kb - 4 * tq)
                q0 = tq * 512
                si = srot[0] % 2
                Pm = Wt[srot[0] % 3]
                srot[0] += 1
                dinfo[n] = Pm
                Q = QTp[slot][m]
                mm(si, psl(si, c0, 512), KT[slot].sl(kb * 128, (kb + 1) * 128), Q.sl(q0 + c0, q0 + 512), True, True,
                   [KT[slot].buf, Q.buf])
                act(Pm.sl(c0, 512), psl(si, c0, 512), AF.Exp, [PB[si]], [Pm.buf])
                if kb >= 4 * tq:
                    ms("pool", Pm.sl(c0, c0 + 64, 64, 128), 0.0, [Pm.buf], [Pm.buf])

            def df_stage2(n):
                h, slot, tq, m, kb = dblocks[n]
                c0 = 128 * max(0, kb - 4 * tq)
                Pm = dinfo.pop(n)
                if m == 0 and kb == 0:
                    ob = 2 + 3 * (grp[0] % 2)
                    grp[0] += 1
                    dstate["ob"] = ob
                    if tq == 0:
                        dstate["dst"] = OST[ost_i[0] % 2]
                        ost_i[0] += 1
                    for b3 in range(3):
                        ms("dve", psl(ob + b3, 0, 512), 0.0, [], [PB[ob + b3]])
                ob = dstate["ob"]
                dst = dstate["dst"]
                for j in range(c0 // 128, 4):
                    a = m * 4 + j
                    bi = ob + a // 3
                    co = (a % 3) * 132
                    mm(bi, psl(bi, co, co + 129), Pm.sl(j * 128, (j + 1) * 128),
                       VP[slot].ap(kb * 132, [[1, 129]]), False, False, [Pm.buf, VP[slot].buf], skip=True)
                if not (m == 1 and kb == 4 * tq + 3):
                    return
                den = DEN[tq % 2]
                ssq = SSQ[tq % 2]
                dj = DJ[tq % 2]
                for a in range(8):
                    bi = ob + a // 3
                    co = (a % 3) * 132
                    cp("dve", den.sl(a, a + 1), psl(bi, co + 128, co + 129), [PB[bi]], [den.buf])
                d8 = den.sl(0, 8)
                P.op("dve", lambda e: e.reciprocal(out=d8, in_=d8), reads=[den.buf], writes=[den.buf])
                ts("dve", den.sl(4, 8), den.sl(4, 8), neglam.sl(l, l + 1), None, MUL, None,
                   [den.buf, neglam.buf], [den.buf])
                ms("pool", ssq.sl(0, 4), 0.0, [], [ssq.buf])
                for j in range(4):
                    a0, a1 = j, 4 + j
                    b0, c0_ = ob + a0 // 3, (a0 % 3) * 132
                    b1, c1_ = ob + a1 // 3, (a1 % 3) * 132
                    djj = dj.sl(j * 128, (j + 1) * 128)
                    ts("dve", djj, psl(b0, c0_, c0_ + 128), den.sl(j, j + 1), None, MUL, None,
                       [PB[b0], den.buf], [dj.buf])
                    stt("dve", djj, psl(b1, c1_, c1_ + 128), den.sl(4 + j, 5 + j), djj, MUL, ADD,
                        [PB[b1], den.buf, dj.buf], [dj.buf])
                    act(JUNK.sl(0, 128), djj, AF.Square, [dj.buf], [JUNK.buf, ssq.buf], accum=ssq.sl(j, j + 1))
                act(ssq.sl(0, 4), ssq.sl(0, 4), AF.Ln, [ssq.buf, epst.buf], [ssq.buf], scale=1.0 / 128, bias=EPSB)
                act(ssq.sl(0, 4), ssq.sl(0, 4), AF.Exp, [ssq.buf], [ssq.buf], scale=-0.5)
                for j in range(4):
                    djj = dj.sl(j * 128, (j + 1) * 128)
                    ts("dve", djj, djj, ssq.sl(j, j + 1), None, MUL, None, [dj.buf, ssq.buf], [dj.buf])
                si = srot[0] % 2
                srot[0] += 1
                for j in range(4):
                    tr(si, psl(si, j * 128, (j + 1) * 128), dj.sl(j * 128, (j + 1) * 128), [dj.buf])
                cp("dve", dst.sl(tq * 512, (tq + 1) * 512), psl(si, 0, 512), [PB[si]], [dst.buf])
                if tq == NTT - 1:
                    dma(dap(mixT_d, (4 + h) * 128 * S, [[S, 128], [1, S]]), dst.sl(0, 4096), reads=[dst.buf])

            for step in range(ndb + 1 if ndb else 0):
                if step < ndb:
                    df_stage1(step)
                if 0 <= step - 1 < ndb:
                    df_stage2(step - 1)
            if debug and debug.startswith("B"):
                break

            P.barrier()
            ABF.reset(); AF32.reset()
            WO = ABF.tile(8192, "wo")
            MT = [ABF.tile(4096, "mt%d" % i) for i in range(2)]
            HT2 = ABF.tile(4096, "ht2")
            W1G = [ABF.tile(4096, "w1g%d" % i) for i in range(2)]
            W2G = [ABF.tile(4096, "w2g%d" % i) for i in range(2)]
            Rt = [ABF.tile(512, "r%d" % i) for i in range(2)]
            UT = ABF.tile(16384, "ut")
            SQ = ABF.tile(4096, "sq")
            HTn = ABF.tile(4096, "htn")
            XTc = [AF32.tile(4096, "xtc%d" % i) for i in range(2)]
            RS = AF32.tile(512, "rs")
            YT = AF32.tile(4096, "yt")
            OUTS = [AF32.tile(1024, "outs%d" % i) for i in range(2)]
            last = (l == n_layers - 1)
            for hf in range(2):
                dma(WO.sl(hf * 4096, (hf + 1) * 4096),
                    dap(wbo, (l * 2 + hf) * 128 * 4096, [[4096, 128], [1, 4096]]), writes=[WO.buf], parallel=(hf > 0))
            load_tile8(MT[0], mixT_d, 0)
            load_tile8(XTc[0], xT_d, 0)
            prot = [0]
            w1i = [0]
            w2i = [0]

            def load_w1(fg):
                w = W1G[w1i[0] % 2]
                w1i[0] += 1
                dma(w.sl(0, 4096), dap(wb1, (l * 8 + fg) * 128 * 4096, [[4096, 128], [1, 4096]]), writes=[w.buf])
                return w

            def load_w2(dc):
                w = W2G[w2i[0] % 2]
                w2i[0] += 1
                dma(w.sl(0, 4096), dap(wb2, (l * 8 + dc) * 128 * 4096, [[4096, 128], [1, 4096]]), writes=[w.buf])
                return w

            for tt_ in range(NTT):
                XT = XTc[tt_ % 2]
                MTt = MT[tt_ % 2]
                if tt_ + 1 < NTT:
                    load_tile8(MT[(tt_ + 1) % 2], mixT_d, tt_ + 1)
                    load_tile8(XTc[(tt_ + 1) % 2], xT_d, tt_ + 1)
                w1n = load_w1(0)
                for dc in range(8):
                    pi = prot[0] % 4
                    prot[0] += 1
                    for ec in range(8):
                        mm(pi, psl(pi, 0, 512), WO.sl(ec * 1024 + dc * 128, ec * 1024 + (dc + 1) * 128),
                           MTt.sl(ec * 512, (ec + 1) * 512), ec == 0, ec == 7, [WO.buf, MTt.buf])
                    tt("dve", XT.sl(dc * 512, (dc + 1) * 512), psl(pi, 0, 512), XT.sl(dc * 512, (dc + 1) * 512), ADD,
                       [PB[pi], XT.buf], [XT.buf])
                norm_stats(XT, SQ, RS, 4)
                norm_apply(XT, RS, HT2, "pool")
                for fg in range(8):
                    w1 = w1n
                    if fg + 1 < 8:
                        w1n = load_w1(fg + 1)
                    else:
                        w2n = load_w2(0)
                    for fi in range(4):
                        fc = fg * 4 + fi
                        pi = prot[0] % 4
                        prot[0] += 1
                        for dmc in range(8):
                            mm(pi, psl(pi, 0, 512), w1.sl(dmc * 512 + fi * 128, dmc * 512 + (fi + 1) * 128),
                               HT2.sl(dmc * 512, (dmc + 1) * 512), dmc == 0, dmc == 7, [w1.buf, HT2.buf])
                        R = Rt[fc % 2]
                        act(R.sl(0, 512), psl(pi, 0, 512), AF.Relu, [PB[pi]], [R.buf])
                        tt("pool", UT.sl(fc * 512, (fc + 1) * 512), R.sl(0, 512), R.sl(0, 512), MUL, [R.buf], [UT.buf])
                for dc in range(8):
                    w2 = w2n
                    if dc + 1 < 8:
                        w2n = load_w2(dc + 1)
                    pi = prot[0] % 4
                    prot[0] += 1
                    for fc in range(32):
                        mm(pi, psl(pi, 0, 512), w2.sl(fc * 128, (fc + 1) * 128), UT.sl(fc * 512, (fc + 1) * 512),
                           fc == 0, fc == 31, [w2.buf, UT.buf])
                    tt("dve", XT.sl(dc * 512, (dc + 1) * 512), psl(pi, 0, 512), XT.sl(dc * 512, (dc + 1) * 512), ADD,
                       [PB[pi], XT.buf], [XT.buf])
                norm_stats(XT, SQ, RS, 4)
                if not last:
                    store_tile8(XT, xT_d, tt_)
                    norm_apply(XT, RS, HTn, "pool")
                    store_tile8(HTn, hT_d, tt_)
                else:
                    if debug:
                        store_tile8(XT, xT_d, tt_)
                    for c in range(8):
                        stt("dve", YT.sl(c * 512, (c + 1) * 512), XT.sl(c * 512, (c + 1) * 512), VEC.sl(64 + c, 65 + c),
                            RS.sl(0, 512), MUL, MUL, [XT.buf, VEC.buf, RS.buf], [YT.buf])
                    for j in range(4):
                        tb = tt_ * 4 + j
                        outs = OUTS[tb % 2]
                        for hb in range(2):
                            pi = 5 + (tb * 2 + hb) % 3
                            for c4 in range(4):
                                c = hb * 4 + c4
                                tr(pi, psl(pi, c4 * 128, (c4 + 1) * 128),
                                   YT.sl(c * 512 + j * 128, c * 512 + (j + 1) * 128), [YT.buf])
                            cp("dve", outs.sl(hb * 512, (hb + 1) * 512), psl(pi, 0, 512), [PB[pi]], [outs.buf])
                        dma(out_d.ap()[tb * 128:(tb + 1) * 128, :], outs.sl(0, 1024), reads=[outs.buf])
            if debug == "C":
                break
        P.emit()
        nc._prog_stats = {e: len(P.ops[e]) for e in ENGS}
    return nc


def host_layout(w_in, attn_norm, subln_norm, lam_q1, lam_k1, lam_q2, lam_k2, mlp_norm, final_norm):
    f32 = np.float32
    perm = np.concatenate([(np.arange(64) + 32) % 64 + 64 * g for g in range(8)])
    sbq, sbk, sbv = w_in[:, :, 0:512], w_in[:, :, 512:1024], w_in[:, :, 1024:1536]
    dfq, dfk, dfv = w_in[:, :, 1536:2048], w_in[:, :, 2048:2560], w_in[:, :, 2560:3072]
    w_ext = np.ascontiguousarray(np.concatenate(
        [sbq, sbk, dfq, dfq[:, :, perm], dfk, dfk[:, :, perm], sbv, dfv], axis=2).astype(f32))
    vecs = np.zeros((128, 76 + 1024), f32)
    vecs[:, 0:32] = attn_norm.reshape(L, 8, 128).transpose(2, 0, 1).reshape(128, 32)
    vecs[:, 32:64] = mlp_norm.reshape(L, 8, 128).transpose(2, 0, 1).reshape(128, 32)
    vecs[:, 64:72] = final_norm.reshape(8, 128).T
    vecs[:, 72:76] = subln_norm.T
    lam = np.stack([lam_q1, lam_k1, lam_q2, lam_k2], 0).reshape(-1)
    vecs[:, 76:] = np.broadcast_to(lam[None, :], (128, 1024))
    consts = np.zeros((128, 640), f32)
    r = np.arange(128)
    consts[:, 0:128] = np.eye(128, dtype=f32)
    consts[:, 128:256] = np.where(r[:, None] >= r[None, :], NEG, 0.0)
    consts[:, 256:384] = np.where(r[:, None] >= r[None, :], -1.0, 0.0)
    consts[:, 384:512] = -1.0
    consts[:, 512:640] = 1.0
    inv = 1.0 / (10000.0 ** (np.arange(0, 64, 2, dtype=np.float32) / 64))
    ang = np.arange(S, dtype=np.float32)[:, None] * inv[None, :]
    ang = np.concatenate([ang, ang], -1)
    cos = np.cos(ang).T.astype(f32)
    sin = np.sin(ang).T.astype(f32)
    sin_s = np.concatenate([-sin[:32], sin[32:]], 0)
    rope = np.stack([np.concatenate([cos, cos], 0), np.concatenate([sin_s, sin_s], 0)], 0)
    return w_ext, vecs, consts, np.ascontiguousarray(rope.astype(f32))


_NC_CACHE = {}


def kernel(x, w_in, w_o, attn_norm, subln_norm, lam_q1, lam_k1, lam_q2, lam_k2, mlp_norm, w_ff1, w_ff2,
           final_norm):
    x = np.asarray(x, np.float32)
    w_ext, vecs, consts, rope = host_layout(np.asarray(w_in, np.float32), np.asarray(attn_norm, np.float32),
                                            np.asarray(subln_norm, np.float32), np.asarray(lam_q1, np.float32),
                                            np.asarray(lam_k1, np.float32), np.asarray(lam_q2, np.float32),
                                            np.asarray(lam_k2, np.float32), np.asarray(mlp_norm, np.float32),
                                            np.asarray(final_norm, np.float32))
    if "nc" not in _NC_CACHE:
        import os
        _NC_CACHE["nc"] = build(poison=bool(os.environ.get("K_POISON")))
    nc = _NC_CACHE["nc"]
    shared = {"w_in": w_ext, "w_o": np.ascontiguousarray(np.asarray(w_o, np.float32)),
              "w_ff1": np.ascontiguousarray(np.asarray(w_ff1, np.float32)),
              "w_ff2": np.ascontiguousarray(np.asarray(w_ff2, np.float32)),
              "vecs": vecs, "consts": consts, "rope": rope}
    in_maps = [dict(shared, x=np.ascontiguousarray(x[c])) for c in range(8)]
    res = run_bass_kernel_spmd(nc, in_maps, core_ids=list(range(8)))
    return np.stack([np.asarray(r["out"], np.float32) for r in res.results], 0)
```

```python
import math
from contextlib import ExitStack

import numpy as np
import concourse.bass as bass
import concourse.mybir as mybir
from concourse.bass_utils import run_bass_kernel_spmd

F32 = mybir.dt.float32
BF16 = mybir.dt.bfloat16
AF = mybir.ActivationFunctionType
ALU = mybir.AluOpType
AX = mybir.AxisListType

S = 4096
D = 1024
L = 4
NTT = 8
EPS = 1e-6
NEG = -30000.0
ENGS = ("pe", "act", "dve", "pool", "sp")
DMA_INC = 16


class Buf:
    __slots__ = ("name", "writers", "readers", "sem", "cnt", "gen_deps")

    def __init__(self, name=""):
        self.name = name
        self.writers = []
        self.readers = []
        self.sem = None
        self.cnt = 0
        self.gen_deps = []


class Op:
    __slots__ = ("eng", "fn", "deps", "sig", "dma", "count")

    def __init__(self, eng, fn):
        self.eng = eng
        self.fn = fn
        self.deps = []
        self.sig = False
        self.dma = None
        self.count = None


class Prog:
    def __init__(self, nc, n_dma_sems=160):
        self.nc = nc
        self.ops = {e: [] for e in ENGS}
        self.n_dma_sems = n_dma_sems
        self.dma_sem_next = 0
        self.dma_sem_max = 0
        self.dma_last = {}
        self.pending = {e: [] for e in ENGS}

    def op(self, eng, fn, reads=(), writes=()):
        o = Op(eng, fn)
        idx = len(self.ops[eng])
        deps = []
        for b in reads:
            deps.extend(b.writers)
        for b in writes:
            deps.extend(b.writers)
            deps.extend(b.readers)
        deps.extend(self.pending[eng])
        self.pending[eng] = []
        ev = ("e", eng, idx)
        o.deps = [d for d in deps if not (d[0] == "e" and d[1] == "pe" and eng == "pe")]
        self.ops[eng].append(o)
        for b in reads:
            b.readers.append(ev)
        for b in writes:
            b.gen_deps = b.writers + b.readers
            b.writers = [ev]
            b.readers = []
        return ev

    def dma(self, eng, fn, reads=(), writes=(), sem_buf=None, parallel=False):
        o = Op(eng, fn)
        if sem_buf is None:
            sem_buf = writes[0] if writes else reads[0]
        if sem_buf.sem is None:
            sem_buf.sem = self.dma_sem_next
            sem_buf.cnt = self.dma_last.get(sem_buf.sem, 0)
            self.dma_sem_next += 1
            self.dma_sem_max = max(self.dma_sem_max, self.dma_sem_next)
            assert self.dma_sem_next <= self.n_dma_sems, "out of dma sems"
        sem_buf.cnt += DMA_INC
        ev = ("d", sem_buf.sem, sem_buf.cnt)
        self.dma_last[sem_buf.sem] = sem_buf.cnt
        deps = []
        for b in reads:
            deps.extend(b.writers)
        for b in writes:
            if parallel and b.writers and all(w[0] == "d" for w in b.writers) and not b.readers:
                deps.extend(b.gen_deps)
            else:
                deps.extend(b.writers)
                deps.extend(b.readers)
        deps.extend(self.pending[eng])
        self.pending[eng] = []
        o.deps = deps
        o.dma = sem_buf.sem
        self.ops[eng].append(o)
        for b in reads:
            b.readers.append(ev)
        for b in writes:
            if parallel and b.writers and all(w[0] == "d" for w in b.writers) and not b.readers:
                b.writers = b.writers + [ev]
            else:
                b.gen_deps = b.writers + b.readers
                b.writers = [ev]
                b.readers = []
        return ev

    def barrier(self):
        evs = []
        for e in ENGS:
            for i in range(len(self.ops[e]) - 1, -1, -1):
                if self.ops[e][i].dma is None:
                    evs.append(("e", e, i))
                    break
        for s, v in self.dma_last.items():
            evs.append(("d", s, v))
        for e in ENGS:
            self.pending[e].extend(evs)
        self.dma_sem_next = 0

    def emit(self):
        nc = self.nc
        for e in ENGS:
            for o in self.ops[e]:
                for d in o.deps:
                    if d[0] == "e":
                        self.ops[d[1]][d[2]].sig = True
        for e in ENGS:
            c = 0
            for o in self.ops[e]:
                if o.dma is None and o.sig:
                    c += 1
                    o.count = c
        with ExitStack() as st:
            esem = {e: st.enter_context(nc.semaphore("s_" + e)) for e in ENGS}
            dsem = [st.enter_context(nc.semaphore("d%d" % i)) for i in range(self.dma_sem_max)]
            block = st.enter_context(nc.Block())
            handles = {"pe": block.tensor, "act": block.scalar, "dve": block.vector,
                       "pool": block.gpsimd, "sp": block.sync}

            def make(e):
                def body(eng):
                    waited = {}
                    for o in self.ops[e]:
                        need = {}
                        for d in o.deps:
                            if d[0] == "e":
                                key = ("e", d[1])
                                val = self.ops[d[1]][d[2]].count
                            else:
                                key = ("d", d[1])
                                val = d[2]
                            if val > need.get(key, 0):
                                need[key] = val
                        for key, val in need.items():
                            if waited.get(key, 0) >= val:
                                continue
                            waited[key] = val
                            sem = esem[key[1]] if key[0] == "e" else dsem[key[1]]
                            eng.wait_ge(sem, val)
                        ins = o.fn(eng)
                        if o.dma is not None:
                            ins.then_inc(dsem[o.dma], DMA_INC)
                        elif o.sig:
                            ins.then_inc(esem[e], 1)
                    if e == "sp":
                        for s, v in self.dma_last.items():
                            if waited.get(("d", s), 0) < v:
                                eng.wait_ge(dsem[s], v)
                return body

            for e in ENGS:
                handles[e](make(e))


class Tile:
    def __init__(self, t, N, off, n, name=""):
        self.t, self.N, self.off, self.n = t, N, off, n
        self.buf = Buf(name)

    def ap(self, o, dims, p0=0, npart=128):
        return bass.AP(self.t, p0 * self.N + self.off + o, [[self.N, npart]] + [list(d) for d in dims])

    def sl(self, a, b, p0=0, p1=128):
        return self.t[p0:p1, self.off + a:self.off + b]


class Arena:
    def __init__(self, t, N):
        self.t, self.N, self.off = t, N, 0

    def reset(self):
        self.off = 0

    def tile(self, n, name=""):
        o = self.off
        self.off += n
        assert self.off <= self.N, ("arena overflow", name, self.off, self.N)
        return Tile(self.t, self.N, o, n, name)


def lam_init(l):
    return 0.8 - 0.6 * math.exp(-0.3 * l)


def build(n_layers=L, debug=False, poison=False):
    nc = bass.Bass("TRN2", target_bir_lowering=False)
    dt_ = nc.dram_tensor
    x_d = dt_("x", [S, D], F32, kind="ExternalInput")
    win_d = dt_("w_in", [L, D, 4096], F32, kind="ExternalInput")
    wo_d = dt_("w_o", [L, D, D], F32, kind="ExternalInput")
    w1_d = dt_("w_ff1", [L, D, 4096], F32, kind="ExternalInput")
    w2_d = dt_("w_ff2", [L, 4096, D], F32, kind="ExternalInput")
    NV = 76 + 1024
    vec_d = dt_("vecs", [128, NV], F32, kind="ExternalInput")
    cst_d = dt_("consts", [128, 640], F32, kind="ExternalInput")
    rope_d = dt_("rope", [2, 128, S], F32, kind="ExternalInput")
    out_d = dt_("out", [S, D], F32, kind="ExternalOutput")
    sk = "ExternalOutput" if debug else "Internal"
    wbin = dt_("wbin", [L * 8, 128, 4096], BF16)
    wbo = dt_("wbo", [L * 2, 128, 4096], BF16)
    wb1 = dt_("wb1", [L * 8, 128, 4096], BF16)
    wb2 = dt_("wb2", [L * 8, 128, 4096], BF16)
    xT_d = dt_("xT", [8, 128, S], F32, kind=sk)
    hT_d = dt_("hT", [8, 128, S], BF16, kind=sk)
    qTs_d = dt_("qTs", [4, 128, S], BF16, kind=sk)
    kTs_d = dt_("kTs", [4, 128, S], BF16, kind=sk)
    qTd_d = dt_("qTd", [4, 128, S], BF16, kind=sk)
    kTd_d = dt_("kTd", [4, 128, S], BF16, kind=sk)
    vs_d = dt_("vs", [32, 128, 512], BF16, kind=sk)
    vd_d = dt_("vd", [32, 128, 512], BF16, kind=sk)
    mixT_d = dt_("mixT", [8, 128, S], BF16, kind=sk)

    def dap(h, off, dims):
        return bass.AP(h, off, [list(d) for d in dims])

    NBF = 63488
    NF32 = 15360
    with ExitStack() as st:
        abf_t = st.enter_context(nc.sbuf_tensor("abf", [128, NBF], BF16))
        af_t = st.enter_context(nc.sbuf_tensor("af32", [128, NF32], F32))
        cb_t = st.enter_context(nc.sbuf_tensor("cb", [128, 640], BF16))
        cf_t = st.enter_context(nc.sbuf_tensor("cf", [128, 640], F32))
        vec_t = st.enter_context(nc.sbuf_tensor("vec", [128, NV], F32))
        sm_t = st.enter_context(nc.sbuf_tensor("small", [128, 512], F32))
        psb = [st.enter_context(nc.psum_tensor("ps%d" % i, [128, 512], F32)) for i in range(8)]
        PB = [Buf("ps%d" % i) for i in range(8)]
        P = Prog(nc)
        ABF = Arena(abf_t, NBF)
        AF32 = Arena(af_t, NF32)
        CB = Tile(cb_t, 640, 0, 640, "cb")
        CF = Tile(cf_t, 640, 0, 640, "cf")
        VEC = Tile(vec_t, NV, 0, NV, "vec")
        SM = Arena(sm_t, 512)
        sfold = SM.tile(4, "sfold")
        neglam = SM.tile(4, "neglam")
        lamt = SM.tile(8, "lamt")
        epst = SM.tile(1, "epst")

        IDF = CF.sl(0, 128)
        IDB = CB.sl(0, 128)
        NEGM = CB.sl(128, 256)
        LTRI = CB.sl(256, 384)
        NEG1 = CB.sl(384, 512)
        ONES = CB.sl(512, 640)

        def psl(i, a, b, p0=0, p1=128):
            return psb[i][p0:p1, a:b]

        EPSB = epst.sl(0, 1)
        P.op("pool", lambda e: e.memset(EPSB, EPS), writes=[epst.buf])

        def mm(pi, out, lhsT, rhs, start, stop, reads, skip=False):
            if skip:
                P.op("pe", lambda e: e.matmul(out=out, lhsT=lhsT, rhs=rhs, start=start, stop=stop,
                                              skip_group_check=True), reads=reads, writes=[PB[pi]])
            else:
                P.op("pe", lambda e: e.matmul(out=out, lhsT=lhsT, rhs=rhs, start=start, stop=stop),
                     reads=reads, writes=[PB[pi]])

        def tr(pi, out, in_, reads):
            P.op("pe", lambda e: e.transpose(out=out, in_=in_, identity=IDF), reads=list(reads) + [CF.buf],
                 writes=[PB[pi]])

        def act(out, in_, func, reads, writes, scale=1.0, bias=0.0, accum=None):
            if accum is None:
                P.op("act", lambda e: e.activation(out=out, in_=in_, func=func, bias=bias, scale=scale),
                     reads=reads, writes=writes)
            else:
                P.op("act", lambda e: e.activation(out=out, in_=in_, func=func, bias=bias, scale=scale,
                                                   accum_out=accum), reads=reads, writes=writes)

        def tt(eng, out, in0, in1, op, reads, writes):
            P.op(eng, lambda e: e.tensor_tensor(out=out, in0=in0, in1=in1, op=op), reads=reads, writes=writes)

        def ts(eng, out, in0, s1, s2, op0, op1, reads, writes):
            if s2 is None:
                P.op(eng, lambda e: e.tensor_scalar(out=out, in0=in0, scalar1=s1, scalar2=None, op0=op0),
                     reads=reads, writes=writes)
            else:
                P.op(eng, lambda e: e.tensor_scalar(out=out, in0=in0, scalar1=s1, scalar2=s2, op0=op0, op1=op1),
                     reads=reads, writes=writes)

        def stt(eng, out, in0, scalar, in1, op0, op1, reads, writes):
            P.op(eng, lambda e: e.scalar_tensor_tensor(out=out, in0=in0, scalar=scalar, in1=in1, op0=op0, op1=op1),
                 reads=reads, writes=writes)

        def cp(eng, out, in_, reads, writes):
            P.op(eng, lambda e: e.tensor_copy(out=out, in_=in_), reads=reads, writes=writes)

        def ms(eng, ap_, val, reads, writes):
            P.op(eng, lambda e: e.memset(ap_, val), reads=reads, writes=writes)

        def dma(out, in_, reads=(), writes=(), parallel=False):
            P.dma("sp", lambda e: e.dma_start(out=out, in_=in_), reads=list(reads), writes=list(writes),
                  parallel=parallel)

        MUL, ADD, SUB = ALU.mult, ALU.add, ALU.subtract

        dma(cf_t[:, :], cst_d.ap()[:, :], writes=[CF.buf])
        dma(vec_t[:, :], vec_d.ap()[:, :], writes=[VEC.buf])
        cp("dve", cb_t[:, :], cf_t[:, :], [CF.buf], [CB.buf])
        for l in range(L):
            ts("pool", sfold.sl(l, l + 1), VEC.sl(72 + l, 73 + l), 1.0 - lam_init(l), None, MUL, None,
               [VEC.buf], [sfold.buf])
        lprod = AF32.tile(512, "lprod")
        tt("dve", lprod.sl(0, 256), VEC.sl(76, 76 + 256), VEC.sl(76 + 256, 76 + 512), MUL, [VEC.buf], [lprod.buf])
        tt("dve", lprod.sl(256, 512), VEC.sl(76 + 512, 76 + 768), VEC.sl(76 + 768, 76 + 1024), MUL, [VEC.buf],
           [lprod.buf])
        lp_in = lprod.ap(0, [[64, 8], [1, 64]])
        lp_out = lamt.sl(0, 8)
        P.op("dve", lambda e: e.tensor_reduce(out=lp_out, in_=lp_in, axis=AX.X, op=ADD),
             reads=[lprod.buf], writes=[lamt.buf])
        act(lamt.sl(0, 8), lamt.sl(0, 8), AF.Exp, [lamt.buf], [lamt.buf])
        tt("dve", lamt.sl(0, 4), lamt.sl(0, 4), lamt.sl(4, 8), SUB, [lamt.buf], [lamt.buf])
        for l in range(L):
            ts("dve", neglam.sl(l, l + 1), lamt.sl(l, l + 1), lam_init(l), -1.0, ADD, MUL, [lamt.buf], [neglam.buf])

        if poison:
            P.barrier()
            ABF.reset(); AF32.reset()
            pzb = ABF.tile(4096, "pzb")
            pzf = AF32.tile(4096, "pzf")
            ms("pool", pzb.sl(0, 4096), float("nan"), [], [pzb.buf])
            ms("pool", pzf.sl(0, 4096), float("nan"), [], [pzf.buf])
            for hnd, n128 in ((wbin, L * 8), (wbo, L * 2), (wb1, L * 8), (wb2, L * 8), (hT_d, 8), (qTs_d, 4),
                              (kTs_d, 4), (qTd_d, 4), (kTd_d, 4), (mixT_d, 8)):
                for i_ in range(n128):
                    dma(dap(hnd, i_ * 128 * 4096, [[4096, 128], [1, 4096]]), pzb.sl(0, 4096), reads=[pzb.buf])
            for hnd in (vs_d, vd_d):
                for i_ in range(4):
                    dma(dap(hnd, i_ * 128 * 4096, [[4096, 128], [1, 4096]]), pzb.sl(0, 4096), reads=[pzb.buf])
            for i_ in range(8):
                dma(dap(xT_d, i_ * 128 * 4096, [[4096, 128], [1, 4096]]), pzf.sl(0, 4096), reads=[pzf.buf])
        P.barrier()
        ABF.reset(); AF32.reset()
        WS = [AF32.tile(4096, "ws%d" % i) for i in range(2)]
        WB = [ABF.tile(4096, "wb%d" % i) for i in range(2)]
        prep = []
        for l in range(n_layers):
            prep += [("in", l, g) for g in range(8)] + [("o", l, h) for h in range(2)]
            prep += [("f1", l, g) for g in range(8)] + [("f2", l, g) for g in range(8)]
        for i, (kind, l, g) in enumerate(prep):
            ws, wb = WS[i % 2], WB[i % 2]
            ce = "dve" if i % 2 == 0 else "pool"
            if kind in ("in", "f1"):
                src_h = win_d if kind == "in" else w1_d
                dma(ws.ap(0, [[512, 8], [1, 512]]),
                    dap(src_h, l * D * 4096 + g * 512, [[4096, 128], [128 * 4096, 8], [1, 512]]), writes=[ws.buf])
                gcol = (0 if kind == "in" else 32) + l * 8
                tt(ce, wb.ap(0, [[512, 8], [1, 512]]), ws.ap(0, [[512, 8], [1, 512]]),
                   VEC.ap(gcol, [[1, 8], [0, 512]]), MUL, [ws.buf, VEC.buf], [wb.buf])
                dst = dap(wbin if kind == "in" else wb1, (l * 8 + g) * 128 * 4096, [[4096, 128], [1, 4096]])
            elif kind == "f2":
                for q4 in range(4):
                    dma(ws.ap(q4 * 8 * 128, [[128, 8], [1, 128]]),
                        dap(w2_d, l * 4096 * D + q4 * 8 * 128 * D + g * 128, [[D, 128], [128 * D, 8], [1, 128]]),
                        writes=[ws.buf], parallel=(q4 > 0))
                cp(ce, wb.sl(0, 4096), ws.sl(0, 4096), [ws.buf], [wb.buf])
                dst = dap(wb2, (l * 8 + g) * 128 * 4096, [[4096, 128], [1, 4096]])
            else:
                dma(ws.ap(0, [[1024, 4], [1, 1024]]),
                    dap(wo_d, l * D * D + g * 512 * D, [[D, 128], [128 * D, 4], [1, 1024]]), writes=[ws.buf])
                if g == 0:
                    cp(ce, wb.sl(0, 4096), ws.sl(0, 4096), [ws.buf], [wb.buf])
                else:
                    ts(ce, wb.sl(0, 4096), ws.sl(0, 4096), sfold.sl(l, l + 1), None, MUL, None,
                       [ws.buf, sfold.buf], [wb.buf])
                dst = dap(wbo, (l * 2 + g) * 128 * 4096, [[4096, 128], [1, 4096]])
            dma(dst, wb.sl(0, 4096), reads=[wb.buf])

        def norm_stats(XT, SQ, RS, pi):
            act(SQ.sl(0, 4096), XT.sl(0, 4096), AF.Square, [XT.buf], [SQ.buf])
            for c in range(8):
                mm(pi, psl(pi, 0, 512), ONES, SQ.sl(c * 512, (c + 1) * 512), c == 0, c == 7, [SQ.buf, CB.buf])
            act(RS.sl(0, 512), psl(pi, 0, 512), AF.Ln, [PB[pi], epst.buf], [RS.buf], scale=1.0 / D, bias=EPSB)
            act(RS.sl(0, 512), RS.sl(0, 512), AF.Exp, [RS.buf], [RS.buf], scale=-0.5)

        def norm_apply(XT, RS, HT, eng):
            tt(eng, HT.ap(0, [[512, 8], [1, 512]]), XT.ap(0, [[512, 8], [1, 512]]), RS.ap(0, [[0, 8], [1, 512]]),
               MUL, [XT.buf, RS.buf], [HT.buf])

        def store_tile8(T_, dram_h, tt_):
            dma(dap(dram_h, tt_ * 512, [[S, 128], [128 * S, 8], [1, 512]]), T_.ap(0, [[512, 8], [1, 512]]),
                reads=[T_.buf])

        def load_tile8(T_, dram_h, tt_):
            dma(T_.ap(0, [[512, 8], [1, 512]]), dap(dram_h, tt_ * 512, [[S, 128], [128 * S, 8], [1, 512]]),
                writes=[T_.buf])

        P.barrier()
        ABF.reset(); AF32.reset()
        XIN = [AF32.tile(1024, "xin%d" % i) for i in range(2)]
        XTt = [AF32.tile(4096, "xt%d" % i) for i in range(2)]
        RSt = [AF32.tile(512, "rs%d" % i) for i in range(2)]
        SQt = [ABF.tile(4096, "sq%d" % i) for i in range(2)]
        HTt = [ABF.tile(4096, "ht%d" % i) for i in range(2)]
        for tt_ in range(NTT):
            XT = XTt[tt_ % 2]
            for j in range(4):
                tb = tt_ * 4 + j
                xin = XIN[tb % 2]
                dma(xin.sl(0, 1024), x_d.ap()[tb * 128:(tb + 1) * 128, :], writes=[xin.buf])
                for hb in range(2):
                    pi = (tb * 2 + hb) % 4
                    for c4 in range(4):
                        c = hb * 4 + c4
                        tr(pi, psl(pi, c4 * 128, (c4 + 1) * 128), xin.sl(c * 128, (c + 1) * 128), [xin.buf])
                    cp("dve", XT.ap(hb * 4 * 512 + j * 128, [[512, 4], [1, 128]]),
                       bass.AP(psb[pi], 0, [[512, 128], [128, 4], [1, 128]]), [PB[pi]], [XT.buf])
            store_tile8(XT, xT_d, tt_)
            norm_stats(XT, SQt[tt_ % 2], RSt[tt_ % 2], 4 + tt_ % 2)
            norm_apply(XT, RSt[tt_ % 2], HTt[tt_ % 2], "pool")
            store_tile8(HTt[tt_ % 2], hT_d, tt_)

        for l in range(n_layers):
            P.barrier()
            ABF.reset(); AF32.reset()
            HTR = ABF.tile(32768, "htr")
            WG = [ABF.tile(4096, "wg%d" % i) for i in range(3)]
            STG = [ABF.tile(4096, "stg%d" % i) for i in range(2)]
            VSTG = [ABF.tile(2048, "vstg%d" % i) for i in range(2)]
            COS = AF32.tile(4096, "cos")
            SIN = AF32.tile(4096, "sin")
            T1 = [AF32.tile(512, "t1_%d" % i) for i in range(2)]
            T2 = [AF32.tile(512, "t2_%d" % i) for i in range(2)]
            for c in range(8):
                dma(HTR.sl(c * 4096, (c + 1) * 4096), dap(hT_d, c * 128 * S, [[S, 128], [1, S]]),
                    writes=[HTR.buf], parallel=(c > 0))
            dma(COS.sl(0, 4096), dap(rope_d, 0, [[S, 128], [1, S]]), writes=[COS.buf])
            dma(SIN.sl(0, 4096), dap(rope_d, 128 * S, [[S, 128], [1, S]]), writes=[SIN.buf])
            wg_i = [0]

            def load_wg(g):
                w = WG[wg_i[0] % 3]
                wg_i[0] += 1
                dma(w.sl(0, 4096), dap(wbin, (l * 8 + g) * 128 * 4096, [[4096, 128], [1, 4096]]), writes=[w.buf])
                return w

            psr = [0]
            stg_i = [0]
            for g, dst_h, sc in ((0, qTs_d, 0.125), (1, kTs_d, 1.0)):
                w = load_wg(g)
                for ci in range(4):
                    stg = STG[stg_i[0] % 2]
                    stg_i[0] += 1
                    for tt_ in range(NTT):
                        pi = psr[0] % 4
                        psr[0] += 1
                        for dmc in range(8):
                            mm(pi, psl(pi, 0, 512), w.sl(dmc * 512 + ci * 128, dmc * 512 + (ci + 1) * 128),
                               HTR.sl(dmc * 4096 + tt_ * 512, dmc * 4096 + (tt_ + 1) * 512), dmc == 0, dmc == 7,
                               [w.buf, HTR.buf])
                        act(stg.sl(tt_ * 512, (tt_ + 1) * 512), psl(pi, 0, 512), AF.Copy, [PB[pi]], [stg.buf],
                            scale=sc)
                    dma(dap(dst_h, ci * 128 * S, [[S, 128], [1, S]]), stg.sl(0, 4096), reads=[stg.buf])
            for g, dst_h, sc in ((2, qTd_d, 0.125), (4, kTd_d, 1.0)):
                wa = load_wg(g)
                wp = load_wg(g + 1)
                for ci in range(4):
                    stg = STG[stg_i[0] % 2]
                    stg_i[0] += 1
                    for tt_ in range(NTT):
                        pa = 4 + psr[0] % 2
                        pb_ = 6 + psr[0] % 2
                        t1 = T1[psr[0] % 2]
                        t2 = T2[psr[0] % 2]
                        psr[0] += 1
                        for pi, w in ((pa, wa), (pb_, wp)):
                            for dmc in range(8):
                                mm(pi, psl(pi, 0, 512), w.sl(dmc * 512 + ci * 128, dmc * 512 + (ci + 1) * 128),
                                   HTR.sl(dmc * 4096 + tt_ * 512, dmc * 4096 + (tt_ + 1) * 512), dmc == 0,
                                   dmc == 7, [w.buf, HTR.buf])
                        stt("dve", t1.sl(0, 512), psl(pa, 0, 512), sc, COS.sl(tt_ * 512, (tt_ + 1) * 512), MUL, MUL,
                            [PB[pa], COS.buf], [t1.buf])
                        stt("dve", t2.sl(0, 512), psl(pb_, 0, 512), sc, SIN.sl(tt_ * 512, (tt_ + 1) * 512), MUL, MUL,
                            [PB[pb_], SIN.buf], [t2.buf])
                        tt("pool", stg.sl(tt_ * 512, (tt_ + 1) * 512), t1.sl(0, 512), t2.sl(0, 512), ADD,
                           [t1.buf, t2.buf], [stg.buf])
                    dma(dap(dst_h, ci * 128 * S, [[S, 128], [1, S]]), stg.sl(0, 4096), reads=[stg.buf])
            for g, dst_h in ((6, vs_d), (7, vd_d)):
                w = load_wg(g)
                for tb in range(32):
                    vst = VSTG[(tb // 4) % 2]
                    pi = psr[0] % 4
                    psr[0] += 1
                    for dmc in range(8):
                        mm(pi, psl(pi, 0, 512), HTR.sl(dmc * 4096 + tb * 128, dmc * 4096 + (tb + 1) * 128),
                           w.sl(dmc * 512, (dmc + 1) * 512), dmc == 0, dmc == 7, [w.buf, HTR.buf])
                    act(vst.sl((tb % 4) * 512, (tb % 4 + 1) * 512), psl(pi, 0, 512), AF.Copy, [PB[pi]], [vst.buf])
                    if tb % 4 == 3:
                        dma(dap(dst_h, (tb - 3) * 128 * 512, [[512, 128], [128 * 512, 4], [1, 512]]),
                            vst.ap(0, [[512, 4], [1, 512]]), reads=[vst.buf])
            if debug == "A":
                break

            P.barrier()
            ABF.reset(); AF32.reset()
            QTp = [[ABF.tile(4096, "qt%d_%d" % (i, j)) for j in range(2)] for i in range(2)]
            KT = [ABF.tile(4096, "kt%d" % i) for i in range(2)]
            VP = [ABF.tile(32 * 132, "vp%d" % i) for i in range(2)]
            SPt = [ABF.tile(512, "sp%d" % i) for i in range(3)]
            Wt = [ABF.tile(512, "w%d" % i) for i in range(3)]
            ACC = [ABF.tile(512, "acc%d" % i) for i in range(2)]
            OST = [ABF.tile(4096, "ost%d" % i) for i in range(2)]
            Et = [AF32.tile(512, "e%d" % i) for i in range(3)]
            DJ = [AF32.tile(512, "dj%d" % i) for i in range(2)]
            DEN = [AF32.tile(16, "den%d" % i) for i in range(2)]
            SSQ = [AF32.tile(8, "ssq%d" % i) for i in range(2)]
            JUNK = AF32.tile(128, "junk")
            for i in range(2):
                ms("pool", VP[i].ap(128, [[132, 32], [1, 1]]), 1.0, [], [VP[i].buf])
                ms("pool", QTp[i][0].sl(0, 4096, 64, 128), 0.0, [], [QTp[i][0].buf])
                ms("pool", QTp[i][1].sl(0, 4096, 0, 64), 0.0, [], [QTp[i][1].buf])

            def load_qkv(qh, kh, vh, c, slot):
                dma(QTp[slot][0].sl(0, 4096, 0, 64), dap(qh, c * 128 * S, [[S, 64], [1, S]]),
                    writes=[QTp[slot][0].buf])
                dma(QTp[slot][1].sl(0, 4096, 64, 128), dap(qh, c * 128 * S + 64 * S, [[S, 64], [1, S]]),
                    writes=[QTp[slot][1].buf])
                dma(KT[slot].sl(0, 4096), dap(kh, c * 128 * S, [[S, 128], [1, S]]), writes=[KT[slot].buf])
                for q8 in range(8):
                    dma(VP[slot].ap(q8 * 4 * 132, [[132, 4], [1, 128]]),
                        dap(vh, q8 * 4 * 128 * 512 + c * 128, [[512, 128], [128 * 512, 4], [1, 128]]),
                        writes=[VP[slot].buf], parallel=(q8 > 0))

            blocks = []
            for pc in range(4):
                for hh in range(2):
                    for tq in range(NTT):
                        for i, kb in enumerate(range(4 * tq + 3, -1, -1)):
                            blocks.append((pc, pc % 2, hh, tq, i, kb))
            nb = len(blocks) if debug != "Bdf" else 0
            load_qkv(qTs_d, kTs_d, vs_d, 0, 0)
            ost_i = [0]
            cur_ost = {}

            def sb_stage1(n):
                pc, slot, hh, tq, i, kb = blocks[n]
                if hh == 0 and tq == 0 and i == 3 and pc + 1 < 4:
                    load_qkv(qTs_d, kTs_d, vs_d, pc + 1, (pc + 1) % 2)
                c0 = 128 * max(0, kb - 4 * tq)
                diag = kb >= 4 * tq
                q0 = tq * 512
                ai = n % 3
                acc = ACC[tq % 2]
                Q = QTp[slot][hh]
                if i == 0:
                    ms("pool", acc.sl(0, 512), 0.0, [], [acc.buf])
                mm(ai, psl(ai, c0, 512), KT[slot].sl(kb * 128, (kb + 1) * 128), Q.sl(q0 + c0, q0 + 512),
                   True, False, [KT[slot].buf, Q.buf], skip=True)
                if diag:
                    mm(ai, psl(ai, c0, c0 + 128), IDB, NEGM, False, False, [CB.buf], skip=True)
                E = Et[n % 3]
                SPn = SPt[n % 3]
                act(E.sl(c0, 512), psl(ai, c0, 512), AF.Exp, [PB[ai]], [E.buf])
                act(SPn.sl(c0, 512), E.sl(c0, 512), AF.Ln, [E.buf], [SPn.buf], bias=1.0)

            def sb_stage2(n):
                pc, slot, hh, tq, i, kb = blocks[n]
                c0 = 128 * max(0, kb - 4 * tq)
                ai = n % 3
                acc = ACC[tq % 2]
                SPn = SPt[n % 3]
                Wn = Wt[n % 3]
                mm(ai, psl(ai, c0, 512), LTRI, SPn.sl(c0, 512), False, i == 0, [CB.buf, SPn.buf], skip=True)
                if i > 0:
                    mm(ai, psl(ai, c0, 512), NEG1, acc.sl(c0, 512), False, True, [CB.buf, acc.buf], skip=True)
                act(Wn.sl(c0, 512), psl(ai, c0, 512), AF.Exp, [PB[ai]], [Wn.buf])
                if kb > 0:
                    tt("pool", acc.sl(c0, 512), acc.sl(c0, 512), SPn.sl(c0, 512), ADD, [SPn.buf, acc.buf], [acc.buf])

            def sb_stage3(n):
                pc, slot, hh, tq, i, kb = blocks[n]
                pb = 64 * hh
                c0 = 128 * max(0, kb - 4 * tq)
                oi = 6 + tq % 2
                Wn = Wt[n % 3]
                mm(oi, psl(oi, c0, 512), VP[slot].ap(kb * 132, [[1, 128]]), Wn.sl(c0, 512),
                   i == 0, kb == 0, [VP[slot].buf, Wn.buf], skip=True)
                if kb == 0:
                    if tq == 0 and hh == 0:
                        cur_ost[pc] = OST[ost_i[0] % 2]
                        ost_i[0] += 1
                    ost = cur_ost[pc]
                    cp("dve", ost.sl(tq * 512, (tq + 1) * 512, pb, pb + 64), psl(oi, 0, 512, pb, pb + 64),
                       [PB[oi]], [ost.buf])
                    if tq == NTT - 1 and hh == 1:
                        dma(dap(mixT_d, pc * 128 * S, [[S, 128], [1, S]]), ost.sl(0, 4096), reads=[ost.buf])

            for step in range(nb + 2 if nb else 0):
                if step < nb:
                    sb_stage1(step)
                if 0 <= step - 1 < nb:
                    sb_stage2(step - 1)
                if 0 <= step - 2 < nb:
                    sb_stage3(step - 2)

            dblocks = []
            for h in range(4):
                for tq in range(NTT):
                    for kb in range(4 * tq + 4):
                        dblocks.append((h, h % 2, tq, kb))
            ndb = len(dblocks) if debug != "Bsb" else 0
            if ndb:
                load_qkv(qTd_d, kTd_d, vd_d, 0, 0)
            PMt = [ABF.tile(512, "pm%d" % i) for i in range(4)]
            srot = [0]
            grp = [0]
            dinfo = {}
            dstate = {}

            def df_stage1(n):
                h, slot, tq, kb = dblocks[n]
                if tq == 0 and kb == 3 and h + 1 < 4:
                    load_qkv(qTd_d, kTd_d, vd_d, h + 1, (h + 1) % 2)
                c0 = 128 * max(0, kb - 4 * tq)
                q0 = tq * 512
                pms = []
                for m in range(2):
                    si = m
                    Pm = PMt[(2 * n + m) % 4]
                    pms.append(Pm)
                    Q = QTp[slot][m]
                    mm(si, psl(si, c0, 512), KT[slot].sl(kb * 128, (kb + 1) * 128), Q.sl(q0 + c0, q0 + 512), True, True,
                       [KT[slot].buf, Q.buf])
                    act(Pm.sl(c0, 512), psl(si, c0, 512), AF.Exp, [PB[si]], [Pm.buf])
                    if kb >= 4 * tq:
                        ms("pool", Pm.sl(c0, c0 + 64, 64, 128), 0.0, [Pm.buf], [Pm.buf])
                dinfo[n] = pms

            def df_stage2(n):
                h, slot, tq, kb = dblocks[n]
                c0 = 128 * max(0, kb - 4 * tq)
                pms = dinfo.pop(n)
                if kb == 0:
                    ob = 2 + 3 * (grp[0] % 2)
                    grp[0] += 1
                    dstate["ob"] = ob
                    if tq == 0:
                        dstate["dst"] = OST[ost_i[0] % 2]
                        ost_i[0] += 1
                    for b3 in range(3):
                        ms("dve", psl(ob + b3, 0, 512), 0.0, [], [PB[ob + b3]])
                ob = dstate["ob"]
                dst = dstate["dst"]
                for m in range(2):
                    Pm = pms[m]
                    for j in range(c0 // 128, 4):
                        a = m * 4 + j
                        bi = ob + a // 3
                        co = (a % 3) * 132
                        mm(bi, psl(bi, co, co + 129), Pm.sl(j * 128, (j + 1) * 128),
                           VP[slot].ap(kb * 132, [[1, 129]]), False, False, [Pm.buf, VP[slot].buf], skip=True)
                if kb != 4 * tq + 3:
                    return
                den = DEN[tq % 2]
                ssq = SSQ[tq % 2]
                dj = DJ[tq % 2]
                for a in range(8):
                    bi = ob + a // 3
                    co = (a % 3) * 132
                    cp("dve", den.sl(a, a + 1), psl(bi, co + 128, co + 129), [PB[bi]], [den.buf])
                d8 = den.sl(0, 8)
                P.op("dve", lambda e: e.reciprocal(out=d8, in_=d8), reads=[den.buf], writes=[den.buf])
                ts("dve", den.sl(4, 8), den.sl(4, 8), neglam.sl(l, l + 1), None, MUL, None,
                   [den.buf, neglam.buf], [den.buf])
                ms("pool", ssq.sl(0, 4), 0.0, [], [ssq.buf])
                for j in range(4):
                    a0, a1 = j, 4 + j
                    b0, c0_ = ob + a0 // 3, (a0 % 3) * 132
                    b1, c1_ = ob + a1 // 3, (a1 % 3) * 132
                    djj = dj.sl(j * 128, (j + 1) * 128)
                    ts("dve", djj, psl(b0, c0_, c0_ + 128), den.sl(j, j + 1), None, MUL, None,
                       [PB[b0], den.buf], [dj.buf])
                    stt("dve", djj, psl(b1, c1_, c1_ + 128), den.sl(4 + j, 5 + j), djj, MUL, ADD,
                        [PB[b1], den.buf, dj.buf], [dj.buf])
                    act(JUNK.sl(0, 128), djj, AF.Square, [dj.buf], [JUNK.buf, ssq.buf], accum=ssq.sl(j, j + 1))
                act(ssq.sl(0, 4), ssq.sl(0, 4), AF.Ln, [ssq.buf, epst.buf], [ssq.buf], scale=1.0 / 128, bias=EPSB)
                act(ssq.sl(0, 4), ssq.sl(0, 4), AF.Exp, [ssq.buf], [ssq.buf], scale=-0.5)
                for j in range(4):
                    djj = dj.sl(j * 128, (j + 1) * 128)
                    ts("dve", djj, djj, ssq.sl(j, j + 1), None, MUL, None, [dj.buf, ssq.buf], [dj.buf])
                si = srot[0] % 2
                srot[0] += 1
                for j in range(4):
                    tr(si, psl(si, j * 128, (j + 1) * 128), dj.sl(j * 128, (j + 1) * 128), [dj.buf])
                cp("dve", dst.sl(tq * 512, (tq + 1) * 512), psl(si, 0, 512), [PB[si]], [dst.buf])
                if tq == NTT - 1:
                    dma(dap(mixT_d, (4 + h) * 128 * S, [[S, 128], [1, S]]), dst.sl(0, 4096), reads=[dst.buf])

            for step in range(ndb + 1 if ndb else 0):
                if step < ndb:
                    df_stage1(step)
                if 0 <= step - 1 < ndb:
                    df_stage2(step - 1)
            if debug and debug.startswith("B"):
                break

            P.barrier()
            ABF.reset(); AF32.reset()
            WO = ABF.tile(8192, "wo")
            MT = [ABF.tile(4096, "mt%d" % i) for i in range(2)]
            HT2 = ABF.tile(4096, "ht2")
            W1G = [ABF.tile(4096, "w1g%d" % i) for i in range(2)]
            W2G = [ABF.tile(4096, "w2g%d" % i) for i in range(2)]
            Rt = [ABF.tile(512, "r%d" % i) for i in range(2)]
            UT = ABF.tile(16384, "ut")
            SQ = ABF.tile(4096, "sq")
            HTn = ABF.tile(4096, "htn")
            XTc = [AF32.tile(4096, "xtc%d" % i) for i in range(2)]
            RS = AF32.tile(512, "rs")
            YT = AF32.tile(4096, "yt")
            OUTS = [AF32.tile(1024, "outs%d" % i) for i in range(2)]
            last = (l == n_layers - 1)
            for hf in range(2):
                dma(WO.sl(hf * 4096, (hf + 1) * 4096),
                    dap(wbo, (l * 2 + hf) * 128 * 4096, [[4096, 128], [1, 4096]]), writes=[WO.buf], parallel=(hf > 0))
            load_tile8(MT[0], mixT_d, 0)
            load_tile8(XTc[0], xT_d, 0)
            prot = [0]
            w1i = [0]
            w2i = [0]

            def load_w1(fg):
                w = W1G[w1i[0] % 2]
                w1i[0] += 1
                dma(w.sl(0, 4096), dap(wb1, (l * 8 + fg) * 128 * 4096, [[4096, 128], [1, 4096]]), writes=[w.buf])
                return w

            def load_w2(dc):
                w = W2G[w2i[0] % 2]
                w2i[0] += 1
                dma(w.sl(0, 4096), dap(wb2, (l * 8 + dc) * 128 * 4096, [[4096, 128], [1, 4096]]), writes=[w.buf])
                return w

            for tt_ in range(NTT):
                XT = XTc[tt_ % 2]
                MTt = MT[tt_ % 2]
                if tt_ + 1 < NTT:
                    load_tile8(MT[(tt_ + 1) % 2], mixT_d, tt_ + 1)
                    load_tile8(XTc[(tt_ + 1) % 2], xT_d, tt_ + 1)
                w1n = load_w1(0)
                for dc in range(8):
                    pi = prot[0] % 4
                    prot[0] += 1
                    for ec in range(8):
                        mm(pi, psl(pi, 0, 512), WO.sl(ec * 1024 + dc * 128, ec * 1024 + (dc + 1) * 128),
                           MTt.sl(ec * 512, (ec + 1) * 512), ec == 0, ec == 7, [WO.buf, MTt.buf])
                    tt("dve", XT.sl(dc * 512, (dc + 1) * 512), psl(pi, 0, 512), XT.sl(dc * 512, (dc + 1) * 512), ADD,
                       [PB[pi], XT.buf], [XT.buf])
                norm_stats(XT, SQ, RS, 4)
                norm_apply(XT, RS, HT2, "pool")
                for fg in range(8):
                    w1 = w1n
                    if fg + 1 < 8:
                        w1n = load_w1(fg + 1)
                    else:
                        w2n = load_w2(0)
                    for fi in range(4):
                        fc = fg * 4 + fi
                        pi = prot[0] % 4
                        prot[0] += 1
                        for dmc in range(8):
                            mm(pi, psl(pi, 0, 512), w1.sl(dmc * 512 + fi * 128, dmc * 512 + (fi + 1) * 128),
                               HT2.sl(dmc * 512, (dmc + 1) * 512), dmc == 0, dmc == 7, [w1.buf, HT2.buf])
                        R = Rt[fc % 2]
                        act(R.sl(0, 512), psl(pi, 0, 512), AF.Relu, [PB[pi]], [R.buf])
                        tt("pool", UT.sl(fc * 512, (fc + 1) * 512), R.sl(0, 512), R.sl(0, 512), MUL, [R.buf], [UT.buf])
                for dc in range(8):
                    w2 = w2n
                    if dc + 1 < 8:
                        w2n = load_w2(dc + 1)
                    pi = prot[0] % 4
                    prot[0] += 1
                    for fc in range(32):
                        mm(pi, psl(pi, 0, 512), w2.sl(fc * 128, (fc + 1) * 128), UT.sl(fc * 512, (fc + 1) * 512),
                           fc == 0, fc == 31, [w2.buf, UT.buf])
                    tt("dve", XT.sl(dc * 512, (dc + 1) * 512), psl(pi, 0, 512), XT.sl(dc * 512, (dc + 1) * 512), ADD,
                       [PB[pi], XT.buf], [XT.buf])
                norm_stats(XT, SQ, RS, 4)
                if not last:
                    store_tile8(XT, xT_d, tt_)
                    norm_apply(XT, RS, HTn, "pool")
                    store_tile8(HTn, hT_d, tt_)
                else:
                    if debug:
                        store_tile8(XT, xT_d, tt_)
                    for c in range(8):
                        stt("dve", YT.sl(c * 512, (c + 1) * 512), XT.sl(c * 512, (c + 1) * 512), VEC.sl(64 + c, 65 + c),
                            RS.sl(0, 512), MUL, MUL, [XT.buf, VEC.buf, RS.buf], [YT.buf])
                    for j in range(4):
                        tb = tt_ * 4 + j
                        outs = OUTS[tb % 2]
                        for hb in range(2):
                            pi = 5 + (tb * 2 + hb) % 3
                            for c4 in range(4):
                                c = hb * 4 + c4
                                tr(pi, psl(pi, c4 * 128, (c4 + 1) * 128),
                                   YT.sl(c * 512 + j * 128, c * 512 + (j + 1) * 128), [YT.buf])
                            cp("dve", outs.sl(hb * 512, (hb + 1) * 512), psl(pi, 0, 512), [PB[pi]], [outs.buf])
                        dma(out_d.ap()[tb * 128:(tb + 1) * 128, :], outs.sl(0, 1024), reads=[outs.buf])
            if debug == "C":
                break
        P.emit()
        nc._prog_stats = {e: len(P.ops[e]) for e in ENGS}
    return nc


def host_layout(w_in, attn_norm, subln_norm, lam_q1, lam_k1, lam_q2, lam_k2, mlp_norm, final_norm):
    f32 = np.float32
    perm = np.concatenate([(np.arange(64) + 32) % 64 + 64 * g for g in range(8)])
    sbq, sbk, sbv = w_in[:, :, 0:512], w_in[:, :, 512:1024], w_in[:, :, 1024:1536]
    dfq, dfk, dfv = w_in[:, :, 1536:2048], w_in[:, :, 2048:2560], w_in[:, :, 2560:3072]
    w_ext = np.ascontiguousarray(np.concatenate(
        [sbq, sbk, dfq, dfq[:, :, perm], dfk, dfk[:, :, perm], sbv, dfv], axis=2).astype(f32))
    vecs = np.zeros((128, 76 + 1024), f32)
    vecs[:, 0:32] = attn_norm.reshape(L, 8, 128).transpose(2, 0, 1).reshape(128, 32)
    vecs[:, 32:64] = mlp_norm.reshape(L, 8, 128).transpose(2, 0, 1).reshape(128, 32)
    vecs[:, 64:72] = final_norm.reshape(8, 128).T
    vecs[:, 72:76] = subln_norm.T
    lam = np.stack([lam_q1, lam_k1, lam_q2, lam_k2], 0).reshape(-1)
    vecs[:, 76:] = np.broadcast_to(lam[None, :], (128, 1024))
    consts = np.zeros((128, 640), f32)
    r = np.arange(128)
    consts[:, 0:128] = np.eye(128, dtype=f32)
    consts[:, 128:256] = np.where(r[:, None] >= r[None, :], NEG, 0.0)
    consts[:, 256:384] = np.where(r[:, None] >= r[None, :], -1.0, 0.0)
    consts[:, 384:512] = -1.0
    consts[:, 512:640] = 1.0
    inv = 1.0 / (10000.0 ** (np.arange(0, 64, 2, dtype=np.float32) / 64))
    ang = np.arange(S, dtype=np.float32)[:, None] * inv[None, :]
    ang = np.concatenate([ang, ang], -1)
    cos = np.cos(ang).T.astype(f32)
    sin = np.sin(ang).T.astype(f32)
    sin_s = np.concatenate([-sin[:32], sin[32:]], 0)
    rope = np.stack([np.concatenate([cos, cos], 0), np.concatenate([sin_s, sin_s], 0)], 0)
    return w_ext, vecs, consts, np.ascontiguousarray(rope.astype(f32))


_NC_CACHE = {}


def kernel(x, w_in, w_o, attn_norm, subln_norm, lam_q1, lam_k1, lam_q2, lam_k2, mlp_norm, w_ff1, w_ff2,
           final_norm):
    x = np.asarray(x, np.float32)
    w_ext, vecs, consts, rope = host_layout(np.asarray(w_in, np.float32), np.asarray(attn_norm, np.float32),
                                            np.asarray(subln_norm, np.float32), np.asarray(lam_q1, np.float32),
                                            np.asarray(lam_k1, np.float32), np.asarray(lam_q2, np.float32),
                                            np.asarray(lam_k2, np.float32), np.asarray(mlp_norm, np.float32),
                                            np.asarray(final_norm, np.float32))
    if "nc" not in _NC_CACHE:
        import os
        _NC_CACHE["nc"] = build(poison=bool(os.environ.get("K_POISON")))
    nc = _NC_CACHE["nc"]
    shared = {"w_in": w_ext, "w_o": np.ascontiguousarray(np.asarray(w_o, np.float32)),
              "w_ff1": np.ascontiguousarray(np.asarray(w_ff1, np.float32)),
              "w_ff2": np.ascontiguousarray(np.asarray(w_ff2, np.float32)),
              "vecs": vecs, "consts": consts, "rope": rope}
    in_maps = [dict(shared, x=np.ascontiguousarray(x[c])) for c in range(8)]
    res = run_bass_kernel_spmd(nc, in_maps, core_ids=list(range(8)))
    return np.stack([np.asarray(r["out"], np.float32) for r in res.results], 0)
```
